# Optimizing a Trainium2 kernel written in Bass

```python
import math
import jax, jax.numpy as jnp
from jax import lax
import numpy as np

D_MODEL = 2048
BATCH = 2
SEQ = 8192
DEPTH = 1

D_MIX = D_MODEL
D_SSM = D_MIX // 2
D_ATTN = D_MIX - D_SSM
HEAD_DIM = 64
N_HEADS = D_ATTN // HEAD_DIM
SSM_GROUP = 16
N_SSM_GROUPS = D_SSM // SSM_GROUP
SSM_STATE = 64
DILATIONS = ((128, 1), (512, 4), (2048, 16))
D_FF = ((8 * D_MODEL // 3 + 127) // 128) * 128
DT_MIN = 0.001
DT_MAX = 0.1
EPS = 1e-6
NEG_INF = -1e30

kernel_name = "hymba_s5_longnet_macaron_block"


def rmsnorm(x, g):
    x32 = x.astype(jnp.float32)
    y = x32 * lax.rsqrt(jnp.mean(x32 * x32, axis=-1, keepdims=True) + EPS) * g.astype(jnp.float32)
    return y.astype(x.dtype)


def swiglu(x, w_gate, w_up, w_down):
    return (jax.nn.silu(x @ w_gate) * (x @ w_up)) @ w_down


def _ssm_combine(left, right):
    ar_i, ai_i, br_i, bi_i = left
    ar_j, ai_j, br_j, bi_j = right
    ar = ar_j * ar_i - ai_j * ai_i
    ai = ar_j * ai_i + ai_j * ar_i
    br = ar_j * br_i - ai_j * bi_i + br_j
    bi = ar_j * bi_i + ai_j * br_i + bi_j
    return (ar, ai, br, bi)


def s5_mixer(u, log_dt, a_re, a_im, b_re, b_im, c_re, c_im, d, w_glu, b_glu):
    Bsz, S, _ = u.shape
    f32 = jnp.float32
    u32 = u.astype(f32).reshape(Bsz, S, N_SSM_GROUPS, SSM_GROUP)
    lr = a_re.astype(f32)
    li = a_im.astype(f32)
    dt = jnp.exp(log_dt.astype(f32))[:, None]
    mag = jnp.exp(lr * dt)
    lb_re = mag * jnp.cos(li * dt)
    lb_im = mag * jnp.sin(li * dt)
    den = lr * lr + li * li
    nr = lb_re - 1.0
    ni = lb_im
    f_re = (nr * lr + ni * li) / den
    f_im = (ni * lr - nr * li) / den
    br = b_re.astype(f32)
    bi = b_im.astype(f32)
    bb_re = f_re[..., None] * br - f_im[..., None] * bi
    bb_im = f_re[..., None] * bi + f_im[..., None] * br
    bu_re = jnp.einsum('bsgh,gph->sbgp', u32, bb_re)
    bu_im = jnp.einsum('bsgh,gph->sbgp', u32, bb_im)
    a_re_t = jnp.broadcast_to(lb_re[None, None], (S, 1, N_SSM_GROUPS, SSM_STATE))
    a_im_t = jnp.broadcast_to(lb_im[None, None], (S, 1, N_SSM_GROUPS, SSM_STATE))
    _, _, s_re, s_im = lax.associative_scan(_ssm_combine, (a_re_t, a_im_t, bu_re, bu_im), axis=0)
    y = (jnp.einsum('sbgp,ghp->bsgh', s_re, c_re.astype(f32))
         - jnp.einsum('sbgp,ghp->bsgh', s_im, c_im.astype(f32))
         + d.astype(f32) * u32)
    y = jax.nn.gelu(y).reshape(Bsz, S, D_SSM)
    return y * jax.nn.sigmoid(y @ w_glu.astype(f32) + b_glu.astype(f32))


def dilated_branch(q, k, v, window, dilation):
    Bsz, S, H, E = q.shape
    nk = window // dilation
    blk = nk
    L = S // dilation
    nb = -(-L // blk)
    Lp = nb * blk

    def gather(t):
        t = t.reshape(Bsz, L, dilation, H, E).transpose(0, 2, 1, 3, 4)
        t = jnp.pad(t, ((0, 0), (0, 0), (0, Lp - L), (0, 0), (0, 0)))
        return t.reshape(Bsz, dilation, nb, blk, H, E)

    qb, kb, vb = gather(q), gather(k), gather(v)

    def with_prev(t):
        prev = jnp.pad(t, ((0, 0), (0, 0), (1, 0), (0, 0), (0, 0), (0, 0)))[:, :, :-1]
        return jnp.concatenate([prev, t], axis=3)

    kk, vv = with_prev(kb), with_prev(vb)
    s = jnp.einsum('bgnqhe,bgnkhe->bgnhqk', qb, kk) * (1.0 / math.sqrt(E))
    qi = jnp.arange(blk)[:, None]
    kj = jnp.arange(2 * blk)[None, :]
    dist = blk + qi - kj
    kpos = (jnp.arange(nb)[:, None, None] - 1) * blk + kj[None]
    valid = (dist >= 0)[None] & (dist <= nk)[None] & (kpos >= 0)
    s = jnp.where(valid[None, None, :, None], s, NEG_INF)
    m = jnp.max(s, axis=-1, keepdims=True)
    p = jnp.exp(s - m)
    den = jnp.sum(p, axis=-1, keepdims=True)
    o = jnp.einsum('bgnhqk,bgnkhe->bgnqhe', p / den, vv)
    lse = (m + jnp.log(den))[..., 0].transpose(0, 1, 2, 4, 3)
    o = o.reshape(Bsz, dilation, Lp, H, E)[:, :, :L].transpose(0, 2, 1, 3, 4).reshape(Bsz, S, H, E)
    lse = lse.reshape(Bsz, dilation, Lp, H)[:, :, :L].transpose(0, 2, 1, 3).reshape(Bsz, S, H)
    return o, lse


def dilated_attention(q, k, v):
    Bsz, S, _ = q.shape
    f32 = jnp.float32
    q = q.astype(f32).reshape(Bsz, S, N_HEADS, HEAD_DIM)
    k = k.astype(f32).reshape(Bsz, S, N_HEADS, HEAD_DIM)
    v = v.astype(f32).reshape(Bsz, S, N_HEADS, HEAD_DIM)
    outs, lses = [], []
    for window, dilation in DILATIONS:
        o, lse = dilated_branch(q, k, v, window, dilation)
        outs.append(o)
        lses.append(lse)
    w = jax.nn.softmax(jnp.stack(lses, axis=0), axis=0)
    o = jnp.sum(w[..., None] * jnp.stack(outs, axis=0), axis=0)
    return o.reshape(Bsz, S, D_ATTN)


def setup_inputs(seed: int = 0) -> dict:
    key = jax.random.key(seed)
    ks = jax.random.split(key, 32)
    f32 = jnp.float32
    G, P, Hg = N_SSM_GROUPS, SSM_STATE, SSM_GROUP

    def nrm(k, shape, scale):
        return jax.random.normal(k, shape, f32) * scale

    def gain(k, shape):
        return 1.0 + 0.01 * jax.random.normal(k, shape, f32)

    a_im = jnp.broadcast_to(math.pi * jnp.arange(P, dtype=f32), (DEPTH, G, P))
    a_im = a_im + 0.01 * jax.random.normal(ks[9], (DEPTH, G, P), f32)
    return {
        "x": jax.random.normal(ks[0], (BATCH, SEQ, D_MODEL), f32),
        "ffn1_norm": gain(ks[1], (DEPTH, D_MODEL)),
        "ffn1_w_gate": nrm(ks[2], (DEPTH, D_MODEL, D_FF), D_MODEL ** -0.5),
        "ffn1_w_up": nrm(ks[3], (DEPTH, D_MODEL, D_FF), D_MODEL ** -0.5),
        "ffn1_w_down": nrm(ks[4], (DEPTH, D_FF, D_MODEL), D_FF ** -0.5),
        "mix_norm": gain(ks[5], (DEPTH, D_MODEL)),
        "w_in": nrm(ks[6], (DEPTH, D_MODEL, 3 * D_ATTN + D_SSM), D_MODEL ** -0.5),
        "ssm_log_dt": jax.random.uniform(ks[7], (DEPTH, G), f32, math.log(DT_MIN), math.log(DT_MAX)),
        "ssm_a_re": -0.5 + 0.01 * jax.random.normal(ks[8], (DEPTH, G, P), f32),
        "ssm_a_im": a_im,
        "ssm_b_re": nrm(ks[10], (DEPTH, G, P, Hg), (2 * Hg) ** -0.5),
        "ssm_b_im": nrm(ks[11], (DEPTH, G, P, Hg), (2 * Hg) ** -0.5),
        "ssm_c_re": nrm(ks[12], (DEPTH, G, Hg, P), (2 * P) ** -0.5),
        "ssm_c_im": nrm(ks[13], (DEPTH, G, Hg, P), (2 * P) ** -0.5),
        "ssm_d": nrm(ks[14], (DEPTH, G, Hg), 1.0),
        "ssm_w_glu": nrm(ks[15], (DEPTH, D_SSM, D_SSM), D_SSM ** -0.5),
        "ssm_b_glu": nrm(ks[16], (DEPTH, D_SSM), 0.01),
        "ssm_out_norm": gain(ks[17], (DEPTH, D_SSM)),
        "attn_out_norm": gain(ks[18], (DEPTH, D_ATTN)),
        "w_out": nrm(ks[19], (DEPTH, D_MIX, D_MODEL), D_MIX ** -0.5),
        "ffn2_norm": gain(ks[20], (DEPTH, D_MODEL)),
        "ffn2_w_gate": nrm(ks[21], (DEPTH, D_MODEL, D_FF), D_MODEL ** -0.5),
        "ffn2_w_up": nrm(ks[22], (DEPTH, D_MODEL, D_FF), D_MODEL ** -0.5),
        "ffn2_w_down": nrm(ks[23], (DEPTH, D_FF, D_MODEL), D_FF ** -0.5),
        "final_norm": gain(ks[24], (D_MODEL,)),
    }


def reference(x, ffn1_norm, ffn1_w_gate, ffn1_w_up, ffn1_w_down, mix_norm, w_in,
              ssm_log_dt, ssm_a_re, ssm_a_im, ssm_b_re, ssm_b_im, ssm_c_re, ssm_c_im, ssm_d,
              ssm_w_glu, ssm_b_glu, ssm_out_norm, attn_out_norm, w_out,
              ffn2_norm, ffn2_w_gate, ffn2_w_up, ffn2_w_down, final_norm):
    h = x
    for l in range(DEPTH):
        h = h + 0.5 * swiglu(rmsnorm(h, ffn1_norm[l]), ffn1_w_gate[l], ffn1_w_up[l], ffn1_w_down[l])
        hn = rmsnorm(h, mix_norm[l])
        proj = hn @ w_in[l]
        q, k, v, u = jnp.split(proj, [D_ATTN, 2 * D_ATTN, 3 * D_ATTN], axis=-1)
        y_ssm = s5_mixer(u, ssm_log_dt[l], ssm_a_re[l], ssm_a_im[l], ssm_b_re[l], ssm_b_im[l],
                         ssm_c_re[l], ssm_c_im[l], ssm_d[l], ssm_w_glu[l], ssm_b_glu[l])
        y_att = dilated_attention(q, k, v)
        mixed = jnp.concatenate([rmsnorm(y_ssm, ssm_out_norm[l]).astype(h.dtype),
                                 rmsnorm(y_att, attn_out_norm[l]).astype(h.dtype)], axis=-1)
        h = h + mixed @ w_out[l]
        h = h + 0.5 * swiglu(rmsnorm(h, ffn2_norm[l]), ffn2_w_gate[l], ffn2_w_up[l], ffn2_w_down[l])
    return rmsnorm(h, final_norm)
```

```python
from contextlib import ExitStack
import math
import numpy as np
import concourse.bass as bass
import concourse.mybir as mybir
from concourse.bass_utils import run_bass_kernel_spmd

F32 = mybir.dt.float32
BF16 = mybir.dt.bfloat16
I32 = mybir.dt.int32
ALU = mybir.AluOpType
AF = mybir.ActivationFunctionType

NCORES = 8
D = 2048
DT = 16
SEQ = 8192
BATCH = 2
OWN = 2048
TB = 512
NTB = SEQ // TB
DFF = 5504
FT = DFF // 128
EPS = 1e-6
NEG = -30000.0
L = 256
SUBS = TB // L
NST = 32


class Prog:
    ENGS = ("sp", "act", "pool", "pe", "dve")

    def __init__(self, nc, stack):
        self.nc = nc
        self.stack = stack
        self.ops = {e: [] for e in self.ENGS}
        self.sems = {}

    def _sem(self, name):
        if name not in self.sems:
            self.sems[name] = [self.stack.enter_context(self.nc.semaphore(name)), 0]
        return self.sems[name]

    def op(self, eng, fn, waits=(), sig=None, inc=1):
        tok = None
        if sig is not None:
            s = self._sem(sig)
            s[1] += inc
            tok = (sig, s[1])
        ws = tuple(w for w in waits if w is not None)
        self.ops[eng].append((fn, ws, sig, inc))
        return tok

    def dma(self, eng, out, in_, waits=(), sig=None):
        return self.op(eng, lambda e, o=out, i=in_: e.dma_start(out=o, in_=i), waits, sig, 16)

    def barrier(self):
        toks = [(s, c) for s, (h, c) in self.sems.items() if c > 0]
        for e in self.ENGS:
            self.op(e, None, waits=toks)
        return toks

    def emit(self):
        sems = self.sems
        ops = self.ops

        def run(e, lst):
            seen = {}
            for fn, ws, sig, inc in lst:
                for (s, v) in ws:
                    if seen.get(s, 0) < v:
                        e.wait_ge(sems[s][0], v)
                        seen[s] = v
                if fn is None:
                    continue
                ins = fn(e)
                if sig is not None:
                    ins.then_inc(sems[sig][0], inc)

        with self.nc.Block() as block:
            @block.sync
            def _(e):
                run(e, ops["sp"])

            @block.scalar
            def _(e):
                run(e, ops["act"])

            @block.gpsimd
            def _(e):
                run(e, ops["pool"])

            @block.tensor
            def _(e):
                run(e, ops["pe"])

            @block.vector
            def _(e):
                run(e, ops["dve"])


def build_nc(dbg=None):
    dbg = dbg or {}
    tbs_p1 = dbg.get("tbs_p1", list(range(NTB)))
    do = dbg.get("phases", (0, 1, 2, 3, 4))
    nc = bass.Bass("TRN2", target_bir_lowering=False)
    st = ExitStack()
    P = Prog(nc, st)

    def din(name, shape, dt=F32):
        return nc.dram_tensor(name, list(shape), dt, kind="ExternalInput").ap()

    def dscr(name, shape, dt):
        return nc.dram_tensor(name, list(shape), dt).ap()

    def sb(name, shape, dt):
        return st.enter_context(nc.sbuf_tensor("s_" + name, list(shape), dt))

    xs_d = din("xs", [SEQ, D])
    ident_d = din("ident", [128, 128])
    maskc_d = din("maskc", [128, 128])
    maskp_d = din("maskp", [128, 128])
    hbias_d = din("hbias", [128, 1])
    gains_d = din("gains", [128, 4 * DT])
    gains8_d = din("gains8", [128, 3 * 8])
    wfp = {
        "wg1": din("wg1", [FT, 128, D]), "wu1": din("wu1", [FT, 128, D]), "wd1": din("wd1", [DT, 128, DFF]),
        "win": din("win", [32, 128, D]), "wout": din("wout", [DT, 128, D]), "wglu": din("wglu", [8, 128, 1024]),
        "wg2": din("wg2", [FT, 128, D]), "wu2": din("wu2", [FT, 128, D]), "wd2": din("wd2", [DT, 128, DFF]),
    }
    ssm_sc_d = din("ssm_sc", [128, 3 * NST])
    ssm_b_d = din("ssm_b", [128, 2 * NST * 16])
    ssm_c_d = din("ssm_c", [128, 2 * NST * 16])
    ssm_dd_d = din("ssm_dd", [128, 8])
    y_d = nc.dram_tensor("y", [OWN, D], F32, kind="ExternalOutput").ap()

    wbf = {k: dscr(k + "b", v.shape, BF16) for k, v in wfp.items()}
    h1scr = dscr("h1scr", [D, OWN], F32)
    qscr = dscr("qscr", [1024, OWN], BF16)
    kscr = dscr("kscr", [1024, 2 * OWN], BF16)
    vscr = dscr("vscr", [2 * OWN, 2048], BF16)
    uscr = dscr("uscr", [1024, SEQ], BF16)
    yscr = dscr("yscr", [1024, OWN], BF16)
    ascr = dscr("ascr", [1024, OWN], BF16)

    dbg_outs = {}
    if dbg.get("outs"):
        for name, shape, dt in dbg["outs"]:
            dbg_outs[name] = nc.dram_tensor(name, list(shape), dt, kind="ExternalOutput").ap()

    ident = sb("ident", [128, 128], F32)
    identb = sb("identb", [128, 128], BF16)
    onesb = sb("onesb", [128, 128], BF16)
    gains = sb("gains", [128, 4 * DT], F32)
    gains8 = sb("gains8", [128, 24], F32)
    epsc = sb("epsc", [128, 1], F32)
    hbias = sb("hbias", [128, 1], F32)

    PS = [st.enter_context(nc.psum_tensor(f"ps{i}", [128, 512], F32)) for i in range(8)]

    t_c = []
    t_c.append(P.dma("sp", ident[:], ident_d, sig="cst"))
    t_c.append(P.dma("sp", gains[:], gains_d, sig="cst"))
    t_c.append(P.dma("sp", gains8[:], gains8_d, sig="cst"))
    t_c.append(P.dma("sp", hbias[:], hbias_d, sig="cst"))
    tk_cst = t_c[-1]
    P.op("dve", lambda e: e.memset(onesb[:], 1.0))
    P.op("dve", lambda e: e.memset(epsc[:], EPS))
    tk_setup = P.op("dve", lambda e: e.tensor_copy(out=identb[:], in_=ident[:]), waits=[tk_cst], sig="dve")

    conv_tok = {}
    gu_tok = [None] * FT
    NG = 4
    per = (FT + NG - 1) // NG
    for gi in range(NG):
        tok = None
        for fc in range(gi * per, min(FT, (gi + 1) * per)):
            P.dma("pool", wbf["wg1"][fc], wfp["wg1"][fc], sig=f"cvG{gi}")
            tok = P.dma("pool", wbf["wu1"][fc], wfp["wu1"][fc], sig=f"cvG{gi}")
        for fc in range(gi * per, min(FT, (gi + 1) * per)):
            gu_tok[fc] = tok
    for nm in ("wd1", "win"):
        tok = None
        for i in range(wfp[nm].shape[0]):
            tok = P.dma("pool", wbf[nm][i], wfp[nm][i], sig="cv_" + nm)
        conv_tok[nm] = tok
    conv_tok["A"] = conv_tok["win"]
    conv_A = {"gu": gu_tok, "d": conv_tok["wd1"]}
    tok = None
    for nm in ("wglu", "wout", "wg2", "wu2", "wd2"):
        for i in range(wfp[nm].shape[0]):
            tok = P.dma("pool", wbf[nm][i], wfp[nm][i], sig="cvB")
    conv_tok["B"] = tok

    class Ctx:
        pass

    def rms_to_bf16(srcT, ntile, gain_ap, dstT, sq, pst, rtmp, rstd, wait_src, dim, war_dst=()):
        t_sq = P.op("act", lambda e: e.activation(out=sq, in_=srcT, func=AF.Square), waits=list(wait_src), sig="act")
        tk = None
        for dt in range(ntile):
            tk = P.op("pe", lambda e, dt=dt: e.matmul(pst[:], lhsT=onesb[:], rhs=sq[:, dt, :], start=(dt == 0), stop=(dt == ntile - 1)),
                      waits=[t_sq, tk_setup] if dt == 0 else [], sig="pe" if dt == ntile - 1 else None)
        t_sqrt = P.op("act", lambda e: e.activation(out=rtmp, in_=pst[:], func=AF.Sqrt, bias=epsc[:, 0:1], scale=1.0 / dim), waits=[tk], sig="act")
        t_r = P.op("dve", lambda e: e.reciprocal(out=rstd, in_=rtmp), waits=[t_sqrt] + list(war_dst), sig="dve")
        tk2 = None
        for dt in range(ntile):
            tk2 = P.op("dve", lambda e, dt=dt: e.scalar_tensor_tensor(out=dstT[:, dt, :], in0=srcT[:, dt, :], scalar=gain_ap[:, dt:dt + 1], in1=rstd, op0=ALU.mult, op1=ALU.mult),
                       waits=[t_r] if dt == 0 else [], sig="dve" if dt == ntile - 1 else None)
        return tk2, t_sqrt

    W = Ctx()
    W.slot_free = [None] * 4
    W.n = 0
    Wd = Ctx()
    Wd.slot_free = [None] * 2
    Wd.n = 0

    def ffn(hT, xnT, hid, wbuf, wdbuf, sg, wg, wu, wd, conv_wait, psum_free, t_xn):
        hid_tok = [None] * FT
        for fc in range(FT):
            toks = []
            for k, wsrc in enumerate((wg, wu)):
                slot = W.n % 4
                W.n += 1
                cw = conv_wait["gu"][fc] if isinstance(conv_wait, dict) else conv_wait
                t_ld = P.dma("sp", wbuf[:, slot, :], wsrc[fc], waits=[W.slot_free[slot], cw], sig=f"w{slot}")
                bank = (0 if k == 0 else 2) + (fc % 2)
                tk = None
                for dt in range(DT):
                    tk = P.op("pe", lambda e, dt=dt, slot=slot, bank=bank: e.matmul(PS[bank][:], lhsT=wbuf[:, slot, dt * 128:(dt + 1) * 128], rhs=xnT[:, dt, :], start=(dt == 0), stop=(dt == DT - 1)),
                              waits=[t_ld, psum_free.get(bank), t_xn] if dt == 0 else [], sig="pe" if dt == DT - 1 else None)
                W.slot_free[slot] = tk
                toks.append(tk)
            bg, bu = fc % 2, 2 + fc % 2
            t_s = P.op("act", lambda e, bg=bg, fc=fc: e.activation(out=sg[:, fc % 2, :], in_=PS[bg][:], func=AF.Silu), waits=[toks[0], ffn.sg_free[fc % 2]], sig="act")
            psum_free[bg] = t_s
            t_h = P.op("dve", lambda e, bu=bu, fc=fc: e.tensor_tensor(out=hid[:, fc, :], in0=sg[:, fc % 2, :], in1=PS[bu][:], op=ALU.mult), waits=[t_s, toks[1], ffn.hid_free], sig="dve")
            psum_free[bu] = t_h
            ffn.sg_free[fc % 2] = t_h
            hid_tok[fc] = t_h
        t_last = None
        for mt in range(DT):
            slot = Wd.n % 2
            Wd.n += 1
            cw = conv_wait["d"] if isinstance(conv_wait, dict) else conv_wait
            t_ld = P.dma("sp", wdbuf[:, slot, :], wd[mt], waits=[Wd.slot_free[slot], cw], sig=f"wd{slot}")
            bank = 4 + mt % 2
            tk = None
            for ft in range(FT):
                tk = P.op("pe", lambda e, ft=ft, slot=slot, bank=bank: e.matmul(PS[bank][:], lhsT=wdbuf[:, slot, ft * 128:(ft + 1) * 128], rhs=hid[:, ft, :], start=(ft == 0), stop=(ft == FT - 1)),
                          waits=[t_ld, psum_free.get(bank), hid_tok[FT - 1]] if ft == 0 else [], sig="pe" if ft == FT - 1 else None)
            Wd.slot_free[slot] = tk
            t_last = P.op("dve", lambda e, mt=mt, bank=bank: e.scalar_tensor_tensor(out=hT[:, mt, :], in0=PS[bank][:], scalar=0.5, in1=hT[:, mt, :], op0=ALU.mult, op1=ALU.add), waits=[tk], sig="dve")
            psum_free[bank] = t_last
            t_pe_last = tk
        ffn.hid_free = t_pe_last
        return t_last, t_pe_last

    ffn.sg_free = [None, None]
    ffn.hid_free = None

    psum_free = {}
    phase_end = []

    if 1 in do:
        p1 = ExitStack()

        def sb1(name, shape, dt):
            return p1.enter_context(nc.sbuf_tensor("s1_" + name, list(shape), dt))
        hT = sb1("hT", [128, DT, TB], F32)
        xnT = sb1("xnT", [128, DT, TB], BF16)
        hid = sb1("hid", [128, FT, TB], BF16)
        xst = sb1("xst", [128, 2, D], F32)
        wbuf = sb1("wbuf", [128, 4, D], BF16)
        wdbuf = sb1("wdbuf", [128, 2, DFF], BF16)
        sg = sb1("sg", [128, 2, TB], F32)
        rtmp = sb1("rtmp", [128, TB], F32)
        rstd = sb1("rstd", [128, TB], F32)
        ostg = sb1("ostg", [128, 2, TB], BF16)
        vstg = sb1("vstg", [128, 4, 16, 128], BF16)
        sq = hid[:, 0:DT, :]

        xs_t = xs_d.rearrange("(n p) d -> n p d", p=128)
        xst_free = [None, None]
        nx = 0
        t_h_prev_readers = []
        t_xn_readers = None
        ostg_free = [None, None]
        no = 0
        vstg_free = P.op("dve", lambda e: e.memset(vstg[:].rearrange("p a h c -> p (a h c)"), 1.0), sig="dve")
        tpbank = [6, 7]
        ntp = 0
        for tb in tbs_p1:
            t_evs = []
            for s in range(4):
                slot = nx % 2
                nx += 1
                t_ld = P.dma("sp", xst[:, slot, :], xs_t[tb * 4 + s], waits=[xst_free[slot]], sig=f"x{slot}")
                tk = None
                for dq in range(4):
                    bank = tpbank[ntp % 2]
                    ntp += 1
                    for j in range(4):
                        dt = dq * 4 + j
                        tk = P.op("pe", lambda e, dt=dt, j=j, slot=slot, bank=bank: e.transpose(out=PS[bank][:, j * 128:(j + 1) * 128], in_=xst[:, slot, dt * 128:(dt + 1) * 128], identity=ident[:]),
                                  waits=[t_ld, psum_free.get(bank), tk_setup] if j == 0 else [], sig="pe" if j == 3 else None)
                    t_ev = P.op("act", lambda e, dq=dq, s=s, bank=bank: e.activation(out=hT[:, dq * 4:dq * 4 + 4, s * 128:(s + 1) * 128], in_=PS[bank][:].rearrange("p (a b) -> p a b", a=4), func=AF.Copy),
                                waits=[tk] + t_h_prev_readers, sig="act")
                    psum_free[bank] = t_ev
                    t_evs.append(t_ev)
                xst_free[slot] = tk
            t_h_prev_readers = []
            t_xn, t_stat = rms_to_bf16(hT[:], DT, gains[:, 0:DT], xnT, sq, PS[6], rtmp[:], rstd[:], [t_evs[-1], ffn.hid_free], D, war_dst=[t_xn_readers])
            psum_free[6] = t_stat
            t_res, t_pe_last = ffn(hT, xnT, hid, wbuf, wdbuf, sg, wbf["wg1"], wbf["wu1"], wbf["wd1"], conv_A, psum_free, t_xn)
            own_i = tb - (NTB - 4)
            halo_i = tb - (NTB - 8)
            if own_i >= 0:
                t_st = P.dma("sp", h1scr.rearrange("(dt p) t -> p dt t", p=128)[:, :, own_i * TB:(own_i + 1) * TB], hT[:], waits=[t_res], sig="h1st")
                t_h_prev_readers.append(t_st)
            t_hn, t_stat = rms_to_bf16(hT[:], DT, gains[:, DT:2 * DT], xnT, sq, PS[6], rtmp[:], rstd[:], [t_res, ffn.hid_free], D, war_dst=[])
            psum_free[6] = t_stat
            t_h_prev_readers.append(t_hn)
            tiles = list(range(24, 32))
            if halo_i >= 0:
                tiles = list(range(8, 24)) + tiles
            if own_i >= 0:
                tiles = list(range(0, 8)) + tiles
            for ct in tiles:
                slot = W.n % 4
                W.n += 1
                t_ld = P.dma("sp", wbuf[:, slot, :], wbf["win"][ct], waits=[W.slot_free[slot], conv_tok["A"]], sig=f"w{slot}")
                bank = 4 + (ct % 2)
                if 16 <= ct < 24:
                    tk = None
                    for s in range(4):
                        for dt in range(DT):
                            first = (s == 0 and dt == 0)
                            last = (s == 3 and dt == DT - 1)
                            tk = P.op("pe", lambda e, s=s, dt=dt, slot=slot, bank=bank: e.matmul(PS[bank][:, s * 128:(s + 1) * 128], lhsT=xnT[:, dt, s * 128:(s + 1) * 128], rhs=wbuf[:, slot, dt * 128:(dt + 1) * 128], start=(dt == 0), stop=(dt == DT - 1)),
                                      waits=[t_ld, psum_free.get(bank), t_hn] if first else [], sig="pe" if last else None)
                    W.slot_free[slot] = tk
                    t_ev = P.op("act", lambda e, ct=ct, bank=bank: e.activation(out=vstg[:, :, 2 * (ct - 16):2 * (ct - 16) + 2, 0:64], in_=PS[bank][:].rearrange("p (a h c) -> p a h c", a=4, h=2), func=AF.Copy),
                                waits=[tk, vstg_free if ct == 16 else None], sig="act")
                    psum_free[bank] = t_ev
                    if ct == 23:
                        vstg_free = P.dma("sp", vscr[halo_i * TB:(halo_i + 1) * TB, :].rearrange("(s p) c -> p s c", p=128), vstg[:].rearrange("p a h c -> p a (h c)"), waits=[t_ev], sig="vst")
                else:
                    tk = None
                    for dt in range(DT):
                        tk = P.op("pe", lambda e, dt=dt, slot=slot, bank=bank: e.matmul(PS[bank][:], lhsT=wbuf[:, slot, dt * 128:(dt + 1) * 128], rhs=xnT[:, dt, :], start=(dt == 0), stop=(dt == DT - 1)),
                                  waits=[t_ld, psum_free.get(bank), t_hn] if dt == 0 else [], sig="pe" if dt == DT - 1 else None)
                    W.slot_free[slot] = tk
                    os_ = no % 2
                    no += 1
                    t_ev = P.op("act", lambda e, os_=os_, bank=bank: e.activation(out=ostg[:, os_, :], in_=PS[bank][:], func=AF.Copy), waits=[tk, ostg_free[os_]], sig="act")
                    psum_free[bank] = t_ev
                    if ct < 8:
                        dst = qscr[ct * 128:(ct + 1) * 128, own_i * TB:(own_i + 1) * TB]
                    elif ct < 16:
                        dst = kscr[(ct - 8) * 128:(ct - 7) * 128, halo_i * TB:(halo_i + 1) * TB]
                    else:
                        dst = uscr[(ct - 24) * 128:(ct - 23) * 128, tb * TB:(tb + 1) * TB]
                    ostg_free[os_] = P.dma("sp", dst, ostg[:, os_, :], waits=[t_ev], sig=f"os{os_}")
                t_xn_readers = tk
        phase_end = [t for t in (ostg_free + [vstg_free, t_xn_readers] + t_h_prev_readers) if t is not None]
        p1.close()

    if "h1" in dbg_outs:
        tk = P.dma("sp", dbg_outs["h1"], h1scr, waits=phase_end, sig="dbg")
        phase_end.append(tk)
    if "u" in dbg_outs:
        phase_end.append(P.dma("sp", dbg_outs["u"], uscr, waits=phase_end, sig="dbg"))
        phase_end.append(P.dma("sp", dbg_outs["q"], qscr, waits=phase_end, sig="dbg"))
        phase_end.append(P.dma("sp", dbg_outs["k"], kscr, waits=phase_end, sig="dbg"))
        phase_end.append(P.dma("sp", dbg_outs["v"], vscr, waits=phase_end, sig="dbg"))


    if dbg.get("u_in"):
        u_in = din("u_in", [1024, SEQ], BF16)
        P.dma("sp", uscr, u_in, sig="dbg")

    if 2 in do:
        def _p2():
            P.barrier()
            p2 = ExitStack()

            def sb2(name, shape, dt):
                return p2.enter_context(nc.sbuf_tensor("s2_" + name, list(shape), dt))
            sc = sb2("sc", [128, 3 * NST], F32)
            rbig = sb2("rbig", [128, 4 * 2 * TB], F32)
            bl = rbig[:, 0:1024].rearrange("p (a s h) -> p a s h", a=2, h=16)
            cl = rbig[:, 1024:2048].rearrange("p (a s h) -> p a s h", a=2, h=16)
            dd = sb2("dd", [128, 8], F32)
            bb = rbig[:, 2048:3072].rearrange("p (a s h) -> p a s h", a=2, h=16)
            Zf = sb2("Zf", [128, NST, 2, 128], F32)
            Bt = sb2("Bt", [128, NST, 2, 128], BF16)
            Ct = sb2("Ct", [128, NST, 3, 128], BF16)
            Tre = sb2("Tre", [128, NST, L], F32)
            Tim = sb2("Tim", [128, NST, L], F32)
            sv = sb2("sv", [128, 24, NST], F32)
            ki = sb2("ki", [128, NST], I32)
            uT = sb2("uT", [128, 2, 8, TB], BF16)
            wre = sb2("wre", [128, 2, TB], F32)
            wim = sb2("wim", [128, 2, TB], F32)
            yv = sb2("yv", [128, TB], F32)
            y2 = sb2("y2", [128, TB], F32)
            ysg = sb2("ysg", [128, TB], F32)
            yst = sb2("yst", [128, 2, TB], BF16)
            ini = sb2("ini", [128, 2, NST], F32)
            rt = sb2("rt", [128, 4, 2], F32)

            V = lambda i: sv[:, i, :]
            LDT, LR, LI, DTv, MAG, TH, TQ, KF, THR, SIN, COS, MSK, THC, LBR, LBI, DEN, NR, FRE, FIM, ERE, EIM, TA, TB_, E7R = range(24)
            E7I = TQ
            t0 = P.dma("sp", sc[:], ssm_sc_d, sig="cst")
            P.dma("sp", rbig[:, 0:1024], ssm_b_d, sig="cst")
            P.dma("sp", rbig[:, 1024:2048], ssm_c_d, sig="cst")
            t_in = P.dma("sp", dd[:], ssm_dd_d, sig="cst")
            lr, li, ldt = sc[:, NST:2 * NST], sc[:, 2 * NST:3 * NST], sc[:, 0:NST]

            chain = {"on": True, "last": None}

            def dv(fn, waits=(), sig="dve"):
                ws = list(waits)
                if chain["on"]:
                    ws.append(chain["last"])
                    sig = "dve"
                tok = P.op("dve", fn, waits=ws, sig=sig)
                if chain["on"]:
                    chain["last"] = tok
                return tok

            def ac(fn, waits=(), sig="act"):
                return P.op("act", fn, waits=waits, sig=sig)
            t = ac(lambda e: e.activation(out=V(DTv), in_=ldt, func=AF.Exp), waits=[t_in])
            t = dv(lambda e: e.tensor_tensor(out=V(TA), in0=lr, in1=V(DTv), op=ALU.mult), waits=[t])
            t_mag = ac(lambda e: e.activation(out=V(MAG), in_=V(TA), func=AF.Exp), waits=[t])
            dv(lambda e: e.tensor_tensor(out=V(TH), in0=li, in1=V(DTv), op=ALU.mult))
            dv(lambda e: e.tensor_scalar(out=V(TQ), in0=V(TH), scalar1=1.0 / (2 * math.pi), scalar2=None, op0=ALU.mult))
            dv(lambda e: e.tensor_copy(out=ki[:], in_=V(TQ)))
            dv(lambda e: e.tensor_copy(out=V(KF), in_=ki[:]))
            dv(lambda e: e.scalar_tensor_tensor(out=V(THR), in0=V(KF), scalar=-6.28125, in1=V(TH), op0=ALU.mult, op1=ALU.add))
            dv(lambda e: e.scalar_tensor_tensor(out=V(THR), in0=V(KF), scalar=-(2 * math.pi - 6.28125), in1=V(THR), op0=ALU.mult, op1=ALU.add))
            dv(lambda e: e.tensor_scalar(out=V(THR), in0=V(THR), scalar1=3.1415925, scalar2=-3.1415925, op0=ALU.min, op1=ALU.max))
            dv(lambda e: e.tensor_scalar(out=V(MSK), in0=V(THR), scalar1=math.pi / 2, scalar2=None, op0=ALU.is_gt))
            dv(lambda e: e.scalar_tensor_tensor(out=V(THC), in0=V(MSK), scalar=-2 * math.pi, in1=V(THR), op0=ALU.mult, op1=ALU.add))
            t = dv(lambda e: e.tensor_scalar(out=V(THC), in0=V(THC), scalar1=math.pi / 2, scalar2=3.1415925, op0=ALU.add, op1=ALU.min))
            ac(lambda e: e.activation(out=V(SIN), in_=V(THR), func=AF.Sin), waits=[t])
            t = ac(lambda e: e.activation(out=V(COS), in_=V(THC), func=AF.Sin))
            dv(lambda e: e.tensor_tensor(out=V(LBR), in0=V(MAG), in1=V(COS), op=ALU.mult), waits=[t, t_mag])
            dv(lambda e: e.tensor_tensor(out=V(LBI), in0=V(MAG), in1=V(SIN), op=ALU.mult))
            dv(lambda e: e.tensor_tensor(out=V(DEN), in0=lr, in1=lr, op=ALU.mult))
            dv(lambda e: e.tensor_tensor(out=V(TA), in0=li, in1=li, op=ALU.mult))
            dv(lambda e: e.tensor_tensor(out=V(DEN), in0=V(DEN), in1=V(TA), op=ALU.add))
            dv(lambda e: e.reciprocal(out=V(DEN), in_=V(DEN)))
            dv(lambda e: e.tensor_scalar(out=V(NR), in0=V(LBR), scalar1=-1.0, scalar2=None, op0=ALU.add))
            dv(lambda e: e.tensor_tensor(out=V(TA), in0=V(NR), in1=lr, op=ALU.mult))
            dv(lambda e: e.tensor_tensor(out=V(TB_), in0=V(LBI), in1=li, op=ALU.mult))
            dv(lambda e: e.tensor_tensor(out=V(TA), in0=V(TA), in1=V(TB_), op=ALU.add))
            dv(lambda e: e.tensor_tensor(out=V(FRE), in0=V(TA), in1=V(DEN), op=ALU.mult))
            dv(lambda e: e.tensor_tensor(out=V(TA), in0=V(LBI), in1=lr, op=ALU.mult))
            dv(lambda e: e.tensor_tensor(out=V(TB_), in0=V(NR), in1=li, op=ALU.mult))
            dv(lambda e: e.tensor_tensor(out=V(TA), in0=V(TA), in1=V(TB_), op=ALU.subtract))
            dv(lambda e: e.tensor_tensor(out=V(FIM), in0=V(TA), in1=V(DEN), op=ALU.mult))
            fre_b = V(FRE).unsqueeze(2).to_broadcast([128, NST, 16])
            fim_b = V(FIM).unsqueeze(2).to_broadcast([128, NST, 16])
            tmp16a = Zf[:, 0:4, :, :].rearrange("p a b c -> p (a b c)")[:, 0:NST * 16].rearrange("p (s h) -> p s h", h=16)
            tmp16b = Zf[:, 4:8, :, :].rearrange("p a b c -> p (a b c)")[:, 0:NST * 16].rearrange("p (s h) -> p s h", h=16)
            dv(lambda e: e.tensor_tensor(out=tmp16a, in0=bl[:, 0], in1=fre_b, op=ALU.mult))
            dv(lambda e: e.tensor_tensor(out=tmp16b, in0=bl[:, 1], in1=fim_b, op=ALU.mult))
            dv(lambda e: e.tensor_tensor(out=bb[:, 0], in0=tmp16a, in1=tmp16b, op=ALU.subtract))
            dv(lambda e: e.tensor_tensor(out=tmp16a, in0=bl[:, 1], in1=fre_b, op=ALU.mult))
            dv(lambda e: e.tensor_tensor(out=tmp16b, in0=bl[:, 0], in1=fim_b, op=ALU.mult))
            dv(lambda e: e.tensor_tensor(out=bb[:, 1], in0=tmp16a, in1=tmp16b, op=ALU.add))
            Zflat = Zf[:].rearrange("p a b c -> p (a b c)")
            tmp1 = Zflat[:, 0:NST * (L // 2)].rearrange("p (s n) -> p s n", n=L // 2)
            dv(lambda e: e.memset(Tre[:, :, 0:1], 1.0))
            dv(lambda e: e.memset(Tim[:, :, 0:1], 0.0))
            dv(lambda e: e.tensor_copy(out=V(ERE), in_=V(COS)))
            dv(lambda e: e.tensor_copy(out=V(EIM), in_=V(SIN)))
            k = 0
            while (1 << k) < L:
                n = 1 << k
                k += 1
                er = V(ERE).unsqueeze(2).to_broadcast([128, NST, n])
                ei = V(EIM).unsqueeze(2).to_broadcast([128, NST, n])
                tt_ = tmp1[:, :, 0:n]
                dv(lambda e, n=n, er=er: e.tensor_tensor(out=Tre[:, :, n:2 * n], in0=Tre[:, :, 0:n], in1=er, op=ALU.mult))
                dv(lambda e, n=n, ei=ei, tt_=tt_: e.tensor_tensor(out=tt_, in0=Tim[:, :, 0:n], in1=ei, op=ALU.mult))
                dv(lambda e, n=n, tt_=tt_: e.tensor_tensor(out=Tre[:, :, n:2 * n], in0=Tre[:, :, n:2 * n], in1=tt_, op=ALU.subtract))
                dv(lambda e, n=n, ei=ei: e.tensor_tensor(out=Tim[:, :, n:2 * n], in0=Tre[:, :, 0:n], in1=ei, op=ALU.mult))
                dv(lambda e, n=n, er=er, tt_=tt_: e.tensor_tensor(out=tt_, in0=Tim[:, :, 0:n], in1=er, op=ALU.mult))
                dv(lambda e, n=n, tt_=tt_: e.tensor_tensor(out=Tim[:, :, n:2 * n], in0=Tim[:, :, n:2 * n], in1=tt_, op=ALU.add))
                dv(lambda e: e.tensor_tensor(out=V(TA), in0=V(ERE), in1=V(ERE), op=ALU.mult))
                dv(lambda e: e.tensor_tensor(out=V(TB_), in0=V(EIM), in1=V(EIM), op=ALU.mult))
                dv(lambda e: e.tensor_tensor(out=V(EIM), in0=V(ERE), in1=V(EIM), op=ALU.mult))
                dv(lambda e: e.tensor_scalar(out=V(EIM), in0=V(EIM), scalar1=2.0, scalar2=None, op0=ALU.mult))
                dv(lambda e: e.tensor_tensor(out=V(ERE), in0=V(TA), in1=V(TB_), op=ALU.subtract))
            dv(lambda e: e.tensor_copy(out=V(E7R), in_=V(ERE)))
            t_tab = dv(lambda e: e.tensor_copy(out=V(E7I), in_=V(EIM)))
            dv(lambda e: e.memset(Zflat, 0.0))
            dv(lambda e: e.memset(Ct[:].rearrange("p a b c -> p (a b c)"), 0.0))
            dv(lambda e: e.memset(ini[:].rearrange("p a s -> p (a s)"), 0.0))
            t_z = None
            for j in range(4):
                for gg in range(2):
                    ps_ = slice(gg * 64, (gg + 1) * 64)
                    cs_ = slice(32 * j + 16 * gg, 32 * j + 16 * gg + 16)
                    for ri in range(2):
                        dv(lambda e, j=j, ps_=ps_, cs_=cs_, ri=ri: e.tensor_copy(out=Zf[ps_, j::4, ri, cs_], in_=bb[ps_, ri, j::4, :]))
                        if ri == 0:
                            dv(lambda e, j=j, ps_=ps_, cs_=cs_: e.tensor_scalar(out=Ct[ps_, j::4, 2, cs_], in0=cl[ps_, 0, j::4, :], scalar1=-1.0, scalar2=None, op0=ALU.mult))
                            t_z = dv(lambda e, j=j, ps_=ps_, cs_=cs_: e.tensor_copy(out=Ct[ps_, j::4, 0, cs_], in_=cl[ps_, 0, j::4, :]))
                        else:
                            t_z = dv(lambda e, j=j, ps_=ps_, cs_=cs_: e.tensor_scalar(out=Ct[ps_, j::4, 1, cs_], in0=cl[ps_, 1, j::4, :], scalar1=-1.0, scalar2=None, op0=ALU.mult))
            t_bt = None
            pfree = {}
            for g4 in range(16):
                bank = 6 + g4 % 2
                tk = None
                for q4 in range(4):
                    st_, ri = divmod(g4 * 4 + q4, 2)
                    tk = P.op("pe", lambda e, st_=st_, ri=ri, q4=q4, bank=bank: e.transpose(out=PS[bank][:, q4 * 128:(q4 + 1) * 128], in_=Zf[:, st_, ri, :], identity=ident[:]),
                              waits=[t_z, pfree.get(bank)] if q4 == 0 else [], sig="pe" if q4 == 3 else None)
                st0_ = (g4 * 4) // 2
                t_bt = ac(lambda e, st0_=st0_, bank=bank: e.activation(out=Bt[:, st0_:st0_ + 2, :, :].rearrange("p a b c -> p (a b) c"), in_=PS[bank][:].rearrange("p (a c) -> p a c", a=4), func=AF.Copy), waits=[tk])
                pfree[bank] = t_bt

            chain["on"] = False
            P.barrier()
            Zfl = Zf[:].rearrange("p a b c -> p (a b c)")
            mm_ = [Zfl[:, i * 2 * TB:(i + 1) * 2 * TB].rearrange("p (j t) -> p j t", j=2) for i in range(4)]
            Zb = Zfl[:, 4 * 2 * TB:8 * 2 * TB].bitcast(BF16)
            dmb = [[Zb[:, (sl_ * 4 + i) * 2 * TB:(sl_ * 4 + i + 1) * 2 * TB].rearrange("p (j t) -> p j t", j=2) for i in range(4)] for sl_ in range(2)]
            rre2 = [rbig[:, (2 * k) * 2 * TB:(2 * k + 1) * 2 * TB].rearrange("p (j t) -> p j t", j=2) for k in range(2)]
            rim2 = [rbig[:, (2 * k + 1) * 2 * TB:(2 * k + 2) * 2 * TB].rearrange("p (j t) -> p j t", j=2) for k in range(2)]
            uscr_t = uscr.rearrange("(ct p) t -> p ct t", p=128)
            wre2 = [wre, sb2("wre_b", [128, 2, TB], F32)]
            wim2 = [wim, sb2("wim_b", [128, 2, TB], F32)]
            NP = NTB * 16
            S = dict(tk_y={}, dm_done={}, bu_free=None, t_u={}, u_free=[None, None], ybank_free={}, s_free=[None, None], dm_free=None,
                     yst_free=[None, None], yv_free=None, ny=0, t_rot=t_tab, mm_free=None, tk_bu={}, t_wj={}, t_m={},
                     chain_end={}, last_bu_tb={}, t_y_tb={})

            def load_u(tb):
                us = tb % 2
                S["t_u"][tb] = P.dma("sp", uT[:, us], uscr_t[:, :, tb * TB:(tb + 1) * TB], waits=[S["u_free"][us]], sig=f"u{us}")

            def do_bu(g):
                tb, pp = divmod(g, 16)
                us, ct = tb % 2, pp // 2
                tk = None
                for jj in range(2):
                    st_ = 2 * pp + jj
                    for ri in range(2):
                        tk = P.op("pe", lambda e, st_=st_, ri=ri, jj=jj, us=us, ct=ct: e.matmul(PS[2 * jj + ri][:], lhsT=Bt[:, st_, ri, :], rhs=uT[:, us, ct, :], start=True, stop=True),
                                  waits=[S["t_u"][tb], t_bt, S["bu_free"]] if (jj == 0 and ri == 0) else [], sig="pe" if (jj == 1 and ri == 1) else None)
                S["tk_bu"][g] = tk
                S["last_bu_tb"][tb] = tk

            def mod_piece(g, jj, half):
                tb, pp = divmod(g, 16)
                st_ = 2 * pp + jj
                wb = g % 2
                trb = Tre[:, st_, :].unsqueeze(1).to_broadcast([128, SUBS, L])
                tib = Tim[:, st_, :].unsqueeze(1).to_broadcast([128, SUBS, L])
                pre = PS[2 * jj][:].rearrange("p (a b) -> p a b", a=SUBS)
                pim = PS[2 * jj + 1][:].rearrange("p (a b) -> p a b", a=SUBS)
                combos = ((pre, trb), (pim, tib), (pim, trb), (pre, tib))
                t_m = None
                for i in (2 * half, 2 * half + 1):
                    src, tab = combos[i]
                    first = (jj == 0 and i == 0)
                    t_m = dv(lambda e, i=i, jj=jj, src=src, tab=tab: e.tensor_tensor(out=mm_[i][:, jj, :].rearrange("p (a b) -> p a b", a=SUBS), in0=src, in1=tab, op=ALU.mult),
                             waits=[S["tk_bu"][g], t_tab, S["mm_free"]] if first else [], sig="dve" if i == 3 else None)
                if half == 1:
                    dv(lambda e, jj=jj, wb=wb: e.tensor_tensor(out=wre2[wb][:, jj, :], in0=mm_[0][:, jj, :], in1=mm_[1][:, jj, :], op=ALU.add), waits=[t_m], sig=None)
                    S["t_wj"][(g, jj)] = dv(lambda e, jj=jj, wb=wb: e.tensor_tensor(out=wim2[wb][:, jj, :], in0=mm_[2][:, jj, :], in1=mm_[3][:, jj, :], op=ALU.subtract))
                    if jj == 1:
                        S["bu_free"] = t_m
                        S["mm_free"] = S["t_wj"][(g, 1)]

            def run_chain(g, pieces):
                tb, pp = divmod(g, 16)
                wb = g % 2
                rre, rim = rre2[wb], rim2[wb]
                pieces = list(pieces)
                for sub in range(SUBS):
                    sl = slice(sub * L, (sub + 1) * L)
                    t_sc = None
                    for jj in range(2):
                        st_ = 2 * pp + jj
                        magb = V(MAG)[:, st_:st_ + 1].to_broadcast([128, L])
                        dv(lambda e, jj=jj, st_=st_, sl=sl, magb=magb, wb=wb: e.tensor_tensor_scan(out=rre[:, jj, sl], data0=magb, data1=wre2[wb][:, jj, sl], initial=ini[:, 0, st_:st_ + 1], op0=ALU.mult, op1=ALU.add),
                           waits=[S["t_wj"][(g, jj)], S["t_rot"], S["dm_done"].get(g - 2)], sig=None)
                        t_sc = dv(lambda e, jj=jj, st_=st_, sl=sl, magb=magb, wb=wb: e.tensor_tensor_scan(out=rim[:, jj, sl], data0=magb, data1=wim2[wb][:, jj, sl], initial=ini[:, 1, st_:st_ + 1], op0=ALU.mult, op1=ALU.add))
                    if pieces:
                        pieces.pop(0)()
                    last = sub * L + L - 1
                    fr = rre[:, :, last]
                    fi = rim[:, :, last]
                    e7r = V(E7R)[:, 2 * pp:2 * pp + 2]
                    e7i = V(E7I)[:, 2 * pp:2 * pp + 2]
                    dv(lambda e, fr=fr, e7r=e7r: e.tensor_tensor(out=rt[:, 0, :], in0=fr, in1=e7r, op=ALU.mult), waits=[t_sc], sig=None)
                    dv(lambda e, fi=fi, e7i=e7i: e.tensor_tensor(out=rt[:, 1, :], in0=fi, in1=e7i, op=ALU.mult), sig=None)
                    dv(lambda e, fr=fr, e7i=e7i: e.tensor_tensor(out=rt[:, 2, :], in0=fr, in1=e7i, op=ALU.mult), sig=None)
                    t_rt = dv(lambda e, fi=fi, e7r=e7r: e.tensor_tensor(out=rt[:, 3, :], in0=fi, in1=e7r, op=ALU.mult))
                    if pieces:
                        pieces.pop(0)()
                    dv(lambda e, pp=pp: e.tensor_tensor(out=ini[:, 0, 2 * pp:2 * pp + 2], in0=rt[:, 0, :], in1=rt[:, 1, :], op=ALU.subtract), waits=[t_rt], sig=None)
                    S["t_rot"] = dv(lambda e, pp=pp: e.tensor_tensor(out=ini[:, 1, 2 * pp:2 * pp + 2], in0=rt[:, 2, :], in1=rt[:, 3, :], op=ALU.add))
                for pc in pieces:
                    pc()
                S["chain_end"][g] = S["t_rot"]

            def own_part(g):
                tb, pp = divmod(g, 16)
                us, ct = tb % 2, pp // 2
                own_i = tb - (NTB - 4)
                ss = pp % 2
                wb = g % 2
                rre, rim = rre2[wb], rim2[wb]
                t_d = None
                for jj in range(2):
                    st_ = 2 * pp + jj
                    trb = Tre[:, st_, :].unsqueeze(1).to_broadcast([128, SUBS, L])
                    tib = Tim[:, st_, :].unsqueeze(1).to_broadcast([128, SUBS, L])
                    rr = rre[:, jj, :].rearrange("p (a b) -> p a b", a=SUBS)
                    ri_ = rim[:, jj, :].rearrange("p (a b) -> p a b", a=SUBS)
                    for i, (src, tab) in enumerate(((rr, trb), (ri_, tib), (rr, tib), (ri_, trb))):
                        t_d = P.op("pool", lambda e, i=i, jj=jj, src=src, tab=tab, ss=ss: e.tensor_tensor(out=dmb[ss][i][:, jj, :].rearrange("p (a b) -> p a b", a=SUBS), in0=src, in1=tab, op=ALU.mult),
                                   waits=[S["t_rot"], S["s_free"][ss]] if (jj == 0 and i == 0) else [], sig="pool")
                S["dm_done"][g] = t_d
                yb = 4 + ct % 2
                tk_y = None
                csel = (0, 2, 1, 1)
                for jj in range(2):
                    st_ = 2 * pp + jj
                    for i in range(4):
                        first = (ss == 0 and jj == 0 and i == 0)
                        lastm = (ss == 1 and jj == 1 and i == 3)
                        tk_y = P.op("pe", lambda e, st_=st_, i=i, jj=jj, ss=ss, yb=yb, first=first, lastm=lastm: e.matmul(PS[yb][:], lhsT=Ct[:, st_, csel[i], :], rhs=dmb[ss][i][:, jj, :], start=first, stop=lastm),
                                    waits=[t_d, t_z, S["ybank_free"].get(yb)] if (jj == 0 and i == 0) else [], sig="pe" if (jj == 1 and i == 3) else None)
                S["s_free"][ss] = tk_y
                S["tk_y"][g] = tk_y

            def own_b(g):
                tb, pp = divmod(g, 16)
                us, ct = tb % 2, pp // 2
                own_i = tb - (NTB - 4)
                ss = pp % 2
                yb = 4 + ct % 2
                tk_y = S["tk_y"][g]
                if ss == 1:
                    t_y = dv(lambda e, us=us, ct=ct, yb=yb: e.scalar_tensor_tensor(out=yv[:], in0=uT[:, us, ct, :], scalar=dd[:, ct:ct + 1], in1=PS[yb][:], op0=ALU.mult, op1=ALU.add), waits=[tk_y, S["yv_free"]])
                    S["ybank_free"][yb] = t_y
                    S["t_y_tb"][tb] = t_y
                    t_i = P.op("pool", lambda e: e.tensor_tensor(out=y2[:], in0=yv[:], in1=yv[:], op=ALU.mult), waits=[t_y], sig="pool")
                    t_i = P.op("pool", lambda e: e.tensor_scalar(out=y2[:], in0=y2[:], scalar1=0.044715, scalar2=1.0, op0=ALU.mult, op1=ALU.add), waits=[t_i], sig="pool")
                    t_i = P.op("pool", lambda e: e.tensor_tensor(out=y2[:], in0=y2[:], in1=yv[:], op=ALU.mult), waits=[t_i], sig="pool")
                    t_g = ac(lambda e: e.activation(out=ysg[:], in_=y2[:], func=AF.Sigmoid, scale=2.0 * 0.7978845608028654), waits=[t_i])
                    ys = S["ny"] % 2
                    S["ny"] += 1
                    t_o = P.op("pool", lambda e, ys=ys: e.tensor_tensor(out=yst[:, ys, :], in0=yv[:], in1=ysg[:], op=ALU.mult), waits=[t_g, S["yst_free"][ys]], sig="pool")
                    S["yv_free"] = t_o
                    S["yst_free"][ys] = P.dma("sp", yscr[ct * 128:(ct + 1) * 128, own_i * TB:(own_i + 1) * TB], yst[:, ys, :], waits=[t_o], sig=f"ys{ys}")

            load_u(0)
            do_bu(0)
            for jj in range(2):
                for half in range(2):
                    mod_piece(0, jj, half)
            do_bu(1)
            for g in range(NP):
                tb, pp = divmod(g, 16)
                if pp == 1 and tb + 1 < NTB:
                    rd = [S["last_bu_tb"].get(tb - 1), S["t_y_tb"].get(tb - 1)]
                    rd = [r for r in rd if r is not None]
                    S["u_free"][(tb + 1) % 2] = rd[-1] if rd else None
                    if len(rd) == 2:
                        P.op("sp", None, waits=[rd[0]])
                    load_u(tb + 1)
                pieces = []
                if g + 1 < NP:
                    pieces = [(lambda jj=jj, half=half, g=g: mod_piece(g + 1, jj, half)) for jj in range(2) for half in range(2)]
                run_chain(g, pieces)
                if g + 2 < NP:
                    do_bu(g + 2)
                if g >= 1 and (g - 1) // 16 >= NTB - 4:
                    own_b(g - 1)
                if tb >= NTB - 4:
                    own_part(g)
            own_b(NP - 1)
            if "sv" in dbg_outs:
                P.barrier()
                P.dma("sp", dbg_outs["sv"], sv[:].rearrange("p a s -> p (a s)"), sig="dbg")
                P.dma("sp", dbg_outs["Tre"], Tre[:].rearrange("p a s -> p (a s)"), sig="dbg")
                P.dma("sp", dbg_outs["Tim"], Tim[:].rearrange("p a s -> p (a s)"), sig="dbg")
                P.dma("sp", dbg_outs["bbo"], rbig[:, 2048:3072], sig="dbg")
                P.dma("sp", dbg_outs["inio"], ini[:].rearrange("p a s -> p (a s)"), sig="dbg")
                P.barrier()
            p2.close()
            if "yssm" in dbg_outs:
                P.barrier()
                P.dma("sp", dbg_outs["yssm"], yscr, sig="dbg")

        _p2()

    if dbg.get("qkv_in"):
        P.dma("sp", qscr, din("q_in", [1024, OWN], BF16), sig="dbg")
        P.dma("sp", kscr, din("k_in", [1024, 2 * OWN], BF16), sig="dbg")
        P.dma("sp", vscr, din("v_in", [2 * OWN, 2048], BF16), sig="dbg")

    if 3 in do:
        def _p3():
            P.barrier()
            p3 = ExitStack()

            def sb3(name, shape, dt):
                return p3.enter_context(nc.sbuf_tensor("s3_" + name, list(shape), dt))
            mtmp = sb3("mtmp", [128, 2, 128], F32)
            qbd = sb3("qbd", [128, 2, 2, OWN], BF16)
            kT = sb3("kT", [128, 2, 2 * OWN], BF16)
            acc = sb3("acc", [128, 4, OWN], F32)
            dtmp = sb3("dtmp", [64, 4, OWN], F32)
            NVS = 12
            vch = sb3("vch", [128, NVS, 512], BF16)
            pT = sb3("pT", [128, 3, 512], BF16)
            pE = sb3("pE", [128, 3, 512], BF16)
            M01 = sb3("M01", [128, 2, 512], BF16)
            hval = sb3("hval", [128, 1], F32)
            yat = sb3("yat", [64, 4, OWN], BF16)

            t0 = P.dma("sp", mtmp[:, 0, :], maskc_d, sig="cst")
            t0 = P.dma("sp", mtmp[:, 1, :], maskp_d, sig="cst")
            t_hv = P.op("dve", lambda e: e.tensor_scalar(out=hval[:], in0=hbias[:, 0:1], scalar1=0.0, scalar2=None, op0=ALU.is_equal), waits=[t0, tk_cst], sig="dve")
            for v_ in range(2):
                for c_ in range(2):
                    P.op("dve", lambda e, v_=v_, c_=c_: e.tensor_scalar(out=M01[:, v_, c_ * 128:(c_ + 1) * 128], in0=mtmp[:, 1, :], scalar1=0.0, scalar2=None, op0=ALU.is_equal), sig="dve")
                    P.op("dve", lambda e, v_=v_, c_=c_: e.tensor_scalar(out=M01[:, v_, 256 + c_ * 128:256 + (c_ + 1) * 128], in0=mtmp[:, 0, :], scalar1=0.0, scalar2=None, op0=ALU.is_equal), sig="dve")
            t_mask = P.op("dve", lambda e: e.tensor_scalar(out=M01[:, 1, 0:256], in0=M01[:, 1, 0:256], scalar1=hval[:, 0:1], scalar2=None, op0=ALU.mult), waits=[t_hv], sig="dve")
            t_qz = P.op("dve", lambda e: e.memset(qbd[:].rearrange("p a b c -> p (a b c)"), 0.0), sig="dve")

            v_free = [None] * NVS
            nv = 0
            ps_s_free = [None, None, None]
            pT_free = [None, None, None]
            pE_free = [None, None, None]
            ps_od_free = [None, None]
            qk_free = t_qz
            acc_free = None
            yat_free = None
            dtmp_free = None
            for hq in range(4):
                t_q = None
                for t in range(2):
                    r0 = (2 * hq + t) * 128
                    P.dma("sp", qbd[0:64, t, 0, :], qscr[r0:r0 + 64, :], waits=[qk_free], sig="qk")
                    P.dma("sp", qbd[64:128, t, 1, :], qscr[r0 + 64:r0 + 128, :], waits=[qk_free], sig="qk")
                    t_q = P.dma("sp", kT[:, t, :], kscr[r0:r0 + 128, :], waits=[qk_free], sig="qk")
                t_z = P.op("dve", lambda e: e.memset(acc[:].rearrange("p a c -> p (a c)"), 0.0), waits=[acc_free], sig="dve")
                chunks = []
                blocks = []
                for d in (1, 4, 16):
                    nb = OWN // (128 * d)
                    for r in range(d):
                        for n in range(-1, nb):
                            chunks.append((OWN + 128 * n * d + r, d))
                            if n >= 0:
                                blocks.append((d, r, n, len(chunks) - 2, len(chunks) - 1))
                chunk_tok = {}
                nextc = {"c": 0}

                def ensure(upto):
                    while nextc["c"] <= min(upto, len(chunks) - 1):
                        c = nextc["c"]
                        a0, d_ = chunks[c]
                        slot = (nv0 + c) % NVS
                        chunk_tok[c] = P.dma("sp", vch[:, slot, :], vscr[a0:a0 + 127 * d_ + 1:d_, hq * 512:(hq + 1) * 512], waits=[v_free[slot]], sig=f"v{slot}")
                        nextc["c"] += 1
                nv0 = nv
                items = []
                for bj, (d, r, n, pc, cc) in enumerate(blocks):
                    for t in range(2):
                        items.append(dict(d=d, r=r, n=n, t=t, bj=bj, pc=pc, cc=cc))
                nv += len(chunks)
                state = {}

                def emit_S(i, it):
                    d, r, n, t = it["d"], it["r"], it["n"], it["t"]
                    b = i % 3
                    o0 = 128 * n * d + r
                    qap = qbd[:, t, :, o0:o0 + 127 * d + 1:d]
                    kcur = kT[:, t, OWN + o0:OWN + o0 + 127 * d + 1:d]
                    kprev = kT[:, t, OWN + o0 - 128 * d:OWN + o0 - d + 1:d]
                    mv = 1 if n == 0 else 0
                    P.op("pe", lambda e, b=b, kprev=kprev, qap=qap: e.matmul(PS[b][:, 0:256].rearrange("p (a c) -> p a c", a=2), lhsT=kprev, rhs=qap, start=True, stop=True), waits=[ps_s_free[b], t_q, tk_setup])
                    tk = P.op("pe", lambda e, b=b, kcur=kcur, qap=qap: e.matmul(PS[b][:, 256:512].rearrange("p (a c) -> p a c", a=2), lhsT=kcur, rhs=qap, start=True, stop=True), sig="pe")
                    t_e = P.op("act", lambda e, b=b: e.activation(out=pE[:, b, :], in_=PS[b][:], func=AF.Exp, scale=0.125), waits=[tk, pE_free[b]], sig="act")
                    ps_s_free[b] = t_e
                    t_p = P.op("pool", lambda e, b=b, mv=mv: e.tensor_tensor(out=pT[:, b, :], in0=pE[:, b, :], in1=M01[:, mv, :], op=ALU.mult), waits=[t_e, pT_free[b], t_mask], sig="pool")
                    pE_free[b] = t_p
                    it["t_e"] = t_p

                def emit_PV(i, it):
                    d, r, n, t = it["d"], it["r"], it["n"], it["t"]
                    b = i % 3
                    ob = 4 + i % 2
                    o0 = 128 * n * d + r
                    sp_, sc_ = (nv0 + it["pc"]) % NVS, (nv0 + it["cc"]) % NVS
                    tk = None
                    for ab in range(2):
                        hh = 2 * t + ab
                        P.op("pe", lambda e, b=b, ob=ob, sp_=sp_, hh=hh, ab=ab: e.matmul(PS[ob][:, ab * 128:(ab + 1) * 128], lhsT=vch[:, sp_, hh * 128:(hh + 1) * 128], rhs=pT[:, b, ab * 128:(ab + 1) * 128], start=True, stop=False),
                             waits=[it["t_e"], chunk_tok[it["pc"]], chunk_tok[it["cc"]], ps_od_free[i % 2]] if ab == 0 else [])
                        tk = P.op("pe", lambda e, b=b, ob=ob, sc_=sc_, hh=hh, ab=ab: e.matmul(PS[ob][:, ab * 128:(ab + 1) * 128], lhsT=vch[:, sc_, hh * 128:(hh + 1) * 128], rhs=pT[:, b, 256 + ab * 128:256 + (ab + 1) * 128], start=False, stop=True),
                                  sig="pe" if ab == 1 else None)
                    pT_free[b] = tk
                    if t == 1:
                        v_free[sp_] = tk
                        v_free[sc_] = tk
                    dst = acc[:, 2 * t:2 * t + 2, o0:o0 + 127 * d + 1:d]
                    t_a = P.op("dve", lambda e, ob=ob, dst=dst: e.tensor_tensor(out=dst, in0=dst, in1=PS[ob][:, 0:256].rearrange("p (a c) -> p a c", a=2), op=ALU.add), waits=[tk, t_z], sig="dve")
                    ps_od_free[i % 2] = t_a
                    state["t_a"] = t_a
                    state["last_pe"] = tk

                for i, it in enumerate(items):
                    if it["t"] == 0:
                        ensure(blocks[min(it["bj"] + 2, len(blocks) - 1)][4])
                    emit_S(i, it)
                    if i >= 2:
                        emit_PV(i - 2, items[i - 2])
                emit_PV(len(items) - 2, items[-2])
                emit_PV(len(items) - 1, items[-1])
                qk_free = state["last_pe"]
                t_dm = P.dma("sp", dtmp[:], acc[64:128, :, :], waits=[state["t_a"], dtmp_free], sig="dtmp")
                P.op("act", lambda e: e.activation(out=dtmp[:], in_=dtmp[:], func=AF.Ln), waits=[t_dm], sig="act")
                t_r = P.op("act", lambda e: e.activation(out=dtmp[:], in_=dtmp[:], func=AF.Exp, scale=-1.0), sig="act")
                t_y = P.op("dve", lambda e: e.tensor_tensor(out=yat[:], in0=acc[0:64, :, :], in1=dtmp[:], op=ALU.mult), waits=[t_r, yat_free], sig="dve")
                acc_free = t_y
                dtmp_free = t_y
                yat_free = P.dma("sp", ascr[hq * 256:(hq + 1) * 256, :].rearrange("(hh e) t -> e hh t", e=64), yat[:], waits=[t_y], sig="yat")
            p3.close()
            if "yatt" in dbg_outs:
                P.barrier()
                P.dma("sp", dbg_outs["yatt"], ascr, sig="dbg")

        _p3()

    if dbg.get("p4_in"):
        P.dma("sp", h1scr, din("h1_in", [D, OWN], F32), sig="dbg")
        P.dma("sp", yscr, din("ys_in", [1024, OWN], BF16), sig="dbg")
        P.dma("sp", ascr, din("ya_in", [1024, OWN], BF16), sig="dbg")

    if 4 in do:
        def _p4():
            P.barrier()
            p4 = ExitStack()

            def sb4(name, shape, dt):
                return p4.enter_context(nc.sbuf_tensor("s4_" + name, list(shape), dt))
            hT = sb4("hT", [128, DT, TB], F32)
            xnT = sb4("xnT", [128, DT, TB], BF16)
            hid = sb4("hid", [128, FT, TB], BF16)
            wbuf = sb4("wbuf", [128, 4, D], BF16)
            wdbuf = sb4("wdbuf", [128, 2, DFF], BF16)
            sg = sb4("sg", [128, 2, TB], F32)
            rtmp = sb4("rtmp", [128, TB], F32)
            rstd = sb4("rstd", [128, TB], F32)
            yT = sb4("yT", [128, 8, TB], BF16)
            aT = sb4("aT", [128, 8, TB], BF16)
            yg = sb4("yg", [128, 8, TB], F32)
            mixT = sb4("mixT", [128, DT, TB], BF16)
            ost = sb4("ost", [128, D], F32)
            sq = hid[:, 0:DT, :]
            W.slot_free = [None] * 4
            Wd.slot_free = [None] * 2
            ffn.sg_free = [None, None]
            ffn.hid_free = None
            psum_free = {}
            y_t = y_d.rearrange("(n p) d -> n p d", p=128)
            in_free = []
            ost_free = None
            mix_readers = None
            ntp = 0
            for i in dbg.get("tbs_p4", list(range(4))):
                sl = slice(i * TB, (i + 1) * TB)
                t_h = P.dma("sp", hT[:], h1scr.rearrange("(dt p) t -> p dt t", p=128)[:, :, sl], waits=in_free, sig="p4h")
                t_ys = P.dma("sp", yT[:], yscr.rearrange("(c p) t -> p c t", p=128)[:, :, sl], waits=in_free, sig="p4y")
                t_ya = P.dma("sp", aT[:], ascr.rearrange("(c p) t -> p c t", p=128)[:, :, sl], waits=in_free, sig="p4a")
                in_free = []
                t_g = None
                for mt in range(8):
                    slot = W.n % 4
                    W.n += 1
                    t_ld = P.dma("sp", wbuf[:, slot, 0:1024], wbf["wglu"][mt], waits=[W.slot_free[slot], conv_tok["B"]], sig=f"w{slot}")
                    bank = 4 + mt % 2
                    tk = None
                    for kt in range(8):
                        tk = P.op("pe", lambda e, kt=kt, slot=slot, bank=bank: e.matmul(PS[bank][:], lhsT=wbuf[:, slot, kt * 128:(kt + 1) * 128], rhs=yT[:, kt, :], start=(kt == 0), stop=(kt == 7)),
                                  waits=[t_ld, psum_free.get(bank), t_ys, tk_setup] if kt == 0 else [], sig="pe" if kt == 7 else None)
                    W.slot_free[slot] = tk
                    t_s = P.op("act", lambda e, mt=mt, bank=bank: e.activation(out=sg[:, mt % 2, :], in_=PS[bank][:], func=AF.Sigmoid, bias=gains8[:, 16 + mt:17 + mt]), waits=[tk, ffn.sg_free[mt % 2]], sig="act")
                    psum_free[bank] = t_s
                    t_g = P.op("dve", lambda e, mt=mt: e.tensor_tensor(out=yg[:, mt, :], in0=yT[:, mt, :], in1=sg[:, mt % 2, :], op=ALU.mult), waits=[t_s, mix_readers], sig="dve")
                    ffn.sg_free[mt % 2] = t_g
                t_m1, t_stat = rms_to_bf16(yg[:], 8, gains8[:, 0:8], mixT[:, 0:8, :], hid[:, 0:8, :], PS[6], rtmp[:], rstd[:], [t_g, ffn.hid_free], 1024, war_dst=[mix_readers])
                psum_free[6] = t_stat
                t_m2, t_stat = rms_to_bf16(aT[:], 8, gains8[:, 8:16], mixT[:, 8:16, :], hid[:, 0:8, :], PS[6], rtmp[:], rstd[:], [t_ya, t_m1], 1024, war_dst=[mix_readers])
                psum_free[6] = t_stat
                t_res = None
                for mt in range(DT):
                    slot = W.n % 4
                    W.n += 1
                    t_ld = P.dma("sp", wbuf[:, slot, :], wbf["wout"][mt], waits=[W.slot_free[slot], conv_tok["B"]], sig=f"w{slot}")
                    bank = 4 + mt % 2
                    tk = None
                    for kt in range(DT):
                        tk = P.op("pe", lambda e, kt=kt, slot=slot, bank=bank: e.matmul(PS[bank][:], lhsT=wbuf[:, slot, kt * 128:(kt + 1) * 128], rhs=mixT[:, kt, :], start=(kt == 0), stop=(kt == DT - 1)),
                                  waits=[t_ld, psum_free.get(bank), t_m1, t_m2] if kt == 0 else [], sig="pe" if kt == DT - 1 else None)
                    W.slot_free[slot] = tk
                    t_res = P.op("dve", lambda e, mt=mt, bank=bank: e.tensor_tensor(out=hT[:, mt, :], in0=PS[bank][:], in1=hT[:, mt, :], op=ALU.add), waits=[tk, t_h], sig="dve")
                    psum_free[bank] = t_res
                    mix_readers = tk
                in_free.append(mix_readers)
                t_xn, t_stat = rms_to_bf16(hT[:], DT, gains[:, 2 * DT:3 * DT], xnT, sq, PS[6], rtmp[:], rstd[:], [t_res, ffn.hid_free], D, war_dst=[])
                psum_free[6] = t_stat
                t_res2, t_pe_last = ffn(hT, xnT, hid, wbuf, wdbuf, sg, wbf["wg2"], wbf["wu2"], wbf["wd2"], conv_tok["B"], psum_free, t_xn)
                t_fin, t_stat = rms_to_bf16(hT[:], DT, gains[:, 3 * DT:4 * DT], hT, sq, PS[6], rtmp[:], rstd[:], [t_res2, ffn.hid_free], D, war_dst=[])
                psum_free[6] = t_stat
                tk = None
                for s_ in range(4):
                    t_ev = None
                    for dq in range(4):
                        bank = 6 + (ntp % 2)
                        ntp += 1
                        for j in range(4):
                            dt = dq * 4 + j
                            tk = P.op("pe", lambda e, dt=dt, j=j, s_=s_, bank=bank: e.transpose(out=PS[bank][:, j * 128:(j + 1) * 128], in_=hT[:, dt, s_ * 128:(s_ + 1) * 128], identity=ident[:]),
                                      waits=[t_fin, psum_free.get(bank)] if j == 0 else [], sig="pe" if j == 3 else None)
                        t_ev = P.op("act", lambda e, dq=dq, bank=bank: e.activation(out=ost[:, dq * 512:(dq + 1) * 512], in_=PS[bank][:], func=AF.Copy), waits=[tk, ost_free if dq == 0 else None], sig="act")
                        psum_free[bank] = t_ev
                    ost_free = P.dma("sp", y_t[i * 4 + s_], ost[:], waits=[t_ev], sig="yout")
                in_free.append(tk)
            p4.close()

        _p4()

    fin = [t for t in phase_end if t is not None]
    for s, (h, cnt) in list(P.sems.items()):
        if cnt > 0:
            fin.append((s, cnt))
    P.op("pool", lambda e: e.memset(epsc[:, 0:1], EPS), waits=fin)
    P.emit()
    st.close()
    return nc


def _tile_w(w, kt, mt):
    return np.ascontiguousarray(w.reshape(kt, 128, mt, 128).transpose(2, 1, 0, 3).reshape(mt, 128, kt * 128))


def _prep_shared(inp):
    f = lambda a: np.asarray(a, dtype=np.float32)
    sh = {}
    sh["wg1"] = _tile_w(f(inp["ffn1_w_gate"])[0], DT, FT)
    sh["wu1"] = _tile_w(f(inp["ffn1_w_up"])[0], DT, FT)
    sh["wd1"] = _tile_w(f(inp["ffn1_w_down"])[0], FT, DT)
    sh["wg2"] = _tile_w(f(inp["ffn2_w_gate"])[0], DT, FT)
    sh["wu2"] = _tile_w(f(inp["ffn2_w_up"])[0], DT, FT)
    sh["wd2"] = _tile_w(f(inp["ffn2_w_down"])[0], FT, DT)
    sh["win"] = _tile_w(f(inp["w_in"])[0], DT, 32)
    sh["wout"] = _tile_w(f(inp["w_out"])[0], DT, DT)
    sh["wglu"] = _tile_w(f(inp["ssm_w_glu"])[0], 8, 8)
    g16 = lambda v: f(v).reshape(DT, 128).T
    g8 = lambda v: f(v).reshape(8, 128).T
    sh["gains"] = np.ascontiguousarray(np.concatenate([g16(inp["ffn1_norm"][0]), g16(inp["mix_norm"][0]), g16(inp["ffn2_norm"][0]), g16(inp["final_norm"])], axis=1))
    sh["gains8"] = np.ascontiguousarray(np.concatenate([g8(inp["ssm_out_norm"][0]), g8(inp["attn_out_norm"][0]), g8(inp["ssm_b_glu"][0])], axis=1))
    def st_l(a):
        a = f(a)
        r = a.reshape(32, 2, 64, *a.shape[2:])
        r = np.moveaxis(r, 0, 2)
        return np.ascontiguousarray(r.reshape(128, 32, *a.shape[2:]))
    ldt = np.broadcast_to(f(inp["ssm_log_dt"])[0][:, None], (64, 64))
    sh["ssm_sc"] = np.ascontiguousarray(np.concatenate([st_l(ldt), st_l(inp["ssm_a_re"][0]), st_l(inp["ssm_a_im"][0])], axis=1))
    sh["ssm_b"] = np.ascontiguousarray(np.concatenate([st_l(inp["ssm_b_re"][0]).reshape(128, -1), st_l(inp["ssm_b_im"][0]).reshape(128, -1)], axis=1))
    cre = np.transpose(f(inp["ssm_c_re"])[0], (0, 2, 1))
    cim = np.transpose(f(inp["ssm_c_im"])[0], (0, 2, 1))
    sh["ssm_c"] = np.ascontiguousarray(np.concatenate([st_l(cre).reshape(128, -1), st_l(cim).reshape(128, -1)], axis=1))
    sh["ssm_dd"] = np.ascontiguousarray(f(inp["ssm_d"])[0].reshape(8, 128).T)
    sh["ident"] = np.eye(128, dtype=np.float32)
    k = np.arange(128)[:, None]
    q = np.arange(128)[None, :]
    sh["maskc"] = np.where(k <= q, 0.0, NEG).astype(np.float32)
    sh["maskp"] = np.where(k >= q, 0.0, NEG).astype(np.float32)
    return sh


def _prep_core(x, c):
    b, q = divmod(c, 4)
    xs = np.zeros((SEQ, D), np.float32)
    n = (q + 1) * OWN
    xs[SEQ - n:] = x[b, :n]
    hb = np.full((128, 1), 0.0 if q >= 1 else NEG, np.float32)
    return {"xs": xs, "hbias": hb}


def kernel(**inputs):
    x = np.asarray(inputs["x"], dtype=np.float32)
    nc = build_nc()
    sh = _prep_shared(inputs)
    in_maps = []
    for c in range(NCORES):
        m = dict(sh)
        m.update(_prep_core(x, c))
        in_maps.append(m)
    res = run_bass_kernel_spmd(nc, in_maps, core_ids=list(range(NCORES)))
    out = np.empty((BATCH, SEQ, D), np.float32)
    for c in range(NCORES):
        b, q = divmod(c, 4)
        out[b, q * OWN:(q + 1) * OWN] = np.asarray(res.results[c]["y"])
    return out
```

```python
from contextlib import ExitStack
import math
import numpy as np
import concourse.bass as bass
import concourse.mybir as mybir
from concourse.bass_utils import run_bass_kernel_spmd

F32 = mybir.dt.float32
BF16 = mybir.dt.bfloat16
I32 = mybir.dt.int32
ALU = mybir.AluOpType
AF = mybir.ActivationFunctionType

NCORES = 8
D = 2048
DT = 16
SEQ = 8192
BATCH = 2
OWN = 2048
TB = 512
NTB = SEQ // TB
DFF = 5504
FT = DFF // 128
EPS = 1e-6
NEG = -30000.0
L = 256
SUBS = TB // L
NST = 32


class Prog:
    ENGS = ("sp", "act", "pool", "pe", "dve")

    def __init__(self, nc, stack):
        self.nc = nc
        self.stack = stack
        self.ops = {e: [] for e in self.ENGS}
        self.sems = {}

    def _sem(self, name):
        if name not in self.sems:
            self.sems[name] = [self.stack.enter_context(self.nc.semaphore(name)), 0]
        return self.sems[name]

    def op(self, eng, fn, waits=(), sig=None, inc=1):
        tok = None
        if sig is not None:
            s = self._sem(sig)
            s[1] += inc
            tok = (sig, s[1])
        ws = tuple(w for w in waits if w is not None)
        self.ops[eng].append((fn, ws, sig, inc))
        return tok

    def dma(self, eng, out, in_, waits=(), sig=None):
        return self.op(eng, lambda e, o=out, i=in_: e.dma_start(out=o, in_=i), waits, sig, 16)

    def barrier(self):
        toks = [(s, c) for s, (h, c) in self.sems.items() if c > 0]
        for e in self.ENGS:
            self.op(e, None, waits=toks)
        return toks

    def emit(self):
        sems = self.sems
        ops = self.ops

        def run(e, lst):
            seen = {}
            for fn, ws, sig, inc in lst:
                for (s, v) in ws:
                    if seen.get(s, 0) < v:
                        e.wait_ge(sems[s][0], v)
                        seen[s] = v
                if fn is None:
                    continue
                ins = fn(e)
                if sig is not None:
                    ins.then_inc(sems[sig][0], inc)

        with self.nc.Block() as block:
            @block.sync
            def _(e):
                run(e, ops["sp"])

            @block.scalar
            def _(e):
                run(e, ops["act"])

            @block.gpsimd
            def _(e):
                run(e, ops["pool"])

            @block.tensor
            def _(e):
                run(e, ops["pe"])

            @block.vector
            def _(e):
                run(e, ops["dve"])


def build_nc(dbg=None):
    dbg = dbg or {}
    tbs_p1 = dbg.get("tbs_p1", list(range(NTB)))
    do = dbg.get("phases", (0, 1, 2, 3, 4))
    nc = bass.Bass("TRN2", target_bir_lowering=False)
    st = ExitStack()
    P = Prog(nc, st)

    def din(name, shape, dt=F32):
        return nc.dram_tensor(name, list(shape), dt, kind="ExternalInput").ap()

    def dscr(name, shape, dt):
        return nc.dram_tensor(name, list(shape), dt).ap()

    def sb(name, shape, dt):
        return st.enter_context(nc.sbuf_tensor("s_" + name, list(shape), dt))

    xs_d = din("xs", [SEQ, D])
    ident_d = din("ident", [128, 128])
    maskc_d = din("maskc", [128, 128])
    maskp_d = din("maskp", [128, 128])
    hbias_d = din("hbias", [128, 1])
    gains_d = din("gains", [128, 4 * DT])
    gains8_d = din("gains8", [128, 3 * 8])
    wfp = {
        "wg1": din("wg1", [FT, 128, D]), "wu1": din("wu1", [FT, 128, D]), "wd1": din("wd1", [DT, 128, DFF]),
        "win": din("win", [32, 128, D]), "wout": din("wout", [DT, 128, D]), "wglu": din("wglu", [8, 128, 1024]),
        "wg2": din("wg2", [FT, 128, D]), "wu2": din("wu2", [FT, 128, D]), "wd2": din("wd2", [DT, 128, DFF]),
    }
    ssm_sc_d = din("ssm_sc", [128, 3 * NST])
    ssm_b_d = din("ssm_b", [128, 2 * NST * 16])
    ssm_c_d = din("ssm_c", [128, 2 * NST * 16])
    ssm_dd_d = din("ssm_dd", [128, 8])
    y_d = nc.dram_tensor("y", [OWN, D], F32, kind="ExternalOutput").ap()

    wbf = {k: dscr(k + "b", v.shape, BF16) for k, v in wfp.items()}
    h1scr = dscr("h1scr", [D, OWN], F32)
    qscr = dscr("qscr", [1024, OWN], BF16)
    kscr = dscr("kscr", [1024, 2 * OWN], BF16)
    vscr = dscr("vscr", [2 * OWN, 2048], BF16)
    uscr = dscr("uscr", [1024, SEQ], BF16)
    yscr = dscr("yscr", [1024, OWN], BF16)
    ascr = dscr("ascr", [1024, OWN], BF16)

    dbg_outs = {}
    if dbg.get("outs"):
        for name, shape, dt in dbg["outs"]:
            dbg_outs[name] = nc.dram_tensor(name, list(shape), dt, kind="ExternalOutput").ap()

    ident = sb("ident", [128, 128], F32)
    identb = sb("identb", [128, 128], BF16)
    onesb = sb("onesb", [128, 128], BF16)
    gains = sb("gains", [128, 4 * DT], F32)
    gains8 = sb("gains8", [128, 24], F32)
    epsc = sb("epsc", [128, 1], F32)
    hbias = sb("hbias", [128, 1], F32)

    PS = [st.enter_context(nc.psum_tensor(f"ps{i}", [128, 512], F32)) for i in range(8)]

    t_c = []
    t_c.append(P.dma("sp", ident[:], ident_d, sig="cst"))
    t_c.append(P.dma("sp", gains[:], gains_d, sig="cst"))
    t_c.append(P.dma("sp", gains8[:], gains8_d, sig="cst"))
    t_c.append(P.dma("sp", hbias[:], hbias_d, sig="cst"))
    tk_cst = t_c[-1]
    P.op("dve", lambda e: e.memset(onesb[:], 1.0))
    P.op("dve", lambda e: e.memset(epsc[:], EPS))
    tk_setup = P.op("dve", lambda e: e.tensor_copy(out=identb[:], in_=ident[:]), waits=[tk_cst], sig="dve")

    conv_tok = {}
    gu_tok = [None] * FT
    NG = 4
    per = (FT + NG - 1) // NG
    for gi in range(NG):
        tok = None
        for fc in range(gi * per, min(FT, (gi + 1) * per)):
            P.dma("pool", wbf["wg1"][fc], wfp["wg1"][fc], sig=f"cvG{gi}")
            tok = P.dma("pool", wbf["wu1"][fc], wfp["wu1"][fc], sig=f"cvG{gi}")
        for fc in range(gi * per, min(FT, (gi + 1) * per)):
            gu_tok[fc] = tok
    for nm in ("wd1", "win"):
        tok = None
        for i in range(wfp[nm].shape[0]):
            tok = P.dma("pool", wbf[nm][i], wfp[nm][i], sig="cv_" + nm)
        conv_tok[nm] = tok
    conv_tok["A"] = conv_tok["win"]
    conv_A = {"gu": gu_tok, "d": conv_tok["wd1"]}
    tok = None
    for nm in ("wglu", "wout", "wg2", "wu2", "wd2"):
        for i in range(wfp[nm].shape[0]):
            tok = P.dma("pool", wbf[nm][i], wfp[nm][i], sig="cvB")
    conv_tok["B"] = tok

    class Ctx:
        pass

    def rms_to_bf16(srcT, ntile, gain_ap, dstT, sq, pst, rtmp, rstd, wait_src, dim, war_dst=()):
        t_sq = P.op("act", lambda e: e.activation(out=sq, in_=srcT, func=AF.Square), waits=list(wait_src), sig="act")
        tk = None
        for dt in range(ntile):
            tk = P.op("pe", lambda e, dt=dt: e.matmul(pst[:], lhsT=onesb[:], rhs=sq[:, dt, :], start=(dt == 0), stop=(dt == ntile - 1)),
                      waits=[t_sq, tk_setup] if dt == 0 else [], sig="pe" if dt == ntile - 1 else None)
        t_sqrt = P.op("act", lambda e: e.activation(out=rtmp, in_=pst[:], func=AF.Sqrt, bias=epsc[:, 0:1], scale=1.0 / dim), waits=[tk], sig="act")
        t_r = P.op("dve", lambda e: e.reciprocal(out=rstd, in_=rtmp), waits=[t_sqrt] + list(war_dst), sig="dve")
        tk2 = None
        for dt in range(ntile):
            tk2 = P.op("dve", lambda e, dt=dt: e.scalar_tensor_tensor(out=dstT[:, dt, :], in0=srcT[:, dt, :], scalar=gain_ap[:, dt:dt + 1], in1=rstd, op0=ALU.mult, op1=ALU.mult),
                       waits=[t_r] if dt == 0 else [], sig="dve" if dt == ntile - 1 else None)
        return tk2, t_sqrt

    W = Ctx()
    W.slot_free = [None] * 4
    W.n = 0
    Wd = Ctx()
    Wd.slot_free = [None] * 2
    Wd.n = 0

    def ffn(hT, xnT, hid, wbuf, wdbuf, sg, wg, wu, wd, conv_wait, psum_free, t_xn):
        hid_tok = [None] * FT
        for fc in range(FT):
            toks = []
            for k, wsrc in enumerate((wg, wu)):
                slot = W.n % 4
                W.n += 1
                cw = conv_wait["gu"][fc] if isinstance(conv_wait, dict) else conv_wait
                t_ld = P.dma("sp", wbuf[:, slot, :], wsrc[fc], waits=[W.slot_free[slot], cw], sig=f"w{slot}")
                bank = (0 if k == 0 else 2) + (fc % 2)
                tk = None
                for dt in range(DT):
                    tk = P.op("pe", lambda e, dt=dt, slot=slot, bank=bank: e.matmul(PS[bank][:], lhsT=wbuf[:, slot, dt * 128:(dt + 1) * 128], rhs=xnT[:, dt, :], start=(dt == 0), stop=(dt == DT - 1)),
                              waits=[t_ld, psum_free.get(bank), t_xn] if dt == 0 else [], sig="pe" if dt == DT - 1 else None)
                W.slot_free[slot] = tk
                toks.append(tk)
            bg, bu = fc % 2, 2 + fc % 2
            t_s = P.op("act", lambda e, bg=bg, fc=fc: e.activation(out=sg[:, fc % 2, :], in_=PS[bg][:], func=AF.Silu), waits=[toks[0], ffn.sg_free[fc % 2]], sig="act")
            psum_free[bg] = t_s
            t_h = P.op("dve", lambda e, bu=bu, fc=fc: e.tensor_tensor(out=hid[:, fc, :], in0=sg[:, fc % 2, :], in1=PS[bu][:], op=ALU.mult), waits=[t_s, toks[1], ffn.hid_free], sig="dve")
            psum_free[bu] = t_h
            ffn.sg_free[fc % 2] = t_h
            hid_tok[fc] = t_h
        t_last = None
        for mt in range(DT):
            slot = Wd.n % 2
            Wd.n += 1
            cw = conv_wait["d"] if isinstance(conv_wait, dict) else conv_wait
            t_ld = P.dma("sp", wdbuf[:, slot, :], wd[mt], waits=[Wd.slot_free[slot], cw], sig=f"wd{slot}")
            bank = 4 + mt % 2
            tk = None
            for ft in range(FT):
                tk = P.op("pe", lambda e, ft=ft, slot=slot, bank=bank: e.matmul(PS[bank][:], lhsT=wdbuf[:, slot, ft * 128:(ft + 1) * 128], rhs=hid[:, ft, :], start=(ft == 0), stop=(ft == FT - 1)),
                          waits=[t_ld, psum_free.get(bank), hid_tok[FT - 1]] if ft == 0 else [], sig="pe" if ft == FT - 1 else None)
            Wd.slot_free[slot] = tk
            t_last = P.op("dve", lambda e, mt=mt, bank=bank: e.scalar_tensor_tensor(out=hT[:, mt, :], in0=PS[bank][:], scalar=0.5, in1=hT[:, mt, :], op0=ALU.mult, op1=ALU.add), waits=[tk], sig="dve")
            psum_free[bank] = t_last
            t_pe_last = tk
        ffn.hid_free = t_pe_last
        return t_last, t_pe_last

    ffn.sg_free = [None, None]
    ffn.hid_free = None

    psum_free = {}
    phase_end = []

    if 1 in do:
        p1 = ExitStack()

        def sb1(name, shape, dt):
            return p1.enter_context(nc.sbuf_tensor("s1_" + name, list(shape), dt))
        hT = sb1("hT", [128, DT, TB], F32)
        xnT = sb1("xnT", [128, DT, TB], BF16)
        hid = sb1("hid", [128, FT, TB], BF16)
        xst = sb1("xst", [128, 2, D], F32)
        wbuf = sb1("wbuf", [128, 4, D], BF16)
        wdbuf = sb1("wdbuf", [128, 2, DFF], BF16)
        sg = sb1("sg", [128, 2, TB], F32)
        rtmp = sb1("rtmp", [128, TB], F32)
        rstd = sb1("rstd", [128, TB], F32)
        ostg = sb1("ostg", [128, 2, TB], BF16)
        vstg = sb1("vstg", [128, 4, 16, 128], BF16)
        sq = hid[:, 0:DT, :]

        xs_t = xs_d.rearrange("(n p) d -> n p d", p=128)
        xst_free = [None, None]
        nx = 0
        t_h_prev_readers = []
        t_xn_readers = None
        ostg_free = [None, None]
        no = 0
        vstg_free = P.op("dve", lambda e: e.memset(vstg[:].rearrange("p a h c -> p (a h c)"), 1.0), sig="dve")
        tpbank = [6, 7]
        ntp = 0
        for tb in tbs_p1:
            t_evs = []
            for s in range(4):
                slot = nx % 2
                nx += 1
                t_ld = P.dma("sp", xst[:, slot, :], xs_t[tb * 4 + s], waits=[xst_free[slot]], sig=f"x{slot}")
                tk = None
                for dq in range(4):
                    bank = tpbank[ntp % 2]
                    ntp += 1
                    for j in range(4):
                        dt = dq * 4 + j
                        tk = P.op("pe", lambda e, dt=dt, j=j, slot=slot, bank=bank: e.transpose(out=PS[bank][:, j * 128:(j + 1) * 128], in_=xst[:, slot, dt * 128:(dt + 1) * 128], identity=ident[:]),
                                  waits=[t_ld, psum_free.get(bank), tk_setup] if j == 0 else [], sig="pe" if j == 3 else None)
                    t_ev = P.op("act", lambda e, dq=dq, s=s, bank=bank: e.activation(out=hT[:, dq * 4:dq * 4 + 4, s * 128:(s + 1) * 128], in_=PS[bank][:].rearrange("p (a b) -> p a b", a=4), func=AF.Copy),
                                waits=[tk] + t_h_prev_readers, sig="act")
                    psum_free[bank] = t_ev
                    t_evs.append(t_ev)
                xst_free[slot] = tk
            t_h_prev_readers = []
            t_xn, t_stat = rms_to_bf16(hT[:], DT, gains[:, 0:DT], xnT, sq, PS[6], rtmp[:], rstd[:], [t_evs[-1], ffn.hid_free], D, war_dst=[t_xn_readers])
            psum_free[6] = t_stat
            t_res, t_pe_last = ffn(hT, xnT, hid, wbuf, wdbuf, sg, wbf["wg1"], wbf["wu1"], wbf["wd1"], conv_A, psum_free, t_xn)
            own_i = tb - (NTB - 4)
            halo_i = tb - (NTB - 8)
            if own_i >= 0:
                t_st = P.dma("pool", h1scr.rearrange("(dt p) t -> p dt t", p=128)[:, :, own_i * TB:(own_i + 1) * TB], hT[:], waits=[t_res], sig="h1st")
                t_h_prev_readers.append(t_st)
            t_hn, t_stat = rms_to_bf16(hT[:], DT, gains[:, DT:2 * DT], xnT, sq, PS[6], rtmp[:], rstd[:], [t_res, ffn.hid_free], D, war_dst=[])
            psum_free[6] = t_stat
            t_h_prev_readers.append(t_hn)
            tiles = list(range(24, 32))
            if halo_i >= 0:
                tiles = list(range(8, 24)) + tiles
            if own_i >= 0:
                tiles = list(range(0, 8)) + tiles
            for ct in tiles:
                slot = W.n % 4
                W.n += 1
                t_ld = P.dma("sp", wbuf[:, slot, :], wbf["win"][ct], waits=[W.slot_free[slot], conv_tok["A"]], sig=f"w{slot}")
                bank = 4 + (ct % 2)
                if 16 <= ct < 24:
                    tk = None
                    for s in range(4):
                        for dt in range(DT):
                            first = (s == 0 and dt == 0)
                            last = (s == 3 and dt == DT - 1)
                            tk = P.op("pe", lambda e, s=s, dt=dt, slot=slot, bank=bank: e.matmul(PS[bank][:, s * 128:(s + 1) * 128], lhsT=xnT[:, dt, s * 128:(s + 1) * 128], rhs=wbuf[:, slot, dt * 128:(dt + 1) * 128], start=(dt == 0), stop=(dt == DT - 1)),
                                      waits=[t_ld, psum_free.get(bank), t_hn] if first else [], sig="pe" if last else None)
                    W.slot_free[slot] = tk
                    t_ev = P.op("act", lambda e, ct=ct, bank=bank: e.activation(out=vstg[:, :, 2 * (ct - 16):2 * (ct - 16) + 2, 0:64], in_=PS[bank][:].rearrange("p (a h c) -> p a h c", a=4, h=2), func=AF.Copy),
                                waits=[tk, vstg_free if ct == 16 else None], sig="act")
                    psum_free[bank] = t_ev
                    if ct == 23:
                        vstg_free = P.dma("pool", vscr[halo_i * TB:(halo_i + 1) * TB, :].rearrange("(s p) c -> p s c", p=128), vstg[:].rearrange("p a h c -> p a (h c)"), waits=[t_ev], sig="vst")
                else:
                    tk = None
                    for dt in range(DT):
                        tk = P.op("pe", lambda e, dt=dt, slot=slot, bank=bank: e.matmul(PS[bank][:], lhsT=wbuf[:, slot, dt * 128:(dt + 1) * 128], rhs=xnT[:, dt, :], start=(dt == 0), stop=(dt == DT - 1)),
                                  waits=[t_ld, psum_free.get(bank), t_hn] if dt == 0 else [], sig="pe" if dt == DT - 1 else None)
                    W.slot_free[slot] = tk
                    os_ = no % 2
                    no += 1
                    t_ev = P.op("act", lambda e, os_=os_, bank=bank: e.activation(out=ostg[:, os_, :], in_=PS[bank][:], func=AF.Copy), waits=[tk, ostg_free[os_]], sig="act")
                    psum_free[bank] = t_ev
                    if ct < 8:
                        dst = qscr[ct * 128:(ct + 1) * 128, own_i * TB:(own_i + 1) * TB]
                    elif ct < 16:
                        dst = kscr[(ct - 8) * 128:(ct - 7) * 128, halo_i * TB:(halo_i + 1) * TB]
                    else:
                        dst = uscr[(ct - 24) * 128:(ct - 23) * 128, tb * TB:(tb + 1) * TB]
                    ostg_free[os_] = P.dma("pool", dst, ostg[:, os_, :], waits=[t_ev], sig=f"os{os_}")
                t_xn_readers = tk
        phase_end = [t for t in (ostg_free + [vstg_free, t_xn_readers] + t_h_prev_readers) if t is not None]
        p1.close()

    if "h1" in dbg_outs:
        tk = P.dma("sp", dbg_outs["h1"], h1scr, waits=phase_end, sig="dbg")
        phase_end.append(tk)
    if "u" in dbg_outs:
        phase_end.append(P.dma("sp", dbg_outs["u"], uscr, waits=phase_end, sig="dbg"))
        phase_end.append(P.dma("sp", dbg_outs["q"], qscr, waits=phase_end, sig="dbg"))
        phase_end.append(P.dma("sp", dbg_outs["k"], kscr, waits=phase_end, sig="dbg"))
        phase_end.append(P.dma("sp", dbg_outs["v"], vscr, waits=phase_end, sig="dbg"))


    if dbg.get("u_in"):
        u_in = din("u_in", [1024, SEQ], BF16)
        P.dma("sp", uscr, u_in, sig="dbg")

    if 2 in do:
        def _p2():
            P.barrier()
            p2 = ExitStack()

            def sb2(name, shape, dt):
                return p2.enter_context(nc.sbuf_tensor("s2_" + name, list(shape), dt))
            sc = sb2("sc", [128, 3 * NST], F32)
            rbig = sb2("rbig", [128, 4 * 2 * TB], F32)
            bl = rbig[:, 0:1024].rearrange("p (a s h) -> p a s h", a=2, h=16)
            cl = rbig[:, 1024:2048].rearrange("p (a s h) -> p a s h", a=2, h=16)
            dd = sb2("dd", [128, 8], F32)
            bb = rbig[:, 2048:3072].rearrange("p (a s h) -> p a s h", a=2, h=16)
            Zf = sb2("Zf", [128, NST, 2, 128], F32)
            Bt = sb2("Bt", [128, NST, 2, 128], BF16)
            Ct = sb2("Ct", [128, NST, 3, 128], BF16)
            Tre = sb2("Tre", [128, NST, L], F32)
            Tim = sb2("Tim", [128, NST, L], F32)
            sv = sb2("sv", [128, 24, NST], F32)
            ki = sb2("ki", [128, NST], I32)
            uT = sb2("uT", [128, 2, 8, TB], BF16)
            wre = sb2("wre", [128, 2, TB], F32)
            wim = sb2("wim", [128, 2, TB], F32)
            yv = sb2("yv", [128, TB], F32)
            y2 = sb2("y2", [128, TB], F32)
            ysg = sb2("ysg", [128, TB], F32)
            yst = sb2("yst", [128, 2, TB], BF16)
            ini = sb2("ini", [128, 2, NST], F32)
            rt = sb2("rt", [128, 4, 2], F32)

            V = lambda i: sv[:, i, :]
            LDT, LR, LI, DTv, MAG, TH, TQ, KF, THR, SIN, COS, MSK, THC, LBR, LBI, DEN, NR, FRE, FIM, ERE, EIM, TA, TB_, E7R = range(24)
            E7I = TQ
            t0 = P.dma("sp", sc[:], ssm_sc_d, sig="cst")
            P.dma("sp", rbig[:, 0:1024], ssm_b_d, sig="cst")
            P.dma("sp", rbig[:, 1024:2048], ssm_c_d, sig="cst")
            t_in = P.dma("sp", dd[:], ssm_dd_d, sig="cst")
            lr, li, ldt = sc[:, NST:2 * NST], sc[:, 2 * NST:3 * NST], sc[:, 0:NST]

            chain = {"on": True, "last": None}

            def dv(fn, waits=(), sig="dve"):
                ws = list(waits)
                if chain["on"]:
                    ws.append(chain["last"])
                    sig = "dve"
                tok = P.op("dve", fn, waits=ws, sig=sig)
                if chain["on"]:
                    chain["last"] = tok
                return tok

            def ac(fn, waits=(), sig="act"):
                return P.op("act", fn, waits=waits, sig=sig)
            t = ac(lambda e: e.activation(out=V(DTv), in_=ldt, func=AF.Exp), waits=[t_in])
            t = dv(lambda e: e.tensor_tensor(out=V(TA), in0=lr, in1=V(DTv), op=ALU.mult), waits=[t])
            t_mag = ac(lambda e: e.activation(out=V(MAG), in_=V(TA), func=AF.Exp), waits=[t])
            dv(lambda e: e.tensor_tensor(out=V(TH), in0=li, in1=V(DTv), op=ALU.mult))
            dv(lambda e: e.tensor_scalar(out=V(TQ), in0=V(TH), scalar1=1.0 / (2 * math.pi), scalar2=None, op0=ALU.mult))
            dv(lambda e: e.tensor_copy(out=ki[:], in_=V(TQ)))
            dv(lambda e: e.tensor_copy(out=V(KF), in_=ki[:]))
            dv(lambda e: e.scalar_tensor_tensor(out=V(THR), in0=V(KF), scalar=-6.28125, in1=V(TH), op0=ALU.mult, op1=ALU.add))
            dv(lambda e: e.scalar_tensor_tensor(out=V(THR), in0=V(KF), scalar=-(2 * math.pi - 6.28125), in1=V(THR), op0=ALU.mult, op1=ALU.add))
            dv(lambda e: e.tensor_scalar(out=V(THR), in0=V(THR), scalar1=3.1415925, scalar2=-3.1415925, op0=ALU.min, op1=ALU.max))
            dv(lambda e: e.tensor_scalar(out=V(MSK), in0=V(THR), scalar1=math.pi / 2, scalar2=None, op0=ALU.is_gt))
            dv(lambda e: e.scalar_tensor_tensor(out=V(THC), in0=V(MSK), scalar=-2 * math.pi, in1=V(THR), op0=ALU.mult, op1=ALU.add))
            t = dv(lambda e: e.tensor_scalar(out=V(THC), in0=V(THC), scalar1=math.pi / 2, scalar2=3.1415925, op0=ALU.add, op1=ALU.min))
            ac(lambda e: e.activation(out=V(SIN), in_=V(THR), func=AF.Sin), waits=[t])
            t = ac(lambda e: e.activation(out=V(COS), in_=V(THC), func=AF.Sin))
            dv(lambda e: e.tensor_tensor(out=V(LBR), in0=V(MAG), in1=V(COS), op=ALU.mult), waits=[t, t_mag])
            dv(lambda e: e.tensor_tensor(out=V(LBI), in0=V(MAG), in1=V(SIN), op=ALU.mult))
            dv(lambda e: e.tensor_tensor(out=V(DEN), in0=lr, in1=lr, op=ALU.mult))
            dv(lambda e: e.tensor_tensor(out=V(TA), in0=li, in1=li, op=ALU.mult))
            dv(lambda e: e.tensor_tensor(out=V(DEN), in0=V(DEN), in1=V(TA), op=ALU.add))
            dv(lambda e: e.reciprocal(out=V(DEN), in_=V(DEN)))
            dv(lambda e: e.tensor_scalar(out=V(NR), in0=V(LBR), scalar1=-1.0, scalar2=None, op0=ALU.add))
            dv(lambda e: e.tensor_tensor(out=V(TA), in0=V(NR), in1=lr, op=ALU.mult))
            dv(lambda e: e.tensor_tensor(out=V(TB_), in0=V(LBI), in1=li, op=ALU.mult))
            dv(lambda e: e.tensor_tensor(out=V(TA), in0=V(TA), in1=V(TB_), op=ALU.add))
            dv(lambda e: e.tensor_tensor(out=V(FRE), in0=V(TA), in1=V(DEN), op=ALU.mult))
            dv(lambda e: e.tensor_tensor(out=V(TA), in0=V(LBI), in1=lr, op=ALU.mult))
            dv(lambda e: e.tensor_tensor(out=V(TB_), in0=V(NR), in1=li, op=ALU.mult))
            dv(lambda e: e.tensor_tensor(out=V(TA), in0=V(TA), in1=V(TB_), op=ALU.subtract))
            dv(lambda e: e.tensor_tensor(out=V(FIM), in0=V(TA), in1=V(DEN), op=ALU.mult))
            fre_b = V(FRE).unsqueeze(2).to_broadcast([128, NST, 16])
            fim_b = V(FIM).unsqueeze(2).to_broadcast([128, NST, 16])
            tmp16a = Zf[:, 0:4, :, :].rearrange("p a b c -> p (a b c)")[:, 0:NST * 16].rearrange("p (s h) -> p s h", h=16)
            tmp16b = Zf[:, 4:8, :, :].rearrange("p a b c -> p (a b c)")[:, 0:NST * 16].rearrange("p (s h) -> p s h", h=16)
            dv(lambda e: e.tensor_tensor(out=tmp16a, in0=bl[:, 0], in1=fre_b, op=ALU.mult))
            dv(lambda e: e.tensor_tensor(out=tmp16b, in0=bl[:, 1], in1=fim_b, op=ALU.mult))
            dv(lambda e: e.tensor_tensor(out=bb[:, 0], in0=tmp16a, in1=tmp16b, op=ALU.subtract))
            dv(lambda e: e.tensor_tensor(out=tmp16a, in0=bl[:, 1], in1=fre_b, op=ALU.mult))
            dv(lambda e: e.tensor_tensor(out=tmp16b, in0=bl[:, 0], in1=fim_b, op=ALU.mult))
            dv(lambda e: e.tensor_tensor(out=bb[:, 1], in0=tmp16a, in1=tmp16b, op=ALU.add))
            Zflat = Zf[:].rearrange("p a b c -> p (a b c)")
            tmp1 = Zflat[:, 0:NST * (L // 2)].rearrange("p (s n) -> p s n", n=L // 2)
            dv(lambda e: e.memset(Tre[:, :, 0:1], 1.0))
            dv(lambda e: e.memset(Tim[:, :, 0:1], 0.0))
            dv(lambda e: e.tensor_copy(out=V(ERE), in_=V(COS)))
            dv(lambda e: e.tensor_copy(out=V(EIM), in_=V(SIN)))
            k = 0
            while (1 << k) < L:
                n = 1 << k
                k += 1
                er = V(ERE).unsqueeze(2).to_broadcast([128, NST, n])
                ei = V(EIM).unsqueeze(2).to_broadcast([128, NST, n])
                tt_ = tmp1[:, :, 0:n]
                dv(lambda e, n=n, er=er: e.tensor_tensor(out=Tre[:, :, n:2 * n], in0=Tre[:, :, 0:n], in1=er, op=ALU.mult))
                dv(lambda e, n=n, ei=ei, tt_=tt_: e.tensor_tensor(out=tt_, in0=Tim[:, :, 0:n], in1=ei, op=ALU.mult))
                dv(lambda e, n=n, tt_=tt_: e.tensor_tensor(out=Tre[:, :, n:2 * n], in0=Tre[:, :, n:2 * n], in1=tt_, op=ALU.subtract))
                dv(lambda e, n=n, ei=ei: e.tensor_tensor(out=Tim[:, :, n:2 * n], in0=Tre[:, :, 0:n], in1=ei, op=ALU.mult))
                dv(lambda e, n=n, er=er, tt_=tt_: e.tensor_tensor(out=tt_, in0=Tim[:, :, 0:n], in1=er, op=ALU.mult))
                dv(lambda e, n=n, tt_=tt_: e.tensor_tensor(out=Tim[:, :, n:2 * n], in0=Tim[:, :, n:2 * n], in1=tt_, op=ALU.add))
                dv(lambda e: e.tensor_tensor(out=V(TA), in0=V(ERE), in1=V(ERE), op=ALU.mult))
                dv(lambda e: e.tensor_tensor(out=V(TB_), in0=V(EIM), in1=V(EIM), op=ALU.mult))
                dv(lambda e: e.tensor_tensor(out=V(EIM), in0=V(ERE), in1=V(EIM), op=ALU.mult))
                dv(lambda e: e.tensor_scalar(out=V(EIM), in0=V(EIM), scalar1=2.0, scalar2=None, op0=ALU.mult))
                dv(lambda e: e.tensor_tensor(out=V(ERE), in0=V(TA), in1=V(TB_), op=ALU.subtract))
            dv(lambda e: e.tensor_copy(out=V(E7R), in_=V(ERE)))
            t_tab = dv(lambda e: e.tensor_copy(out=V(E7I), in_=V(EIM)))
            dv(lambda e: e.memset(Zflat, 0.0))
            dv(lambda e: e.memset(Ct[:].rearrange("p a b c -> p (a b c)"), 0.0))
            dv(lambda e: e.memset(ini[:].rearrange("p a s -> p (a s)"), 0.0))
            t_z = None
            for j in range(4):
                for gg in range(2):
                    ps_ = slice(gg * 64, (gg + 1) * 64)
                    cs_ = slice(32 * j + 16 * gg, 32 * j + 16 * gg + 16)
                    for ri in range(2):
                        dv(lambda e, j=j, ps_=ps_, cs_=cs_, ri=ri: e.tensor_copy(out=Zf[ps_, j::4, ri, cs_], in_=bb[ps_, ri, j::4, :]))
                        if ri == 0:
                            dv(lambda e, j=j, ps_=ps_, cs_=cs_: e.tensor_scalar(out=Ct[ps_, j::4, 2, cs_], in0=cl[ps_, 0, j::4, :], scalar1=-1.0, scalar2=None, op0=ALU.mult))
                            t_z = dv(lambda e, j=j, ps_=ps_, cs_=cs_: e.tensor_copy(out=Ct[ps_, j::4, 0, cs_], in_=cl[ps_, 0, j::4, :]))
                        else:
                            t_z = dv(lambda e, j=j, ps_=ps_, cs_=cs_: e.tensor_scalar(out=Ct[ps_, j::4, 1, cs_], in0=cl[ps_, 1, j::4, :], scalar1=-1.0, scalar2=None, op0=ALU.mult))
            t_bt = None
            pfree = {}
            for g4 in range(16):
                bank = 6 + g4 % 2
                tk = None
                for q4 in range(4):
                    st_, ri = divmod(g4 * 4 + q4, 2)
                    tk = P.op("pe", lambda e, st_=st_, ri=ri, q4=q4, bank=bank: e.transpose(out=PS[bank][:, q4 * 128:(q4 + 1) * 128], in_=Zf[:, st_, ri, :], identity=ident[:]),
                              waits=[t_z, pfree.get(bank)] if q4 == 0 else [], sig="pe" if q4 == 3 else None)
                st0_ = (g4 * 4) // 2
                t_bt = ac(lambda e, st0_=st0_, bank=bank: e.activation(out=Bt[:, st0_:st0_ + 2, :, :].rearrange("p a b c -> p (a b) c"), in_=PS[bank][:].rearrange("p (a c) -> p a c", a=4), func=AF.Copy), waits=[tk])
                pfree[bank] = t_bt

            chain["on"] = False
            P.barrier()
            Zfl = Zf[:].rearrange("p a b c -> p (a b c)")
            mm_ = [Zfl[:, i * 2 * TB:(i + 1) * 2 * TB].rearrange("p (j t) -> p j t", j=2) for i in range(4)]
            Zb = Zfl[:, 4 * 2 * TB:8 * 2 * TB].bitcast(BF16)
            dmb = [[Zb[:, (sl_ * 4 + i) * 2 * TB:(sl_ * 4 + i + 1) * 2 * TB].rearrange("p (j t) -> p j t", j=2) for i in range(4)] for sl_ in range(2)]
            rre2 = [rbig[:, (2 * k) * 2 * TB:(2 * k + 1) * 2 * TB].rearrange("p (j t) -> p j t", j=2) for k in range(2)]
            rim2 = [rbig[:, (2 * k + 1) * 2 * TB:(2 * k + 2) * 2 * TB].rearrange("p (j t) -> p j t", j=2) for k in range(2)]
            uscr_t = uscr.rearrange("(ct p) t -> p ct t", p=128)
            wre2 = [wre, sb2("wre_b", [128, 2, TB], F32)]
            wim2 = [wim, sb2("wim_b", [128, 2, TB], F32)]
            NP = NTB * 16
            S = dict(tk_y={}, dm_done={}, bu_free=None, t_u={}, u_free=[None, None], ybank_free={}, s_free=[None, None], dm_free=None,
                     yst_free=[None, None], yv_free=None, ny=0, t_rot=t_tab, mm_free=None, tk_bu={}, t_wj={}, t_m={},
                     chain_end={}, last_bu_tb={}, t_y_tb={})

            def load_u(tb):
                us = tb % 2
                S["t_u"][tb] = P.dma("sp", uT[:, us], uscr_t[:, :, tb * TB:(tb + 1) * TB], waits=[S["u_free"][us]], sig=f"u{us}")

            def do_bu(g):
                tb, pp = divmod(g, 16)
                us, ct = tb % 2, pp // 2
                tk = None
                for jj in range(2):
                    st_ = 2 * pp + jj
                    for ri in range(2):
                        tk = P.op("pe", lambda e, st_=st_, ri=ri, jj=jj, us=us, ct=ct: e.matmul(PS[2 * jj + ri][:], lhsT=Bt[:, st_, ri, :], rhs=uT[:, us, ct, :], start=True, stop=True),
                                  waits=[S["t_u"][tb], t_bt, S["bu_free"]] if (jj == 0 and ri == 0) else [], sig="pe" if (jj == 1 and ri == 1) else None)
                S["tk_bu"][g] = tk
                S["last_bu_tb"][tb] = tk

            def mod_piece(g, jj, half):
                tb, pp = divmod(g, 16)
                st_ = 2 * pp + jj
                wb = g % 2
                trb = Tre[:, st_, :].unsqueeze(1).to_broadcast([128, SUBS, L])
                tib = Tim[:, st_, :].unsqueeze(1).to_broadcast([128, SUBS, L])
                pre = PS[2 * jj][:].rearrange("p (a b) -> p a b", a=SUBS)
                pim = PS[2 * jj + 1][:].rearrange("p (a b) -> p a b", a=SUBS)
                combos = ((pre, trb), (pim, tib), (pim, trb), (pre, tib))
                t_m = None
                for i in (2 * half, 2 * half + 1):
                    src, tab = combos[i]
                    first = (jj == 0 and i == 0)
                    t_m = dv(lambda e, i=i, jj=jj, src=src, tab=tab: e.tensor_tensor(out=mm_[i][:, jj, :].rearrange("p (a b) -> p a b", a=SUBS), in0=src, in1=tab, op=ALU.mult),
                             waits=[S["tk_bu"][g], t_tab, S["mm_free"]] if first else [], sig="dve" if i == 3 else None)
                if half == 1:
                    dv(lambda e, jj=jj, wb=wb: e.tensor_tensor(out=wre2[wb][:, jj, :], in0=mm_[0][:, jj, :], in1=mm_[1][:, jj, :], op=ALU.add), waits=[t_m], sig=None)
                    S["t_wj"][(g, jj)] = dv(lambda e, jj=jj, wb=wb: e.tensor_tensor(out=wim2[wb][:, jj, :], in0=mm_[2][:, jj, :], in1=mm_[3][:, jj, :], op=ALU.subtract))
                    if jj == 1:
                        S["bu_free"] = t_m
                        S["mm_free"] = S["t_wj"][(g, 1)]

            def run_chain(g, pieces):
                tb, pp = divmod(g, 16)
                wb = g % 2
                rre, rim = rre2[wb], rim2[wb]
                pieces = list(pieces)
                for sub in range(SUBS):
                    sl = slice(sub * L, (sub + 1) * L)
                    t_sc = None
                    for jj in range(2):
                        st_ = 2 * pp + jj
                        magb = V(MAG)[:, st_:st_ + 1].to_broadcast([128, L])
                        dv(lambda e, jj=jj, st_=st_, sl=sl, magb=magb, wb=wb: e.tensor_tensor_scan(out=rre[:, jj, sl], data0=magb, data1=wre2[wb][:, jj, sl], initial=ini[:, 0, st_:st_ + 1], op0=ALU.mult, op1=ALU.add),
                           waits=[S["t_wj"][(g, jj)], S["t_rot"], S["dm_done"].get(g - 2)], sig=None)
                        t_sc = dv(lambda e, jj=jj, st_=st_, sl=sl, magb=magb, wb=wb: e.tensor_tensor_scan(out=rim[:, jj, sl], data0=magb, data1=wim2[wb][:, jj, sl], initial=ini[:, 1, st_:st_ + 1], op0=ALU.mult, op1=ALU.add))
                    if pieces:
                        pieces.pop(0)()
                    last = sub * L + L - 1
                    fr = rre[:, :, last]
                    fi = rim[:, :, last]
                    e7r = V(E7R)[:, 2 * pp:2 * pp + 2]
                    e7i = V(E7I)[:, 2 * pp:2 * pp + 2]
                    dv(lambda e, fr=fr, e7r=e7r: e.tensor_tensor(out=rt[:, 0, :], in0=fr, in1=e7r, op=ALU.mult), waits=[t_sc], sig=None)
                    dv(lambda e, fi=fi, e7i=e7i: e.tensor_tensor(out=rt[:, 1, :], in0=fi, in1=e7i, op=ALU.mult), sig=None)
                    dv(lambda e, fr=fr, e7i=e7i: e.tensor_tensor(out=rt[:, 2, :], in0=fr, in1=e7i, op=ALU.mult), sig=None)
                    t_rt = dv(lambda e, fi=fi, e7r=e7r: e.tensor_tensor(out=rt[:, 3, :], in0=fi, in1=e7r, op=ALU.mult))
                    if pieces:
                        pieces.pop(0)()
                    dv(lambda e, pp=pp: e.tensor_tensor(out=ini[:, 0, 2 * pp:2 * pp + 2], in0=rt[:, 0, :], in1=rt[:, 1, :], op=ALU.subtract), waits=[t_rt], sig=None)
                    S["t_rot"] = dv(lambda e, pp=pp: e.tensor_tensor(out=ini[:, 1, 2 * pp:2 * pp + 2], in0=rt[:, 2, :], in1=rt[:, 3, :], op=ALU.add))
                for pc in pieces:
                    pc()
                S["chain_end"][g] = S["t_rot"]

            def own_part(g):
                tb, pp = divmod(g, 16)
                us, ct = tb % 2, pp // 2
                own_i = tb - (NTB - 4)
                ss = pp % 2
                wb = g % 2
                rre, rim = rre2[wb], rim2[wb]
                t_d = None
                for jj in range(2):
                    st_ = 2 * pp + jj
                    trb = Tre[:, st_, :].unsqueeze(1).to_broadcast([128, SUBS, L])
                    tib = Tim[:, st_, :].unsqueeze(1).to_broadcast([128, SUBS, L])
                    rr = rre[:, jj, :].rearrange("p (a b) -> p a b", a=SUBS)
                    ri_ = rim[:, jj, :].rearrange("p (a b) -> p a b", a=SUBS)
                    for i, (src, tab) in enumerate(((rr, trb), (ri_, tib), (rr, tib), (ri_, trb))):
                        t_d = P.op("pool", lambda e, i=i, jj=jj, src=src, tab=tab, ss=ss: e.tensor_tensor(out=dmb[ss][i][:, jj, :].rearrange("p (a b) -> p a b", a=SUBS), in0=src, in1=tab, op=ALU.mult),
                                   waits=[S["t_rot"], S["s_free"][ss]] if (jj == 0 and i == 0) else [], sig="pool")
                S["dm_done"][g] = t_d
                yb = 4 + ct % 2
                tk_y = None
                csel = (0, 2, 1, 1)
                for jj in range(2):
                    st_ = 2 * pp + jj
                    for i in range(4):
                        first = (ss == 0 and jj == 0 and i == 0)
                        lastm = (ss == 1 and jj == 1 and i == 3)
                        tk_y = P.op("pe", lambda e, st_=st_, i=i, jj=jj, ss=ss, yb=yb, first=first, lastm=lastm: e.matmul(PS[yb][:], lhsT=Ct[:, st_, csel[i], :], rhs=dmb[ss][i][:, jj, :], start=first, stop=lastm),
                                    waits=[t_d, t_z, S["ybank_free"].get(yb)] if (jj == 0 and i == 0) else [], sig="pe" if (jj == 1 and i == 3) else None)
                S["s_free"][ss] = tk_y
                S["tk_y"][g] = tk_y

            def own_b(g):
                tb, pp = divmod(g, 16)
                us, ct = tb % 2, pp // 2
                own_i = tb - (NTB - 4)
                ss = pp % 2
                yb = 4 + ct % 2
                tk_y = S["tk_y"][g]
                if ss == 1:
                    t_y = dv(lambda e, us=us, ct=ct, yb=yb: e.scalar_tensor_tensor(out=yv[:], in0=uT[:, us, ct, :], scalar=dd[:, ct:ct + 1], in1=PS[yb][:], op0=ALU.mult, op1=ALU.add), waits=[tk_y, S["yv_free"]])
                    S["ybank_free"][yb] = t_y
                    S["t_y_tb"][tb] = t_y
                    t_i = P.op("pool", lambda e: e.tensor_tensor(out=y2[:], in0=yv[:], in1=yv[:], op=ALU.mult), waits=[t_y], sig="pool")
                    t_i = P.op("pool", lambda e: e.tensor_scalar(out=y2[:], in0=y2[:], scalar1=0.044715, scalar2=1.0, op0=ALU.mult, op1=ALU.add), waits=[t_i], sig="pool")
                    t_i = P.op("pool", lambda e: e.tensor_tensor(out=y2[:], in0=y2[:], in1=yv[:], op=ALU.mult), waits=[t_i], sig="pool")
                    t_g = ac(lambda e: e.activation(out=ysg[:], in_=y2[:], func=AF.Sigmoid, scale=2.0 * 0.7978845608028654), waits=[t_i])
                    ys = S["ny"] % 2
                    S["ny"] += 1
                    t_o = P.op("pool", lambda e, ys=ys: e.tensor_tensor(out=yst[:, ys, :], in0=yv[:], in1=ysg[:], op=ALU.mult), waits=[t_g, S["yst_free"][ys]], sig="pool")
                    S["yv_free"] = t_o
                    S["yst_free"][ys] = P.dma("sp", yscr[ct * 128:(ct + 1) * 128, own_i * TB:(own_i + 1) * TB], yst[:, ys, :], waits=[t_o], sig=f"ys{ys}")

            load_u(0)
            do_bu(0)
            for jj in range(2):
                for half in range(2):
                    mod_piece(0, jj, half)
            do_bu(1)
            for g in range(NP):
                tb, pp = divmod(g, 16)
                if pp == 1 and tb + 1 < NTB:
                    rd = [S["last_bu_tb"].get(tb - 1), S["t_y_tb"].get(tb - 1)]
                    rd = [r for r in rd if r is not None]
                    S["u_free"][(tb + 1) % 2] = rd[-1] if rd else None
                    if len(rd) == 2:
                        P.op("sp", None, waits=[rd[0]])
                    load_u(tb + 1)
                pieces = []
                if g + 1 < NP:
                    pieces = [(lambda jj=jj, half=half, g=g: mod_piece(g + 1, jj, half)) for jj in range(2) for half in range(2)]
                run_chain(g, pieces)
                if g + 2 < NP:
                    do_bu(g + 2)
                if g >= 1 and (g - 1) // 16 >= NTB - 4:
                    own_b(g - 1)
                if tb >= NTB - 4:
                    own_part(g)
            own_b(NP - 1)
            if "sv" in dbg_outs:
                P.barrier()
                P.dma("sp", dbg_outs["sv"], sv[:].rearrange("p a s -> p (a s)"), sig="dbg")
                P.dma("sp", dbg_outs["Tre"], Tre[:].rearrange("p a s -> p (a s)"), sig="dbg")
                P.dma("sp", dbg_outs["Tim"], Tim[:].rearrange("p a s -> p (a s)"), sig="dbg")
                P.dma("sp", dbg_outs["bbo"], rbig[:, 2048:3072], sig="dbg")
                P.dma("sp", dbg_outs["inio"], ini[:].rearrange("p a s -> p (a s)"), sig="dbg")
                P.barrier()
            p2.close()
            if "yssm" in dbg_outs:
                P.barrier()
                P.dma("sp", dbg_outs["yssm"], yscr, sig="dbg")

        _p2()

    if dbg.get("qkv_in"):
        P.dma("sp", qscr, din("q_in", [1024, OWN], BF16), sig="dbg")
        P.dma("sp", kscr, din("k_in", [1024, 2 * OWN], BF16), sig="dbg")
        P.dma("sp", vscr, din("v_in", [2 * OWN, 2048], BF16), sig="dbg")

    if 3 in do:
        def _p3():
            P.barrier()
            p3 = ExitStack()

            def sb3(name, shape, dt):
                return p3.enter_context(nc.sbuf_tensor("s3_" + name, list(shape), dt))
            mtmp = sb3("mtmp", [128, 2, 128], F32)
            qbd = sb3("qbd", [128, 2, 2, OWN], BF16)
            kT = sb3("kT", [128, 2, 2 * OWN], BF16)
            acc = sb3("acc", [128, 4, OWN], F32)
            dtmp = sb3("dtmp", [64, 4, OWN], F32)
            NVS = 12
            vch = sb3("vch", [128, NVS, 512], BF16)
            pT = sb3("pT", [128, 3, 512], BF16)
            pE = sb3("pE", [128, 3, 512], BF16)
            M01 = sb3("M01", [128, 2, 512], BF16)
            hval = sb3("hval", [128, 1], F32)
            yat = sb3("yat", [64, 4, OWN], BF16)

            t0 = P.dma("sp", mtmp[:, 0, :], maskc_d, sig="cst")
            t0 = P.dma("sp", mtmp[:, 1, :], maskp_d, sig="cst")
            t_hv = P.op("dve", lambda e: e.tensor_scalar(out=hval[:], in0=hbias[:, 0:1], scalar1=0.0, scalar2=None, op0=ALU.is_equal), waits=[t0, tk_cst], sig="dve")
            for v_ in range(2):
                for c_ in range(2):
                    P.op("dve", lambda e, v_=v_, c_=c_: e.tensor_scalar(out=M01[:, v_, c_ * 128:(c_ + 1) * 128], in0=mtmp[:, 1, :], scalar1=0.0, scalar2=None, op0=ALU.is_equal), sig="dve")
                    P.op("dve", lambda e, v_=v_, c_=c_: e.tensor_scalar(out=M01[:, v_, 256 + c_ * 128:256 + (c_ + 1) * 128], in0=mtmp[:, 0, :], scalar1=0.0, scalar2=None, op0=ALU.is_equal), sig="dve")
            t_mask = P.op("dve", lambda e: e.tensor_scalar(out=M01[:, 1, 0:256], in0=M01[:, 1, 0:256], scalar1=hval[:, 0:1], scalar2=None, op0=ALU.mult), waits=[t_hv], sig="dve")
            t_qz = P.op("dve", lambda e: e.memset(qbd[:].rearrange("p a b c -> p (a b c)"), 0.0), sig="dve")

            v_free = [None] * NVS
            nv = 0
            ps_s_free = [None, None, None]
            pT_free = [None, None, None]
            pE_free = [None, None, None]
            ps_od_free = [None, None]
            qk_free = t_qz
            acc_free = None
            yat_free = None
            dtmp_free = None
            for hq in range(4):
                t_q = None
                for t in range(2):
                    r0 = (2 * hq + t) * 128
                    P.dma("sp", qbd[0:64, t, 0, :], qscr[r0:r0 + 64, :], waits=[qk_free], sig="qk")
                    P.dma("sp", qbd[64:128, t, 1, :], qscr[r0 + 64:r0 + 128, :], waits=[qk_free], sig="qk")
                    t_q = P.dma("sp", kT[:, t, :], kscr[r0:r0 + 128, :], waits=[qk_free], sig="qk")
                t_z = P.op("dve", lambda e: e.memset(acc[:].rearrange("p a c -> p (a c)"), 0.0), waits=[acc_free], sig="dve")
                chunks = []
                blocks = []
                for d in (1, 4, 16):
                    nb = OWN // (128 * d)
                    for r in range(d):
                        for n in range(-1, nb):
                            chunks.append((OWN + 128 * n * d + r, d))
                            if n >= 0:
                                blocks.append((d, r, n, len(chunks) - 2, len(chunks) - 1))
                chunk_tok = {}
                nextc = {"c": 0}

                def ensure(upto):
                    while nextc["c"] <= min(upto, len(chunks) - 1):
                        c = nextc["c"]
                        a0, d_ = chunks[c]
                        slot = (nv0 + c) % NVS
                        chunk_tok[c] = P.dma("sp", vch[:, slot, :], vscr[a0:a0 + 127 * d_ + 1:d_, hq * 512:(hq + 1) * 512], waits=[v_free[slot]], sig=f"v{slot}")
                        nextc["c"] += 1
                nv0 = nv
                items = []
                for bj, (d, r, n, pc, cc) in enumerate(blocks):
                    for t in range(2):
                        items.append(dict(d=d, r=r, n=n, t=t, bj=bj, pc=pc, cc=cc))
                nv += len(chunks)
                state = {}

                def emit_S(i, it):
                    d, r, n, t = it["d"], it["r"], it["n"], it["t"]
                    b = i % 3
                    o0 = 128 * n * d + r
                    qap = qbd[:, t, :, o0:o0 + 127 * d + 1:d]
                    kcur = kT[:, t, OWN + o0:OWN + o0 + 127 * d + 1:d]
                    kprev = kT[:, t, OWN + o0 - 128 * d:OWN + o0 - d + 1:d]
                    mv = 1 if n == 0 else 0
                    P.op("pe", lambda e, b=b, kprev=kprev, qap=qap: e.matmul(PS[b][:, 0:256].rearrange("p (a c) -> p a c", a=2), lhsT=kprev, rhs=qap, start=True, stop=True), waits=[ps_s_free[b], t_q, tk_setup])
                    tk = P.op("pe", lambda e, b=b, kcur=kcur, qap=qap: e.matmul(PS[b][:, 256:512].rearrange("p (a c) -> p a c", a=2), lhsT=kcur, rhs=qap, start=True, stop=True), sig="pe")
                    t_e = P.op("act", lambda e, b=b: e.activation(out=pE[:, b, :], in_=PS[b][:], func=AF.Exp, scale=0.125), waits=[tk, pE_free[b]], sig="act")
                    ps_s_free[b] = t_e
                    t_p = P.op("pool", lambda e, b=b, mv=mv: e.tensor_tensor(out=pT[:, b, :], in0=pE[:, b, :], in1=M01[:, mv, :], op=ALU.mult), waits=[t_e, pT_free[b], t_mask], sig="pool")
                    pE_free[b] = t_p
                    it["t_e"] = t_p

                def emit_PV(i, it):
                    d, r, n, t = it["d"], it["r"], it["n"], it["t"]
                    b = i % 3
                    ob = 4 + i % 2
                    o0 = 128 * n * d + r
                    sp_, sc_ = (nv0 + it["pc"]) % NVS, (nv0 + it["cc"]) % NVS
                    tk = None
                    for ab in range(2):
                        hh = 2 * t + ab
                        P.op("pe", lambda e, b=b, ob=ob, sp_=sp_, hh=hh, ab=ab: e.matmul(PS[ob][:, ab * 128:(ab + 1) * 128], lhsT=vch[:, sp_, hh * 128:(hh + 1) * 128], rhs=pT[:, b, ab * 128:(ab + 1) * 128], start=True, stop=False),
                             waits=[it["t_e"], chunk_tok[it["pc"]], chunk_tok[it["cc"]], ps_od_free[i % 2]] if ab == 0 else [])
                        tk = P.op("pe", lambda e, b=b, ob=ob, sc_=sc_, hh=hh, ab=ab: e.matmul(PS[ob][:, ab * 128:(ab + 1) * 128], lhsT=vch[:, sc_, hh * 128:(hh + 1) * 128], rhs=pT[:, b, 256 + ab * 128:256 + (ab + 1) * 128], start=False, stop=True),
                                  sig="pe" if ab == 1 else None)
                    pT_free[b] = tk
                    if t == 1:
                        v_free[sp_] = tk
                        v_free[sc_] = tk
                    dst = acc[:, 2 * t:2 * t + 2, o0:o0 + 127 * d + 1:d]
                    t_a = P.op("dve", lambda e, ob=ob, dst=dst: e.tensor_tensor(out=dst, in0=dst, in1=PS[ob][:, 0:256].rearrange("p (a c) -> p a c", a=2), op=ALU.add), waits=[tk, t_z], sig="dve")
                    ps_od_free[i % 2] = t_a
                    state["t_a"] = t_a
                    state["last_pe"] = tk

                for i, it in enumerate(items):
                    if it["t"] == 0:
                        ensure(blocks[min(it["bj"] + 2, len(blocks) - 1)][4])
                    emit_S(i, it)
                    if i >= 2:
                        emit_PV(i - 2, items[i - 2])
                emit_PV(len(items) - 2, items[-2])
                emit_PV(len(items) - 1, items[-1])
                qk_free = state["last_pe"]
                t_dm = P.dma("sp", dtmp[:], acc[64:128, :, :], waits=[state["t_a"], dtmp_free], sig="dtmp")
                P.op("act", lambda e: e.activation(out=dtmp[:], in_=dtmp[:], func=AF.Ln), waits=[t_dm], sig="act")
                t_r = P.op("act", lambda e: e.activation(out=dtmp[:], in_=dtmp[:], func=AF.Exp, scale=-1.0), sig="act")
                t_y = P.op("dve", lambda e: e.tensor_tensor(out=yat[:], in0=acc[0:64, :, :], in1=dtmp[:], op=ALU.mult), waits=[t_r, yat_free], sig="dve")
                acc_free = t_y
                dtmp_free = t_y
                yat_free = P.dma("sp", ascr[hq * 256:(hq + 1) * 256, :].rearrange("(hh e) t -> e hh t", e=64), yat[:], waits=[t_y], sig="yat")
            p3.close()
            if "yatt" in dbg_outs:
                P.barrier()
                P.dma("sp", dbg_outs["yatt"], ascr, sig="dbg")

        _p3()

    if dbg.get("p4_in"):
        P.dma("sp", h1scr, din("h1_in", [D, OWN], F32), sig="dbg")
        P.dma("sp", yscr, din("ys_in", [1024, OWN], BF16), sig="dbg")
        P.dma("sp", ascr, din("ya_in", [1024, OWN], BF16), sig="dbg")

    if 4 in do:
        def _p4():
            P.barrier()
            p4 = ExitStack()

            def sb4(name, shape, dt):
                return p4.enter_context(nc.sbuf_tensor("s4_" + name, list(shape), dt))
            hT = sb4("hT", [128, DT, TB], F32)
            xnT = sb4("xnT", [128, DT, TB], BF16)
            hid = sb4("hid", [128, FT, TB], BF16)
            wbuf = sb4("wbuf", [128, 4, D], BF16)
            wdbuf = sb4("wdbuf", [128, 2, DFF], BF16)
            sg = sb4("sg", [128, 2, TB], F32)
            rtmp = sb4("rtmp", [128, TB], F32)
            rstd = sb4("rstd", [128, TB], F32)
            yT = sb4("yT", [128, 8, TB], BF16)
            aT = sb4("aT", [128, 8, TB], BF16)
            yg = sb4("yg", [128, 8, TB], F32)
            mixT = sb4("mixT", [128, DT, TB], BF16)
            ost = sb4("ost", [128, D], F32)
            sq = hid[:, 0:DT, :]
            W.slot_free = [None] * 4
            Wd.slot_free = [None] * 2
            ffn.sg_free = [None, None]
            ffn.hid_free = None
            psum_free = {}
            y_t = y_d.rearrange("(n p) d -> n p d", p=128)
            in_free = []
            ost_free = None
            mix_readers = None
            ntp = 0
            for i in dbg.get("tbs_p4", list(range(4))):
                sl = slice(i * TB, (i + 1) * TB)
                t_h = P.dma("sp", hT[:], h1scr.rearrange("(dt p) t -> p dt t", p=128)[:, :, sl], waits=in_free, sig="p4h")
                t_ys = P.dma("sp", yT[:], yscr.rearrange("(c p) t -> p c t", p=128)[:, :, sl], waits=in_free, sig="p4y")
                t_ya = P.dma("sp", aT[:], ascr.rearrange("(c p) t -> p c t", p=128)[:, :, sl], waits=in_free, sig="p4a")
                in_free = []
                t_g = None
                for mt in range(8):
                    slot = W.n % 4
                    W.n += 1
                    t_ld = P.dma("sp", wbuf[:, slot, 0:1024], wbf["wglu"][mt], waits=[W.slot_free[slot], conv_tok["B"]], sig=f"w{slot}")
                    bank = 4 + mt % 2
                    tk = None
                    for kt in range(8):
                        tk = P.op("pe", lambda e, kt=kt, slot=slot, bank=bank: e.matmul(PS[bank][:], lhsT=wbuf[:, slot, kt * 128:(kt + 1) * 128], rhs=yT[:, kt, :], start=(kt == 0), stop=(kt == 7)),
                                  waits=[t_ld, psum_free.get(bank), t_ys, tk_setup] if kt == 0 else [], sig="pe" if kt == 7 else None)
                    W.slot_free[slot] = tk
                    t_s = P.op("act", lambda e, mt=mt, bank=bank: e.activation(out=sg[:, mt % 2, :], in_=PS[bank][:], func=AF.Sigmoid, bias=gains8[:, 16 + mt:17 + mt]), waits=[tk, ffn.sg_free[mt % 2]], sig="act")
                    psum_free[bank] = t_s
                    t_g = P.op("dve", lambda e, mt=mt: e.tensor_tensor(out=yg[:, mt, :], in0=yT[:, mt, :], in1=sg[:, mt % 2, :], op=ALU.mult), waits=[t_s, mix_readers], sig="dve")
                    ffn.sg_free[mt % 2] = t_g
                t_m1, t_stat = rms_to_bf16(yg[:], 8, gains8[:, 0:8], mixT[:, 0:8, :], hid[:, 0:8, :], PS[6], rtmp[:], rstd[:], [t_g, ffn.hid_free], 1024, war_dst=[mix_readers])
                psum_free[6] = t_stat
                t_m2, t_stat = rms_to_bf16(aT[:], 8, gains8[:, 8:16], mixT[:, 8:16, :], hid[:, 0:8, :], PS[6], rtmp[:], rstd[:], [t_ya, t_m1], 1024, war_dst=[mix_readers])
                psum_free[6] = t_stat
                t_res = None
                for mt in range(DT):
                    slot = W.n % 4
                    W.n += 1
                    t_ld = P.dma("sp", wbuf[:, slot, :], wbf["wout"][mt], waits=[W.slot_free[slot], conv_tok["B"]], sig=f"w{slot}")
                    bank = 4 + mt % 2
                    tk = None
                    for kt in range(DT):
                        tk = P.op("pe", lambda e, kt=kt, slot=slot, bank=bank: e.matmul(PS[bank][:], lhsT=wbuf[:, slot, kt * 128:(kt + 1) * 128], rhs=mixT[:, kt, :], start=(kt == 0), stop=(kt == DT - 1)),
                                  waits=[t_ld, psum_free.get(bank), t_m1, t_m2] if kt == 0 else [], sig="pe" if kt == DT - 1 else None)
                    W.slot_free[slot] = tk
                    t_res = P.op("dve", lambda e, mt=mt, bank=bank: e.tensor_tensor(out=hT[:, mt, :], in0=PS[bank][:], in1=hT[:, mt, :], op=ALU.add), waits=[tk, t_h], sig="dve")
                    psum_free[bank] = t_res
                    mix_readers = tk
                in_free.append(mix_readers)
                t_xn, t_stat = rms_to_bf16(hT[:], DT, gains[:, 2 * DT:3 * DT], xnT, sq, PS[6], rtmp[:], rstd[:], [t_res, ffn.hid_free], D, war_dst=[])
                psum_free[6] = t_stat
                t_res2, t_pe_last = ffn(hT, xnT, hid, wbuf, wdbuf, sg, wbf["wg2"], wbf["wu2"], wbf["wd2"], conv_tok["B"], psum_free, t_xn)
                t_fin, t_stat = rms_to_bf16(hT[:], DT, gains[:, 3 * DT:4 * DT], hT, sq, PS[6], rtmp[:], rstd[:], [t_res2, ffn.hid_free], D, war_dst=[])
                psum_free[6] = t_stat
                tk = None
                for s_ in range(4):
                    t_ev = None
                    for dq in range(4):
                        bank = 6 + (ntp % 2)
                        ntp += 1
                        for j in range(4):
                            dt = dq * 4 + j
                            tk = P.op("pe", lambda e, dt=dt, j=j, s_=s_, bank=bank: e.transpose(out=PS[bank][:, j * 128:(j + 1) * 128], in_=hT[:, dt, s_ * 128:(s_ + 1) * 128], identity=ident[:]),
                                      waits=[t_fin, psum_free.get(bank)] if j == 0 else [], sig="pe" if j == 3 else None)
                        t_ev = P.op("act", lambda e, dq=dq, bank=bank: e.activation(out=ost[:, dq * 512:(dq + 1) * 512], in_=PS[bank][:], func=AF.Copy), waits=[tk, ost_free if dq == 0 else None], sig="act")
                        psum_free[bank] = t_ev
                    ost_free = P.dma("sp", y_t[i * 4 + s_], ost[:], waits=[t_ev], sig="yout")
                in_free.append(tk)
            p4.close()

        _p4()

    fin = [t for t in phase_end if t is not None]
    for s, (h, cnt) in list(P.sems.items()):
        if cnt > 0:
            fin.append((s, cnt))
    P.op("pool", lambda e: e.memset(epsc[:, 0:1], EPS), waits=fin)
    P.emit()
    st.close()
    return nc


def _tile_w(w, kt, mt):
    return np.ascontiguousarray(w.reshape(kt, 128, mt, 128).transpose(2, 1, 0, 3).reshape(mt, 128, kt * 128))


def _prep_shared(inp):
    f = lambda a: np.asarray(a, dtype=np.float32)
    sh = {}
    sh["wg1"] = _tile_w(f(inp["ffn1_w_gate"])[0], DT, FT)
    sh["wu1"] = _tile_w(f(inp["ffn1_w_up"])[0], DT, FT)
    sh["wd1"] = _tile_w(f(inp["ffn1_w_down"])[0], FT, DT)
    sh["wg2"] = _tile_w(f(inp["ffn2_w_gate"])[0], DT, FT)
    sh["wu2"] = _tile_w(f(inp["ffn2_w_up"])[0], DT, FT)
    sh["wd2"] = _tile_w(f(inp["ffn2_w_down"])[0], FT, DT)
    sh["win"] = _tile_w(f(inp["w_in"])[0], DT, 32)
    sh["wout"] = _tile_w(f(inp["w_out"])[0], DT, DT)
    sh["wglu"] = _tile_w(f(inp["ssm_w_glu"])[0], 8, 8)
    g16 = lambda v: f(v).reshape(DT, 128).T
    g8 = lambda v: f(v).reshape(8, 128).T
    sh["gains"] = np.ascontiguousarray(np.concatenate([g16(inp["ffn1_norm"][0]), g16(inp["mix_norm"][0]), g16(inp["ffn2_norm"][0]), g16(inp["final_norm"])], axis=1))
    sh["gains8"] = np.ascontiguousarray(np.concatenate([g8(inp["ssm_out_norm"][0]), g8(inp["attn_out_norm"][0]), g8(inp["ssm_b_glu"][0])], axis=1))
    def st_l(a):
        a = f(a)
        r = a.reshape(32, 2, 64, *a.shape[2:])
        r = np.moveaxis(r, 0, 2)
        return np.ascontiguousarray(r.reshape(128, 32, *a.shape[2:]))
    ldt = np.broadcast_to(f(inp["ssm_log_dt"])[0][:, None], (64, 64))
    sh["ssm_sc"] = np.ascontiguousarray(np.concatenate([st_l(ldt), st_l(inp["ssm_a_re"][0]), st_l(inp["ssm_a_im"][0])], axis=1))
    sh["ssm_b"] = np.ascontiguousarray(np.concatenate([st_l(inp["ssm_b_re"][0]).reshape(128, -1), st_l(inp["ssm_b_im"][0]).reshape(128, -1)], axis=1))
    cre = np.transpose(f(inp["ssm_c_re"])[0], (0, 2, 1))
    cim = np.transpose(f(inp["ssm_c_im"])[0], (0, 2, 1))
    sh["ssm_c"] = np.ascontiguousarray(np.concatenate([st_l(cre).reshape(128, -1), st_l(cim).reshape(128, -1)], axis=1))
    sh["ssm_dd"] = np.ascontiguousarray(f(inp["ssm_d"])[0].reshape(8, 128).T)
    sh["ident"] = np.eye(128, dtype=np.float32)
    k = np.arange(128)[:, None]
    q = np.arange(128)[None, :]
    sh["maskc"] = np.where(k <= q, 0.0, NEG).astype(np.float32)
    sh["maskp"] = np.where(k >= q, 0.0, NEG).astype(np.float32)
    return sh


def _prep_core(x, c):
    b, q = divmod(c, 4)
    xs = np.zeros((SEQ, D), np.float32)
    n = (q + 1) * OWN
    xs[SEQ - n:] = x[b, :n]
    hb = np.full((128, 1), 0.0 if q >= 1 else NEG, np.float32)
    return {"xs": xs, "hbias": hb}


def kernel(**inputs):
    x = np.asarray(inputs["x"], dtype=np.float32)
    nc = build_nc()
    sh = _prep_shared(inputs)
    in_maps = []
    for c in range(NCORES):
        m = dict(sh)
        m.update(_prep_core(x, c))
        in_maps.append(m)
    res = run_bass_kernel_spmd(nc, in_maps, core_ids=list(range(NCORES)))
    out = np.empty((BATCH, SEQ, D), np.float32)
    for c in range(NCORES):
        b, q = divmod(c, 4)
        out[b, q * OWN:(q + 1) * OWN] = np.asarray(res.results[c]["y"])
    return out
```

```python
from contextlib import ExitStack
import math
import numpy as np
import concourse.bass as bass
import concourse.mybir as mybir
from concourse.bass_utils import run_bass_kernel_spmd

F32 = mybir.dt.float32
BF16 = mybir.dt.bfloat16
I32 = mybir.dt.int32
ALU = mybir.AluOpType
AF = mybir.ActivationFunctionType

NCORES = 8
D = 2048
DT = 16
SEQ = 8192
BATCH = 2
OWN = 2048
TB = 512
NTB = SEQ // TB
DFF = 5504
FT = DFF // 128
EPS = 1e-6
NEG = -30000.0
L = 256
SUBS = TB // L
NST = 32


class Prog:
    ENGS = ("sp", "act", "pool", "pe", "dve")

    def __init__(self, nc, stack):
        self.nc = nc
        self.stack = stack
        self.ops = {e: [] for e in self.ENGS}
        self.sems = {}

    def _sem(self, name):
        if name not in self.sems:
            self.sems[name] = [self.stack.enter_context(self.nc.semaphore(name)), 0]
        return self.sems[name]

    def op(self, eng, fn, waits=(), sig=None, inc=1):
        tok = None
        if sig is not None:
            s = self._sem(sig)
            s[1] += inc
            tok = (sig, s[1])
        ws = tuple(w for w in waits if w is not None)
        self.ops[eng].append((fn, ws, sig, inc))
        return tok

    def dma(self, eng, out, in_, waits=(), sig=None):
        return self.op(eng, lambda e, o=out, i=in_: e.dma_start(out=o, in_=i), waits, sig, 16)

    def barrier(self):
        toks = [(s, c) for s, (h, c) in self.sems.items() if c > 0]
        for e in self.ENGS:
            self.op(e, None, waits=toks)
        return toks

    def emit(self):
        sems = self.sems
        ops = self.ops

        def run(e, lst):
            seen = {}
            for fn, ws, sig, inc in lst:
                for (s, v) in ws:
                    if seen.get(s, 0) < v:
                        e.wait_ge(sems[s][0], v)
                        seen[s] = v
                if fn is None:
                    continue
                ins = fn(e)
                if sig is not None:
                    ins.then_inc(sems[sig][0], inc)

        with self.nc.Block() as block:
            @block.sync
            def _(e):
                run(e, ops["sp"])

            @block.scalar
            def _(e):
                run(e, ops["act"])

            @block.gpsimd
            def _(e):
                run(e, ops["pool"])

            @block.tensor
            def _(e):
                run(e, ops["pe"])

            @block.vector
            def _(e):
                run(e, ops["dve"])


def build_nc(dbg=None):
    dbg = dbg or {}
    tbs_p1 = dbg.get("tbs_p1", list(range(NTB)))
    do = dbg.get("phases", (0, 1, 2, 3, 4))
    nc = bass.Bass("TRN2", target_bir_lowering=False)
    st = ExitStack()
    P = Prog(nc, st)

    def din(name, shape, dt=F32):
        return nc.dram_tensor(name, list(shape), dt, kind="ExternalInput").ap()

    def dscr(name, shape, dt):
        return nc.dram_tensor(name, list(shape), dt).ap()

    def sb(name, shape, dt):
        return st.enter_context(nc.sbuf_tensor("s_" + name, list(shape), dt))

    xs_d = din("xs", [SEQ, D])
    ident_d = din("ident", [128, 128])
    maskc_d = din("maskc", [128, 128])
    maskp_d = din("maskp", [128, 128])
    hbias_d = din("hbias", [128, 1])
    gains_d = din("gains", [128, 4 * DT])
    gains8_d = din("gains8", [128, 3 * 8])
    wfp = {
        "wg1": din("wg1", [FT, 128, D]), "wu1": din("wu1", [FT, 128, D]), "wd1": din("wd1", [DT, 128, DFF]),
        "win": din("win", [32, 128, D]), "wout": din("wout", [DT, 128, D]), "wglu": din("wglu", [8, 128, 1024]),
        "wg2": din("wg2", [FT, 128, D]), "wu2": din("wu2", [FT, 128, D]), "wd2": din("wd2", [DT, 128, DFF]),
    }
    ssm_sc_d = din("ssm_sc", [128, 3 * NST])
    ssm_b_d = din("ssm_b", [128, 2 * NST * 16])
    ssm_c_d = din("ssm_c", [128, 2 * NST * 16])
    ssm_dd_d = din("ssm_dd", [128, 8])
    y_d = nc.dram_tensor("y", [OWN, D], F32, kind="ExternalOutput").ap()

    wbf = {k: dscr(k + "b", v.shape, BF16) for k, v in wfp.items()}
    h1scr = dscr("h1scr", [D, OWN], F32)
    qscr = dscr("qscr", [1024, OWN], BF16)
    kscr = dscr("kscr", [1024, 2 * OWN], BF16)
    vscr = dscr("vscr", [2 * OWN, 2048], BF16)
    uscr = dscr("uscr", [1024, SEQ], BF16)
    yscr = dscr("yscr", [1024, OWN], BF16)
    ascr = dscr("ascr", [1024, OWN], BF16)

    dbg_outs = {}
    if dbg.get("outs"):
        for name, shape, dt in dbg["outs"]:
            dbg_outs[name] = nc.dram_tensor(name, list(shape), dt, kind="ExternalOutput").ap()

    ident = sb("ident", [128, 128], F32)
    identb = sb("identb", [128, 128], BF16)
    onesb = sb("onesb", [128, 128], BF16)
    gains = sb("gains", [128, 4 * DT], F32)
    gains8 = sb("gains8", [128, 24], F32)
    epsc = sb("epsc", [128, 1], F32)
    hbias = sb("hbias", [128, 1], F32)

    PS = [st.enter_context(nc.psum_tensor(f"ps{i}", [128, 512], F32)) for i in range(8)]

    t_c = []
    t_c.append(P.dma("sp", ident[:], ident_d, sig="cst"))
    t_c.append(P.dma("sp", gains[:], gains_d, sig="cst"))
    t_c.append(P.dma("sp", gains8[:], gains8_d, sig="cst"))
    t_c.append(P.dma("sp", hbias[:], hbias_d, sig="cst"))
    tk_cst = t_c[-1]
    P.op("dve", lambda e: e.memset(onesb[:], 1.0))
    P.op("dve", lambda e: e.memset(epsc[:], EPS))
    tk_setup = P.op("dve", lambda e: e.tensor_copy(out=identb[:], in_=ident[:]), waits=[tk_cst], sig="dve")

    conv_tok = {}
    gu_tok = [None] * FT
    NG = 4
    per = (FT + NG - 1) // NG
    for gi in range(NG):
        tok = None
        for fc in range(gi * per, min(FT, (gi + 1) * per)):
            P.dma("pool", wbf["wg1"][fc], wfp["wg1"][fc], sig=f"cvG{gi}")
            tok = P.dma("pool", wbf["wu1"][fc], wfp["wu1"][fc], sig=f"cvG{gi}")
        for fc in range(gi * per, min(FT, (gi + 1) * per)):
            gu_tok[fc] = tok
    for nm in ("wd1", "win"):
        tok = None
        for i in range(wfp[nm].shape[0]):
            tok = P.dma("pool", wbf[nm][i], wfp[nm][i], sig="cv_" + nm)
        conv_tok[nm] = tok
    conv_tok["A"] = conv_tok["win"]
    conv_A = {"gu": gu_tok, "d": conv_tok["wd1"]}
    tok = None
    for nm in ("wglu", "wout", "wg2", "wu2", "wd2"):
        for i in range(wfp[nm].shape[0]):
            tok = P.dma("pool", wbf[nm][i], wfp[nm][i], sig="cvB")
    conv_tok["B"] = tok

    class Ctx:
        pass

    def rms_to_bf16(srcT, ntile, gain_ap, dstT, sq, pst, rtmp, rstd, wait_src, dim, war_dst=()):
        t_sqs = []
        for g4 in range(ntile // 4):
            t_sqs.append(P.op("act", lambda e, g4=g4: e.activation(out=sq[:, 4 * g4:4 * g4 + 4, :], in_=srcT[:, 4 * g4:4 * g4 + 4, :], func=AF.Square), waits=list(wait_src) if g4 == 0 else [], sig="act"))
        tk = None
        for dt in range(ntile):
            tk = P.op("pe", lambda e, dt=dt: e.matmul(pst[:], lhsT=onesb[:], rhs=sq[:, dt, :], start=(dt == 0), stop=(dt == ntile - 1)),
                      waits=[t_sqs[dt // 4], tk_setup] if dt % 4 == 0 else [], sig="pe" if dt == ntile - 1 else None)
        t_sqrt = P.op("act", lambda e: e.activation(out=rtmp, in_=pst[:], func=AF.Sqrt, bias=epsc[:, 0:1], scale=1.0 / dim), waits=[tk], sig="act")
        t_r = P.op("dve", lambda e: e.reciprocal(out=rstd, in_=rtmp), waits=[t_sqrt] + list(war_dst), sig="dve")
        tk2 = None
        toks = []
        for dt in range(ntile):
            tk2 = P.op("dve", lambda e, dt=dt: e.scalar_tensor_tensor(out=dstT[:, dt, :], in0=srcT[:, dt, :], scalar=gain_ap[:, dt:dt + 1], in1=rstd, op0=ALU.mult, op1=ALU.mult),
                       waits=[t_r] if dt == 0 else [], sig="dve" if (dt % 4 == 3 or dt == ntile - 1) else None)
            toks.append(tk2)
        for dt in range(ntile - 2, -1, -1):
            if toks[dt] is None:
                toks[dt] = toks[dt + 1]
        rms_to_bf16.last_toks = toks
        return tk2, t_sqrt

    W = Ctx()
    W.slot_free = [None] * 4
    W.n = 0
    Wd = Ctx()
    Wd.slot_free = [None] * 2
    Wd.n = 0

    def ffn(hT, xnT, hid, wbuf, wdbuf, sg, wg, wu, wd, conv_wait, psum_free, t_xn):
        hid_tok = [None] * FT
        for fc in range(FT):
            toks = []
            for k, wsrc in enumerate((wg, wu)):
                slot = W.n % 4
                W.n += 1
                cw = conv_wait["gu"][fc] if isinstance(conv_wait, dict) else conv_wait
                t_ld = P.dma("sp", wbuf[:, slot, :], wsrc[fc], waits=[W.slot_free[slot], cw], sig=f"w{slot}")
                bank = (0 if k == 0 else 2) + (fc % 2)
                tk = None
                for dt in range(DT):
                    xw = t_xn[dt] if isinstance(t_xn, list) else (t_xn if dt == 0 else None)
                    tk = P.op("pe", lambda e, dt=dt, slot=slot, bank=bank: e.matmul(PS[bank][:], lhsT=wbuf[:, slot, dt * 128:(dt + 1) * 128], rhs=xnT[:, dt, :], start=(dt == 0), stop=(dt == DT - 1)),
                              waits=([t_ld, psum_free.get(bank)] if dt == 0 else []) + [xw], sig="pe" if dt == DT - 1 else None)
                W.slot_free[slot] = tk
                toks.append(tk)
            bg, bu = fc % 2, 2 + fc % 2
            t_s = P.op("act", lambda e, bg=bg, fc=fc: e.activation(out=sg[:, fc % 2, :], in_=PS[bg][:], func=AF.Silu), waits=[toks[0], ffn.sg_free[fc % 2]], sig="act")
            psum_free[bg] = t_s
            t_h = P.op("dve", lambda e, bu=bu, fc=fc: e.tensor_tensor(out=hid[:, fc, :], in0=sg[:, fc % 2, :], in1=PS[bu][:], op=ALU.mult), waits=[t_s, toks[1], ffn.hid_free], sig="dve")
            psum_free[bu] = t_h
            ffn.sg_free[fc % 2] = t_h
            hid_tok[fc] = t_h
        t_last = None
        for mt in range(DT):
            slot = Wd.n % 2
            Wd.n += 1
            cw = conv_wait["d"] if isinstance(conv_wait, dict) else conv_wait
            t_ld = P.dma("sp", wdbuf[:, slot, :], wd[mt], waits=[Wd.slot_free[slot], cw], sig=f"wd{slot}")
            bank = 4 + mt % 2
            tk = None
            for ft in range(FT):
                tk = P.op("pe", lambda e, ft=ft, slot=slot, bank=bank: e.matmul(PS[bank][:], lhsT=wdbuf[:, slot, ft * 128:(ft + 1) * 128], rhs=hid[:, ft, :], start=(ft == 0), stop=(ft == FT - 1)),
                          waits=[t_ld, psum_free.get(bank), hid_tok[FT - 1]] if ft == 0 else [], sig="pe" if ft == FT - 1 else None)
            Wd.slot_free[slot] = tk
            t_last = P.op("dve", lambda e, mt=mt, bank=bank: e.scalar_tensor_tensor(out=hT[:, mt, :], in0=PS[bank][:], scalar=0.5, in1=hT[:, mt, :], op0=ALU.mult, op1=ALU.add), waits=[tk], sig="dve")
            psum_free[bank] = t_last
            t_pe_last = tk
        ffn.hid_free = t_pe_last
        return t_last, t_pe_last

    ffn.sg_free = [None, None]
    ffn.hid_free = None

    psum_free = {}
    phase_end = []

    if 1 in do:
        p1 = ExitStack()

        def sb1(name, shape, dt):
            return p1.enter_context(nc.sbuf_tensor("s1_" + name, list(shape), dt))
        hT = sb1("hT", [128, DT, TB], F32)
        xnT = sb1("xnT", [128, DT, TB], BF16)
        hid = sb1("hid", [128, FT, TB], BF16)
        xst = sb1("xst", [128, 2, D], F32)
        wbuf = sb1("wbuf", [128, 4, D], BF16)
        wdbuf = sb1("wdbuf", [128, 2, DFF], BF16)
        sg = sb1("sg", [128, 2, TB], F32)
        rtmp = sb1("rtmp", [128, TB], F32)
        rstd = sb1("rstd", [128, TB], F32)
        ostg = sb1("ostg", [128, 2, TB], BF16)
        vstg = sb1("vstg", [128, 4, 16, 128], BF16)
        sq = hid[:, 0:DT, :]

        xs_t = xs_d.rearrange("(n p) d -> n p d", p=128)
        xst_free = [None, None]
        nx = 0
        t_h_prev_readers = []
        t_xn_readers = None
        ostg_free = [None, None]
        no = 0
        vstg_free = P.op("dve", lambda e: e.memset(vstg[:].rearrange("p a h c -> p (a h c)"), 1.0), sig="dve")
        tpbank = [6, 7]
        ntp = 0
        for tb in tbs_p1:
            t_evs = []
            for s in range(4):
                slot = nx % 2
                nx += 1
                t_ld = P.dma("sp", xst[:, slot, :], xs_t[tb * 4 + s], waits=[xst_free[slot]], sig=f"x{slot}")
                tk = None
                for dq in range(4):
                    bank = tpbank[ntp % 2]
                    ntp += 1
                    for j in range(4):
                        dt = dq * 4 + j
                        tk = P.op("pe", lambda e, dt=dt, j=j, slot=slot, bank=bank: e.transpose(out=PS[bank][:, j * 128:(j + 1) * 128], in_=xst[:, slot, dt * 128:(dt + 1) * 128], identity=ident[:]),
                                  waits=[t_ld, psum_free.get(bank), tk_setup] if j == 0 else [], sig="pe" if j == 3 else None)
                    t_ev = P.op("act", lambda e, dq=dq, s=s, bank=bank: e.activation(out=hT[:, dq * 4:dq * 4 + 4, s * 128:(s + 1) * 128], in_=PS[bank][:].rearrange("p (a b) -> p a b", a=4), func=AF.Copy),
                                waits=[tk] + t_h_prev_readers, sig="act")
                    psum_free[bank] = t_ev
                    t_evs.append(t_ev)
                xst_free[slot] = tk
            t_h_prev_readers = []
            t_xn, t_stat = rms_to_bf16(hT[:], DT, gains[:, 0:DT], xnT, sq, PS[6], rtmp[:], rstd[:], [t_evs[-1], ffn.hid_free], D, war_dst=[t_xn_readers])
            psum_free[6] = t_stat
            t_res, t_pe_last = ffn(hT, xnT, hid, wbuf, wdbuf, sg, wbf["wg1"], wbf["wu1"], wbf["wd1"], conv_A, psum_free, list(rms_to_bf16.last_toks))
            own_i = tb - (NTB - 4)
            halo_i = tb - (NTB - 8)
            if own_i >= 0:
                t_st = P.dma("pool", h1scr.rearrange("(dt p) t -> p dt t", p=128)[:, :, own_i * TB:(own_i + 1) * TB], hT[:], waits=[t_res], sig="h1st")
                t_h_prev_readers.append(t_st)
            t_hn, t_stat = rms_to_bf16(hT[:], DT, gains[:, DT:2 * DT], xnT, sq, PS[6], rtmp[:], rstd[:], [t_res, ffn.hid_free], D, war_dst=[])
            psum_free[6] = t_stat
            t_h_prev_readers.append(t_hn)
            tiles = list(range(24, 32))
            if halo_i >= 0:
                tiles = list(range(8, 24)) + tiles
            if own_i >= 0:
                tiles = list(range(0, 8)) + tiles
            for ct in tiles:
                slot = W.n % 4
                W.n += 1
                t_ld = P.dma("sp", wbuf[:, slot, :], wbf["win"][ct], waits=[W.slot_free[slot], conv_tok["A"]], sig=f"w{slot}")
                bank = 4 + (ct % 2)
                if 16 <= ct < 24:
                    tk = None
                    for s in range(4):
                        for dt in range(DT):
                            first = (s == 0 and dt == 0)
                            last = (s == 3 and dt == DT - 1)
                            tk = P.op("pe", lambda e, s=s, dt=dt, slot=slot, bank=bank: e.matmul(PS[bank][:, s * 128:(s + 1) * 128], lhsT=xnT[:, dt, s * 128:(s + 1) * 128], rhs=wbuf[:, slot, dt * 128:(dt + 1) * 128], start=(dt == 0), stop=(dt == DT - 1)),
                                      waits=[t_ld, psum_free.get(bank), t_hn] if first else [], sig="pe" if last else None)
                    W.slot_free[slot] = tk
                    t_ev = P.op("act", lambda e, ct=ct, bank=bank: e.activation(out=vstg[:, :, 2 * (ct - 16):2 * (ct - 16) + 2, 0:64], in_=PS[bank][:].rearrange("p (a h c) -> p a h c", a=4, h=2), func=AF.Copy),
                                waits=[tk, vstg_free if ct == 16 else None], sig="act")
                    psum_free[bank] = t_ev
                    if ct == 23:
                        vstg_free = P.dma("pool", vscr[halo_i * TB:(halo_i + 1) * TB, :].rearrange("(s p) c -> p s c", p=128), vstg[:].rearrange("p a h c -> p a (h c)"), waits=[t_ev], sig="vst")
                else:
                    tk = None
                    for dt in range(DT):
                        tk = P.op("pe", lambda e, dt=dt, slot=slot, bank=bank: e.matmul(PS[bank][:], lhsT=wbuf[:, slot, dt * 128:(dt + 1) * 128], rhs=xnT[:, dt, :], start=(dt == 0), stop=(dt == DT - 1)),
                                  waits=[t_ld, psum_free.get(bank), t_hn] if dt == 0 else [], sig="pe" if dt == DT - 1 else None)
                    W.slot_free[slot] = tk
                    os_ = no % 2
                    no += 1
                    t_ev = P.op("act", lambda e, os_=os_, bank=bank: e.activation(out=ostg[:, os_, :], in_=PS[bank][:], func=AF.Copy), waits=[tk, ostg_free[os_]], sig="act")
                    psum_free[bank] = t_ev
                    if ct < 8:
                        dst = qscr[ct * 128:(ct + 1) * 128, own_i * TB:(own_i + 1) * TB]
                    elif ct < 16:
                        dst = kscr[(ct - 8) * 128:(ct - 7) * 128, halo_i * TB:(halo_i + 1) * TB]
                    else:
                        dst = uscr[(ct - 24) * 128:(ct - 23) * 128, tb * TB:(tb + 1) * TB]
                    ostg_free[os_] = P.dma("pool", dst, ostg[:, os_, :], waits=[t_ev], sig=f"os{os_}")
                t_xn_readers = tk
        phase_end = [t for t in (ostg_free + [vstg_free, t_xn_readers] + t_h_prev_readers) if t is not None]
        p1.close()

    if "h1" in dbg_outs:
        tk = P.dma("sp", dbg_outs["h1"], h1scr, waits=phase_end, sig="dbg")
        phase_end.append(tk)
    if "u" in dbg_outs:
        phase_end.append(P.dma("sp", dbg_outs["u"], uscr, waits=phase_end, sig="dbg"))
        phase_end.append(P.dma("sp", dbg_outs["q"], qscr, waits=phase_end, sig="dbg"))
        phase_end.append(P.dma("sp", dbg_outs["k"], kscr, waits=phase_end, sig="dbg"))
        phase_end.append(P.dma("sp", dbg_outs["v"], vscr, waits=phase_end, sig="dbg"))


    if dbg.get("u_in"):
        u_in = din("u_in", [1024, SEQ], BF16)
        P.dma("sp", uscr, u_in, sig="dbg")

    if 2 in do:
        def _p2():
            P.barrier()
            p2 = ExitStack()

            def sb2(name, shape, dt):
                return p2.enter_context(nc.sbuf_tensor("s2_" + name, list(shape), dt))
            sc = sb2("sc", [128, 3 * NST], F32)
            rbig = sb2("rbig", [128, 4 * 2 * TB], F32)
            bl = rbig[:, 0:1024].rearrange("p (a s h) -> p a s h", a=2, h=16)
            cl = rbig[:, 1024:2048].rearrange("p (a s h) -> p a s h", a=2, h=16)
            dd = sb2("dd", [128, 8], F32)
            bb = rbig[:, 2048:3072].rearrange("p (a s h) -> p a s h", a=2, h=16)
            Zf = sb2("Zf", [128, NST, 2, 128], F32)
            Bt = sb2("Bt", [128, NST, 2, 128], BF16)
            Ct = sb2("Ct", [128, NST, 3, 128], BF16)
            Tre = sb2("Tre", [128, NST, L], F32)
            Tim = sb2("Tim", [128, NST, L], F32)
            sv = sb2("sv", [128, 24, NST], F32)
            ki = sb2("ki", [128, NST], I32)
            uT = sb2("uT", [128, 2, 8, TB], BF16)
            wre = sb2("wre", [128, 2, TB], F32)
            wim = sb2("wim", [128, 2, TB], F32)
            yv = sb2("yv", [128, TB], F32)
            y2 = sb2("y2", [128, TB], F32)
            ysg = sb2("ysg", [128, TB], F32)
            yst = sb2("yst", [128, 2, TB], BF16)
            ini = sb2("ini", [128, 2, NST], F32)
            rt = sb2("rt", [128, 4, 2], F32)

            V = lambda i: sv[:, i, :]
            LDT, LR, LI, DTv, MAG, TH, TQ, KF, THR, SIN, COS, MSK, THC, LBR, LBI, DEN, NR, FRE, FIM, ERE, EIM, TA, TB_, E7R = range(24)
            E7I = TQ
            t0 = P.dma("sp", sc[:], ssm_sc_d, sig="cst")
            P.dma("sp", rbig[:, 0:1024], ssm_b_d, sig="cst")
            P.dma("sp", rbig[:, 1024:2048], ssm_c_d, sig="cst")
            t_in = P.dma("sp", dd[:], ssm_dd_d, sig="cst")
            lr, li, ldt = sc[:, NST:2 * NST], sc[:, 2 * NST:3 * NST], sc[:, 0:NST]

            chain = {"on": True, "last": None}

            def dv(fn, waits=(), sig="dve"):
                ws = list(waits)
                if chain["on"]:
                    ws.append(chain["last"])
                    sig = "dve"
                tok = P.op("dve", fn, waits=ws, sig=sig)
                if chain["on"]:
                    chain["last"] = tok
                return tok

            def ac(fn, waits=(), sig="act"):
                return P.op("act", fn, waits=waits, sig=sig)
            t = ac(lambda e: e.activation(out=V(DTv), in_=ldt, func=AF.Exp), waits=[t_in])
            t = dv(lambda e: e.tensor_tensor(out=V(TA), in0=lr, in1=V(DTv), op=ALU.mult), waits=[t])
            t_mag = ac(lambda e: e.activation(out=V(MAG), in_=V(TA), func=AF.Exp), waits=[t])
            dv(lambda e: e.tensor_tensor(out=V(TH), in0=li, in1=V(DTv), op=ALU.mult))
            dv(lambda e: e.tensor_scalar(out=V(TQ), in0=V(TH), scalar1=1.0 / (2 * math.pi), scalar2=None, op0=ALU.mult))
            dv(lambda e: e.tensor_copy(out=ki[:], in_=V(TQ)))
            dv(lambda e: e.tensor_copy(out=V(KF), in_=ki[:]))
            dv(lambda e: e.scalar_tensor_tensor(out=V(THR), in0=V(KF), scalar=-6.28125, in1=V(TH), op0=ALU.mult, op1=ALU.add))
            dv(lambda e: e.scalar_tensor_tensor(out=V(THR), in0=V(KF), scalar=-(2 * math.pi - 6.28125), in1=V(THR), op0=ALU.mult, op1=ALU.add))
            dv(lambda e: e.tensor_scalar(out=V(THR), in0=V(THR), scalar1=3.1415925, scalar2=-3.1415925, op0=ALU.min, op1=ALU.max))
            dv(lambda e: e.tensor_scalar(out=V(MSK), in0=V(THR), scalar1=math.pi / 2, scalar2=None, op0=ALU.is_gt))
            dv(lambda e: e.scalar_tensor_tensor(out=V(THC), in0=V(MSK), scalar=-2 * math.pi, in1=V(THR), op0=ALU.mult, op1=ALU.add))
            t = dv(lambda e: e.tensor_scalar(out=V(THC), in0=V(THC), scalar1=math.pi / 2, scalar2=3.1415925, op0=ALU.add, op1=ALU.min))
            ac(lambda e: e.activation(out=V(SIN), in_=V(THR), func=AF.Sin), waits=[t])
            t = ac(lambda e: e.activation(out=V(COS), in_=V(THC), func=AF.Sin))
            dv(lambda e: e.tensor_tensor(out=V(LBR), in0=V(MAG), in1=V(COS), op=ALU.mult), waits=[t, t_mag])
            dv(lambda e: e.tensor_tensor(out=V(LBI), in0=V(MAG), in1=V(SIN), op=ALU.mult))
            dv(lambda e: e.tensor_tensor(out=V(DEN), in0=lr, in1=lr, op=ALU.mult))
            dv(lambda e: e.tensor_tensor(out=V(TA), in0=li, in1=li, op=ALU.mult))
            dv(lambda e: e.tensor_tensor(out=V(DEN), in0=V(DEN), in1=V(TA), op=ALU.add))
            dv(lambda e: e.reciprocal(out=V(DEN), in_=V(DEN)))
            dv(lambda e: e.tensor_scalar(out=V(NR), in0=V(LBR), scalar1=-1.0, scalar2=None, op0=ALU.add))
            dv(lambda e: e.tensor_tensor(out=V(TA), in0=V(NR), in1=lr, op=ALU.mult))
            dv(lambda e: e.tensor_tensor(out=V(TB_), in0=V(LBI), in1=li, op=ALU.mult))
            dv(lambda e: e.tensor_tensor(out=V(TA), in0=V(TA), in1=V(TB_), op=ALU.add))
            dv(lambda e: e.tensor_tensor(out=V(FRE), in0=V(TA), in1=V(DEN), op=ALU.mult))
            dv(lambda e: e.tensor_tensor(out=V(TA), in0=V(LBI), in1=lr, op=ALU.mult))
            dv(lambda e: e.tensor_tensor(out=V(TB_), in0=V(NR), in1=li, op=ALU.mult))
            dv(lambda e: e.tensor_tensor(out=V(TA), in0=V(TA), in1=V(TB_), op=ALU.subtract))
            dv(lambda e: e.tensor_tensor(out=V(FIM), in0=V(TA), in1=V(DEN), op=ALU.mult))
            fre_b = V(FRE).unsqueeze(2).to_broadcast([128, NST, 16])
            fim_b = V(FIM).unsqueeze(2).to_broadcast([128, NST, 16])
            tmp16a = Zf[:, 0:4, :, :].rearrange("p a b c -> p (a b c)")[:, 0:NST * 16].rearrange("p (s h) -> p s h", h=16)
            tmp16b = Zf[:, 4:8, :, :].rearrange("p a b c -> p (a b c)")[:, 0:NST * 16].rearrange("p (s h) -> p s h", h=16)
            dv(lambda e: e.tensor_tensor(out=tmp16a, in0=bl[:, 0], in1=fre_b, op=ALU.mult))
            dv(lambda e: e.tensor_tensor(out=tmp16b, in0=bl[:, 1], in1=fim_b, op=ALU.mult))
            dv(lambda e: e.tensor_tensor(out=bb[:, 0], in0=tmp16a, in1=tmp16b, op=ALU.subtract))
            dv(lambda e: e.tensor_tensor(out=tmp16a, in0=bl[:, 1], in1=fre_b, op=ALU.mult))
            dv(lambda e: e.tensor_tensor(out=tmp16b, in0=bl[:, 0], in1=fim_b, op=ALU.mult))
            dv(lambda e: e.tensor_tensor(out=bb[:, 1], in0=tmp16a, in1=tmp16b, op=ALU.add))
            Zflat = Zf[:].rearrange("p a b c -> p (a b c)")
            tmp1 = Zflat[:, 0:NST * (L // 2)].rearrange("p (s n) -> p s n", n=L // 2)
            dv(lambda e: e.memset(Tre[:, :, 0:1], 1.0))
            dv(lambda e: e.memset(Tim[:, :, 0:1], 0.0))
            dv(lambda e: e.tensor_copy(out=V(ERE), in_=V(COS)))
            dv(lambda e: e.tensor_copy(out=V(EIM), in_=V(SIN)))
            k = 0
            while (1 << k) < L:
                n = 1 << k
                k += 1
                er = V(ERE).unsqueeze(2).to_broadcast([128, NST, n])
                ei = V(EIM).unsqueeze(2).to_broadcast([128, NST, n])
                tt_ = tmp1[:, :, 0:n]
                dv(lambda e, n=n, er=er: e.tensor_tensor(out=Tre[:, :, n:2 * n], in0=Tre[:, :, 0:n], in1=er, op=ALU.mult))
                dv(lambda e, n=n, ei=ei, tt_=tt_: e.tensor_tensor(out=tt_, in0=Tim[:, :, 0:n], in1=ei, op=ALU.mult))
                dv(lambda e, n=n, tt_=tt_: e.tensor_tensor(out=Tre[:, :, n:2 * n], in0=Tre[:, :, n:2 * n], in1=tt_, op=ALU.subtract))
                dv(lambda e, n=n, ei=ei: e.tensor_tensor(out=Tim[:, :, n:2 * n], in0=Tre[:, :, 0:n], in1=ei, op=ALU.mult))
                dv(lambda e, n=n, er=er, tt_=tt_: e.tensor_tensor(out=tt_, in0=Tim[:, :, 0:n], in1=er, op=ALU.mult))
                dv(lambda e, n=n, tt_=tt_: e.tensor_tensor(out=Tim[:, :, n:2 * n], in0=Tim[:, :, n:2 * n], in1=tt_, op=ALU.add))
                dv(lambda e: e.tensor_tensor(out=V(TA), in0=V(ERE), in1=V(ERE), op=ALU.mult))
                dv(lambda e: e.tensor_tensor(out=V(TB_), in0=V(EIM), in1=V(EIM), op=ALU.mult))
                dv(lambda e: e.tensor_tensor(out=V(EIM), in0=V(ERE), in1=V(EIM), op=ALU.mult))
                dv(lambda e: e.tensor_scalar(out=V(EIM), in0=V(EIM), scalar1=2.0, scalar2=None, op0=ALU.mult))
                dv(lambda e: e.tensor_tensor(out=V(ERE), in0=V(TA), in1=V(TB_), op=ALU.subtract))
            dv(lambda e: e.tensor_copy(out=V(E7R), in_=V(ERE)))
            t_tab = dv(lambda e: e.tensor_copy(out=V(E7I), in_=V(EIM)))
            dv(lambda e: e.memset(Zflat, 0.0))
            dv(lambda e: e.memset(Ct[:].rearrange("p a b c -> p (a b c)"), 0.0))
            dv(lambda e: e.memset(ini[:].rearrange("p a s -> p (a s)"), 0.0))
            t_z = None
            for j in range(4):
                for gg in range(2):
                    ps_ = slice(gg * 64, (gg + 1) * 64)
                    cs_ = slice(32 * j + 16 * gg, 32 * j + 16 * gg + 16)
                    for ri in range(2):
                        dv(lambda e, j=j, ps_=ps_, cs_=cs_, ri=ri: e.tensor_copy(out=Zf[ps_, j::4, ri, cs_], in_=bb[ps_, ri, j::4, :]))
                        if ri == 0:
                            dv(lambda e, j=j, ps_=ps_, cs_=cs_: e.tensor_scalar(out=Ct[ps_, j::4, 2, cs_], in0=cl[ps_, 0, j::4, :], scalar1=-1.0, scalar2=None, op0=ALU.mult))
                            t_z = dv(lambda e, j=j, ps_=ps_, cs_=cs_: e.tensor_copy(out=Ct[ps_, j::4, 0, cs_], in_=cl[ps_, 0, j::4, :]))
                        else:
                            t_z = dv(lambda e, j=j, ps_=ps_, cs_=cs_: e.tensor_scalar(out=Ct[ps_, j::4, 1, cs_], in0=cl[ps_, 1, j::4, :], scalar1=-1.0, scalar2=None, op0=ALU.mult))
            t_bt = None
            pfree = {}
            for g4 in range(16):
                bank = 6 + g4 % 2
                tk = None
                for q4 in range(4):
                    st_, ri = divmod(g4 * 4 + q4, 2)
                    tk = P.op("pe", lambda e, st_=st_, ri=ri, q4=q4, bank=bank: e.transpose(out=PS[bank][:, q4 * 128:(q4 + 1) * 128], in_=Zf[:, st_, ri, :], identity=ident[:]),
                              waits=[t_z, pfree.get(bank)] if q4 == 0 else [], sig="pe" if q4 == 3 else None)
                st0_ = (g4 * 4) // 2
                t_bt = ac(lambda e, st0_=st0_, bank=bank: e.activation(out=Bt[:, st0_:st0_ + 2, :, :].rearrange("p a b c -> p (a b) c"), in_=PS[bank][:].rearrange("p (a c) -> p a c", a=4), func=AF.Copy), waits=[tk])
                pfree[bank] = t_bt

            chain["on"] = False
            P.barrier()
            Zfl = Zf[:].rearrange("p a b c -> p (a b c)")
            mm_ = [Zfl[:, i * 2 * TB:(i + 1) * 2 * TB].rearrange("p (j t) -> p j t", j=2) for i in range(4)]
            Zb = Zfl[:, 4 * 2 * TB:8 * 2 * TB].bitcast(BF16)
            dmb = [[Zb[:, (sl_ * 4 + i) * 2 * TB:(sl_ * 4 + i + 1) * 2 * TB].rearrange("p (j t) -> p j t", j=2) for i in range(4)] for sl_ in range(2)]
            rre2 = [rbig[:, (2 * k) * 2 * TB:(2 * k + 1) * 2 * TB].rearrange("p (j t) -> p j t", j=2) for k in range(2)]
            rim2 = [rbig[:, (2 * k + 1) * 2 * TB:(2 * k + 2) * 2 * TB].rearrange("p (j t) -> p j t", j=2) for k in range(2)]
            uscr_t = uscr.rearrange("(ct p) t -> p ct t", p=128)
            wre2 = [wre, sb2("wre_b", [128, 2, TB], F32)]
            wim2 = [wim, sb2("wim_b", [128, 2, TB], F32)]
            NP = NTB * 16
            S = dict(tk_y={}, dm_done={}, bu_free=None, t_u={}, u_free=[None, None], ybank_free={}, s_free=[None, None], dm_free=None,
                     yst_free=[None, None], yv_free=None, ny=0, t_rot=t_tab, mm_free=None, tk_bu={}, t_wj={}, t_m={},
                     chain_end={}, last_bu_tb={}, t_y_tb={})

            def load_u(tb):
                us = tb % 2
                S["t_u"][tb] = P.dma("sp", uT[:, us], uscr_t[:, :, tb * TB:(tb + 1) * TB], waits=[S["u_free"][us]], sig=f"u{us}")

            def do_bu(g):
                tb, pp = divmod(g, 16)
                us, ct = tb % 2, pp // 2
                tk = None
                for jj in range(2):
                    st_ = 2 * pp + jj
                    for ri in range(2):
                        tk = P.op("pe", lambda e, st_=st_, ri=ri, jj=jj, us=us, ct=ct: e.matmul(PS[2 * jj + ri][:], lhsT=Bt[:, st_, ri, :], rhs=uT[:, us, ct, :], start=True, stop=True),
                                  waits=[S["t_u"][tb], t_bt, S["bu_free"]] if (jj == 0 and ri == 0) else [], sig="pe" if (jj == 1 and ri == 1) else None)
                S["tk_bu"][g] = tk
                S["last_bu_tb"][tb] = tk

            def mod_piece(g, jj, half):
                tb, pp = divmod(g, 16)
                st_ = 2 * pp + jj
                wb = g % 2
                trb = Tre[:, st_, :].unsqueeze(1).to_broadcast([128, SUBS, L])
                tib = Tim[:, st_, :].unsqueeze(1).to_broadcast([128, SUBS, L])
                pre = PS[2 * jj][:].rearrange("p (a b) -> p a b", a=SUBS)
                pim = PS[2 * jj + 1][:].rearrange("p (a b) -> p a b", a=SUBS)
                combos = ((pre, trb), (pim, tib), (pim, trb), (pre, tib))
                t_m = None
                for i in (2 * half, 2 * half + 1):
                    src, tab = combos[i]
                    first = (jj == 0 and i == 0)
                    t_m = dv(lambda e, i=i, jj=jj, src=src, tab=tab: e.tensor_tensor(out=mm_[i][:, jj, :].rearrange("p (a b) -> p a b", a=SUBS), in0=src, in1=tab, op=ALU.mult),
                             waits=[S["tk_bu"][g], t_tab, S["mm_free"]] if first else [], sig="dve" if i == 3 else None)
                if half == 1:
                    dv(lambda e, jj=jj, wb=wb: e.tensor_tensor(out=wre2[wb][:, jj, :], in0=mm_[0][:, jj, :], in1=mm_[1][:, jj, :], op=ALU.add), waits=[t_m], sig=None)
                    S["t_wj"][(g, jj)] = dv(lambda e, jj=jj, wb=wb: e.tensor_tensor(out=wim2[wb][:, jj, :], in0=mm_[2][:, jj, :], in1=mm_[3][:, jj, :], op=ALU.subtract))
                    if jj == 1:
                        S["bu_free"] = t_m
                        S["mm_free"] = S["t_wj"][(g, 1)]

            def run_chain(g, pieces):
                tb, pp = divmod(g, 16)
                wb = g % 2
                rre, rim = rre2[wb], rim2[wb]
                pieces = list(pieces)
                for sub in range(SUBS):
                    sl = slice(sub * L, (sub + 1) * L)
                    t_sc = None
                    for jj in range(2):
                        st_ = 2 * pp + jj
                        magb = V(MAG)[:, st_:st_ + 1].to_broadcast([128, L])
                        dv(lambda e, jj=jj, st_=st_, sl=sl, magb=magb, wb=wb: e.tensor_tensor_scan(out=rre[:, jj, sl], data0=magb, data1=wre2[wb][:, jj, sl], initial=ini[:, 0, st_:st_ + 1], op0=ALU.mult, op1=ALU.add),
                           waits=[S["t_wj"][(g, jj)], S["t_rot"], S["dm_done"].get(g - 2)], sig=None)
                        t_sc = dv(lambda e, jj=jj, st_=st_, sl=sl, magb=magb, wb=wb: e.tensor_tensor_scan(out=rim[:, jj, sl], data0=magb, data1=wim2[wb][:, jj, sl], initial=ini[:, 1, st_:st_ + 1], op0=ALU.mult, op1=ALU.add))
                    if pieces:
                        pieces.pop(0)()
                    last = sub * L + L - 1
                    fr = rre[:, :, last]
                    fi = rim[:, :, last]
                    e7r = V(E7R)[:, 2 * pp:2 * pp + 2]
                    e7i = V(E7I)[:, 2 * pp:2 * pp + 2]
                    dv(lambda e, fr=fr, e7r=e7r: e.tensor_tensor(out=rt[:, 0, :], in0=fr, in1=e7r, op=ALU.mult), waits=[t_sc], sig=None)
                    dv(lambda e, fi=fi, e7i=e7i: e.tensor_tensor(out=rt[:, 1, :], in0=fi, in1=e7i, op=ALU.mult), sig=None)
                    dv(lambda e, fr=fr, e7i=e7i: e.tensor_tensor(out=rt[:, 2, :], in0=fr, in1=e7i, op=ALU.mult), sig=None)
                    t_rt = dv(lambda e, fi=fi, e7r=e7r: e.tensor_tensor(out=rt[:, 3, :], in0=fi, in1=e7r, op=ALU.mult))
                    if pieces:
                        pieces.pop(0)()
                    dv(lambda e, pp=pp: e.tensor_tensor(out=ini[:, 0, 2 * pp:2 * pp + 2], in0=rt[:, 0, :], in1=rt[:, 1, :], op=ALU.subtract), waits=[t_rt], sig=None)
                    S["t_rot"] = dv(lambda e, pp=pp: e.tensor_tensor(out=ini[:, 1, 2 * pp:2 * pp + 2], in0=rt[:, 2, :], in1=rt[:, 3, :], op=ALU.add))
                for pc in pieces:
                    pc()
                S["chain_end"][g] = S["t_rot"]

            def own_part(g):
                tb, pp = divmod(g, 16)
                us, ct = tb % 2, pp // 2
                own_i = tb - (NTB - 4)
                ss = pp % 2
                wb = g % 2
                rre, rim = rre2[wb], rim2[wb]
                t_d = None
                for jj in range(2):
                    st_ = 2 * pp + jj
                    trb = Tre[:, st_, :].unsqueeze(1).to_broadcast([128, SUBS, L])
                    tib = Tim[:, st_, :].unsqueeze(1).to_broadcast([128, SUBS, L])
                    rr = rre[:, jj, :].rearrange("p (a b) -> p a b", a=SUBS)
                    ri_ = rim[:, jj, :].rearrange("p (a b) -> p a b", a=SUBS)
                    for i, (src, tab) in enumerate(((rr, trb), (ri_, tib), (rr, tib), (ri_, trb))):
                        t_d = P.op("pool", lambda e, i=i, jj=jj, src=src, tab=tab, ss=ss: e.tensor_tensor(out=dmb[ss][i][:, jj, :].rearrange("p (a b) -> p a b", a=SUBS), in0=src, in1=tab, op=ALU.mult),
                                   waits=[S["t_rot"], S["s_free"][ss]] if (jj == 0 and i == 0) else [], sig="pool")
                S["dm_done"][g] = t_d
                yb = 4 + ct % 2
                tk_y = None
                csel = (0, 2, 1, 1)
                for jj in range(2):
                    st_ = 2 * pp + jj
                    for i in range(4):
                        first = (ss == 0 and jj == 0 and i == 0)
                        lastm = (ss == 1 and jj == 1 and i == 3)
                        tk_y = P.op("pe", lambda e, st_=st_, i=i, jj=jj, ss=ss, yb=yb, first=first, lastm=lastm: e.matmul(PS[yb][:], lhsT=Ct[:, st_, csel[i], :], rhs=dmb[ss][i][:, jj, :], start=first, stop=lastm),
                                    waits=[t_d, t_z, S["ybank_free"].get(yb)] if (jj == 0 and i == 0) else [], sig="pe" if (jj == 1 and i == 3) else None)
                S["s_free"][ss] = tk_y
                S["tk_y"][g] = tk_y

            def own_b(g):
                tb, pp = divmod(g, 16)
                us, ct = tb % 2, pp // 2
                own_i = tb - (NTB - 4)
                ss = pp % 2
                yb = 4 + ct % 2
                tk_y = S["tk_y"][g]
                if ss == 1:
                    t_y = dv(lambda e, us=us, ct=ct, yb=yb: e.scalar_tensor_tensor(out=yv[:], in0=uT[:, us, ct, :], scalar=dd[:, ct:ct + 1], in1=PS[yb][:], op0=ALU.mult, op1=ALU.add), waits=[tk_y, S["yv_free"]])
                    S["ybank_free"][yb] = t_y
                    S["t_y_tb"][tb] = t_y
                    t_i = P.op("pool", lambda e: e.tensor_tensor(out=y2[:], in0=yv[:], in1=yv[:], op=ALU.mult), waits=[t_y], sig="pool")
                    t_i = P.op("pool", lambda e: e.tensor_scalar(out=y2[:], in0=y2[:], scalar1=0.044715, scalar2=1.0, op0=ALU.mult, op1=ALU.add), waits=[t_i], sig="pool")
                    t_i = P.op("pool", lambda e: e.tensor_tensor(out=y2[:], in0=y2[:], in1=yv[:], op=ALU.mult), waits=[t_i], sig="pool")
                    t_g = ac(lambda e: e.activation(out=ysg[:], in_=y2[:], func=AF.Sigmoid, scale=2.0 * 0.7978845608028654), waits=[t_i])
                    ys = S["ny"] % 2
                    S["ny"] += 1
                    t_o = P.op("pool", lambda e, ys=ys: e.tensor_tensor(out=yst[:, ys, :], in0=yv[:], in1=ysg[:], op=ALU.mult), waits=[t_g, S["yst_free"][ys]], sig="pool")
                    S["yv_free"] = t_o
                    S["yst_free"][ys] = P.dma("act", yscr[ct * 128:(ct + 1) * 128, own_i * TB:(own_i + 1) * TB], yst[:, ys, :], waits=[t_o], sig=f"ys{ys}")

            load_u(0)
            do_bu(0)
            for jj in range(2):
                for half in range(2):
                    mod_piece(0, jj, half)
            do_bu(1)
            for g in range(NP):
                tb, pp = divmod(g, 16)
                if pp == 1 and tb + 1 < NTB:
                    rd = [S["last_bu_tb"].get(tb - 1), S["t_y_tb"].get(tb - 1)]
                    rd = [r for r in rd if r is not None]
                    S["u_free"][(tb + 1) % 2] = rd[-1] if rd else None
                    if len(rd) == 2:
                        P.op("sp", None, waits=[rd[0]])
                    load_u(tb + 1)
                pieces = []
                if g + 1 < NP:
                    pieces = [(lambda jj=jj, half=half, g=g: mod_piece(g + 1, jj, half)) for jj in range(2) for half in range(2)]
                run_chain(g, pieces)
                if g + 2 < NP:
                    do_bu(g + 2)
                if g >= 1 and (g - 1) // 16 >= NTB - 4:
                    own_b(g - 1)
                if tb >= NTB - 4:
                    own_part(g)
            own_b(NP - 1)
            if "sv" in dbg_outs:
                P.barrier()
                P.dma("sp", dbg_outs["sv"], sv[:].rearrange("p a s -> p (a s)"), sig="dbg")
                P.dma("sp", dbg_outs["Tre"], Tre[:].rearrange("p a s -> p (a s)"), sig="dbg")
                P.dma("sp", dbg_outs["Tim"], Tim[:].rearrange("p a s -> p (a s)"), sig="dbg")
                P.dma("sp", dbg_outs["bbo"], rbig[:, 2048:3072], sig="dbg")
                P.dma("sp", dbg_outs["inio"], ini[:].rearrange("p a s -> p (a s)"), sig="dbg")
                P.barrier()
            p2.close()
            if "yssm" in dbg_outs:
                P.barrier()
                P.dma("sp", dbg_outs["yssm"], yscr, sig="dbg")

        _p2()

    if dbg.get("qkv_in"):
        P.dma("sp", qscr, din("q_in", [1024, OWN], BF16), sig="dbg")
        P.dma("sp", kscr, din("k_in", [1024, 2 * OWN], BF16), sig="dbg")
        P.dma("sp", vscr, din("v_in", [2 * OWN, 2048], BF16), sig="dbg")

    if 3 in do:
        def _p3():
            P.barrier()
            p3 = ExitStack()

            def sb3(name, shape, dt):
                return p3.enter_context(nc.sbuf_tensor("s3_" + name, list(shape), dt))
            mtmp = sb3("mtmp", [128, 2, 128], F32)
            qbd = sb3("qbd", [128, 2, 2, OWN], BF16)
            kT = sb3("kT", [128, 2, 2 * OWN], BF16)
            acc = sb3("acc", [128, 4, OWN], F32)
            dtmp = sb3("dtmp", [64, 4, OWN], F32)
            NVS = 12
            vch = sb3("vch", [128, NVS, 512], BF16)
            pT = sb3("pT", [128, 3, 512], BF16)
            pE = sb3("pE", [128, 3, 512], BF16)
            M01 = sb3("M01", [128, 2, 512], BF16)
            hval = sb3("hval", [128, 1], F32)
            yat = sb3("yat", [64, 4, OWN], BF16)

            t0 = P.dma("sp", mtmp[:, 0, :], maskc_d, sig="cst")
            t0 = P.dma("sp", mtmp[:, 1, :], maskp_d, sig="cst")
            t_hv = P.op("dve", lambda e: e.tensor_scalar(out=hval[:], in0=hbias[:, 0:1], scalar1=0.0, scalar2=None, op0=ALU.is_equal), waits=[t0, tk_cst], sig="dve")
            for v_ in range(2):
                for c_ in range(2):
                    P.op("dve", lambda e, v_=v_, c_=c_: e.tensor_scalar(out=M01[:, v_, c_ * 128:(c_ + 1) * 128], in0=mtmp[:, 1, :], scalar1=0.0, scalar2=None, op0=ALU.is_equal), sig="dve")
                    P.op("dve", lambda e, v_=v_, c_=c_: e.tensor_scalar(out=M01[:, v_, 256 + c_ * 128:256 + (c_ + 1) * 128], in0=mtmp[:, 0, :], scalar1=0.0, scalar2=None, op0=ALU.is_equal), sig="dve")
            t_mask = P.op("dve", lambda e: e.tensor_scalar(out=M01[:, 1, 0:256], in0=M01[:, 1, 0:256], scalar1=hval[:, 0:1], scalar2=None, op0=ALU.mult), waits=[t_hv], sig="dve")
            t_qz = P.op("dve", lambda e: e.memset(qbd[:].rearrange("p a b c -> p (a b c)"), 0.0), sig="dve")

            v_free = [None] * NVS
            nv = 0
            ps_s_free = [None, None, None]
            pT_free = [None, None, None]
            pE_free = [None, None, None]
            ps_od_free = [None, None]
            qk_free = t_qz
            acc_free = None
            yat_free = None
            dtmp_free = None
            for hq in range(4):
                t_q = None
                for t in range(2):
                    r0 = (2 * hq + t) * 128
                    P.dma("sp", qbd[0:64, t, 0, :], qscr[r0:r0 + 64, :], waits=[qk_free], sig="qk")
                    P.dma("sp", qbd[64:128, t, 1, :], qscr[r0 + 64:r0 + 128, :], waits=[qk_free], sig="qk")
                    t_q = P.dma("sp", kT[:, t, :], kscr[r0:r0 + 128, :], waits=[qk_free], sig="qk")
                t_z = P.op("dve", lambda e: e.memset(acc[:].rearrange("p a c -> p (a c)"), 0.0), waits=[acc_free], sig="dve")
                chunks = []
                blocks = []
                for d in (1, 4, 16):
                    nb = OWN // (128 * d)
                    for r in range(d):
                        for n in range(-1, nb):
                            chunks.append((OWN + 128 * n * d + r, d))
                            if n >= 0:
                                blocks.append((d, r, n, len(chunks) - 2, len(chunks) - 1))
                chunk_tok = {}
                nextc = {"c": 0}

                def ensure(upto):
                    while nextc["c"] <= min(upto, len(chunks) - 1):
                        c = nextc["c"]
                        a0, d_ = chunks[c]
                        slot = (nv0 + c) % NVS
                        chunk_tok[c] = P.dma("sp", vch[:, slot, :], vscr[a0:a0 + 127 * d_ + 1:d_, hq * 512:(hq + 1) * 512], waits=[v_free[slot]], sig=f"v{slot}")
                        nextc["c"] += 1
                nv0 = nv
                items = []
                for bj, (d, r, n, pc, cc) in enumerate(blocks):
                    for t in range(2):
                        items.append(dict(d=d, r=r, n=n, t=t, bj=bj, pc=pc, cc=cc))
                nv += len(chunks)
                state = {}

                def emit_S(i, it):
                    d, r, n, t = it["d"], it["r"], it["n"], it["t"]
                    b = i % 3
                    o0 = 128 * n * d + r
                    qap = qbd[:, t, :, o0:o0 + 127 * d + 1:d]
                    kcur = kT[:, t, OWN + o0:OWN + o0 + 127 * d + 1:d]
                    kprev = kT[:, t, OWN + o0 - 128 * d:OWN + o0 - d + 1:d]
                    mv = 1 if n == 0 else 0
                    P.op("pe", lambda e, b=b, kprev=kprev, qap=qap: e.matmul(PS[b][:, 0:256].rearrange("p (a c) -> p a c", a=2), lhsT=kprev, rhs=qap, start=True, stop=True), waits=[ps_s_free[b], t_q, tk_setup])
                    tk = P.op("pe", lambda e, b=b, kcur=kcur, qap=qap: e.matmul(PS[b][:, 256:512].rearrange("p (a c) -> p a c", a=2), lhsT=kcur, rhs=qap, start=True, stop=True), sig="pe")
                    t_e = P.op("act", lambda e, b=b: e.activation(out=pE[:, b, :], in_=PS[b][:], func=AF.Exp, scale=0.125), waits=[tk, pE_free[b]], sig="act")
                    ps_s_free[b] = t_e
                    t_p = P.op("pool", lambda e, b=b, mv=mv: e.tensor_tensor(out=pT[:, b, :], in0=pE[:, b, :], in1=M01[:, mv, :], op=ALU.mult), waits=[t_e, pT_free[b], t_mask], sig="pool")
                    pE_free[b] = t_p
                    it["t_e"] = t_p

                def emit_PV(i, it):
                    d, r, n, t = it["d"], it["r"], it["n"], it["t"]
                    b = i % 3
                    ob = 4 + i % 2
                    o0 = 128 * n * d + r
                    sp_, sc_ = (nv0 + it["pc"]) % NVS, (nv0 + it["cc"]) % NVS
                    tk = None
                    for ab in range(2):
                        hh = 2 * t + ab
                        P.op("pe", lambda e, b=b, ob=ob, sp_=sp_, hh=hh, ab=ab: e.matmul(PS[ob][:, ab * 128:(ab + 1) * 128], lhsT=vch[:, sp_, hh * 128:(hh + 1) * 128], rhs=pT[:, b, ab * 128:(ab + 1) * 128], start=True, stop=False),
                             waits=[it["t_e"], chunk_tok[it["pc"]], chunk_tok[it["cc"]], ps_od_free[i % 2]] if ab == 0 else [])
                        tk = P.op("pe", lambda e, b=b, ob=ob, sc_=sc_, hh=hh, ab=ab: e.matmul(PS[ob][:, ab * 128:(ab + 1) * 128], lhsT=vch[:, sc_, hh * 128:(hh + 1) * 128], rhs=pT[:, b, 256 + ab * 128:256 + (ab + 1) * 128], start=False, stop=True),
                                  sig="pe" if ab == 1 else None)
                    pT_free[b] = tk
                    if t == 1:
                        v_free[sp_] = tk
                        v_free[sc_] = tk
                    dst = acc[:, 2 * t:2 * t + 2, o0:o0 + 127 * d + 1:d]
                    t_a = P.op("dve", lambda e, ob=ob, dst=dst: e.tensor_tensor(out=dst, in0=dst, in1=PS[ob][:, 0:256].rearrange("p (a c) -> p a c", a=2), op=ALU.add), waits=[tk, t_z], sig="dve")
                    ps_od_free[i % 2] = t_a
                    state["t_a"] = t_a
                    state["last_pe"] = tk

                for i, it in enumerate(items):
                    if it["t"] == 0:
                        ensure(blocks[min(it["bj"] + 2, len(blocks) - 1)][4])
                    emit_S(i, it)
                    if i >= 2:
                        emit_PV(i - 2, items[i - 2])
                emit_PV(len(items) - 2, items[-2])
                emit_PV(len(items) - 1, items[-1])
                qk_free = state["last_pe"]
                t_dm = P.dma("sp", dtmp[:], acc[64:128, :, :], waits=[state["t_a"], dtmp_free], sig="dtmp")
                P.op("act", lambda e: e.activation(out=dtmp[:], in_=dtmp[:], func=AF.Ln), waits=[t_dm], sig="act")
                t_r = P.op("act", lambda e: e.activation(out=dtmp[:], in_=dtmp[:], func=AF.Exp, scale=-1.0), sig="act")
                t_y = P.op("dve", lambda e: e.tensor_tensor(out=yat[:], in0=acc[0:64, :, :], in1=dtmp[:], op=ALU.mult), waits=[t_r, yat_free], sig="dve")
                acc_free = t_y
                dtmp_free = t_y
                yat_free = P.dma("sp", ascr[hq * 256:(hq + 1) * 256, :].rearrange("(hh e) t -> e hh t", e=64), yat[:], waits=[t_y], sig="yat")
            p3.close()
            if "yatt" in dbg_outs:
                P.barrier()
                P.dma("sp", dbg_outs["yatt"], ascr, sig="dbg")

        _p3()

    if dbg.get("p4_in"):
        P.dma("sp", h1scr, din("h1_in", [D, OWN], F32), sig="dbg")
        P.dma("sp", yscr, din("ys_in", [1024, OWN], BF16), sig="dbg")
        P.dma("sp", ascr, din("ya_in", [1024, OWN], BF16), sig="dbg")

    if 4 in do:
        def _p4():
            P.barrier()
            p4 = ExitStack()

            def sb4(name, shape, dt):
                return p4.enter_context(nc.sbuf_tensor("s4_" + name, list(shape), dt))
            hT = sb4("hT", [128, DT, TB], F32)
            xnT = sb4("xnT", [128, DT, TB], BF16)
            hid = sb4("hid", [128, FT, TB], BF16)
            wbuf = sb4("wbuf", [128, 4, D], BF16)
            wdbuf = sb4("wdbuf", [128, 2, DFF], BF16)
            sg = sb4("sg", [128, 2, TB], F32)
            rtmp = sb4("rtmp", [128, TB], F32)
            rstd = sb4("rstd", [128, TB], F32)
            yT = sb4("yT", [128, 8, TB], BF16)
            aT = sb4("aT", [128, 8, TB], BF16)
            yg = sb4("yg", [128, 8, TB], F32)
            mixT = sb4("mixT", [128, DT, TB], BF16)
            ost = sb4("ost", [128, D], F32)
            sq = hid[:, 0:DT, :]
            W.slot_free = [None] * 4
            Wd.slot_free = [None] * 2
            ffn.sg_free = [None, None]
            ffn.hid_free = None
            psum_free = {}
            y_t = y_d.rearrange("(n p) d -> n p d", p=128)
            in_free = []
            ost_free = None
            mix_readers = None
            ntp = 0
            for i in dbg.get("tbs_p4", list(range(4))):
                sl = slice(i * TB, (i + 1) * TB)
                t_h = P.dma("sp", hT[:], h1scr.rearrange("(dt p) t -> p dt t", p=128)[:, :, sl], waits=in_free, sig="p4h")
                t_ys = P.dma("sp", yT[:], yscr.rearrange("(c p) t -> p c t", p=128)[:, :, sl], waits=in_free, sig="p4y")
                t_ya = P.dma("sp", aT[:], ascr.rearrange("(c p) t -> p c t", p=128)[:, :, sl], waits=in_free, sig="p4a")
                in_free = []
                t_g = None
                for mt in range(8):
                    slot = W.n % 4
                    W.n += 1
                    t_ld = P.dma("sp", wbuf[:, slot, 0:1024], wbf["wglu"][mt], waits=[W.slot_free[slot], conv_tok["B"]], sig=f"w{slot}")
                    bank = 4 + mt % 2
                    tk = None
                    for kt in range(8):
                        tk = P.op("pe", lambda e, kt=kt, slot=slot, bank=bank: e.matmul(PS[bank][:], lhsT=wbuf[:, slot, kt * 128:(kt + 1) * 128], rhs=yT[:, kt, :], start=(kt == 0), stop=(kt == 7)),
                                  waits=[t_ld, psum_free.get(bank), t_ys, tk_setup] if kt == 0 else [], sig="pe" if kt == 7 else None)
                    W.slot_free[slot] = tk
                    t_s = P.op("act", lambda e, mt=mt, bank=bank: e.activation(out=sg[:, mt % 2, :], in_=PS[bank][:], func=AF.Sigmoid, bias=gains8[:, 16 + mt:17 + mt]), waits=[tk, ffn.sg_free[mt % 2]], sig="act")
                    psum_free[bank] = t_s
                    t_g = P.op("dve", lambda e, mt=mt: e.tensor_tensor(out=yg[:, mt, :], in0=yT[:, mt, :], in1=sg[:, mt % 2, :], op=ALU.mult), waits=[t_s, mix_readers], sig="dve")
                    ffn.sg_free[mt % 2] = t_g
                t_m1, t_stat = rms_to_bf16(yg[:], 8, gains8[:, 0:8], mixT[:, 0:8, :], hid[:, 0:8, :], PS[6], rtmp[:], rstd[:], [t_g, ffn.hid_free], 1024, war_dst=[mix_readers])
                psum_free[6] = t_stat
                t_m2, t_stat = rms_to_bf16(aT[:], 8, gains8[:, 8:16], mixT[:, 8:16, :], hid[:, 0:8, :], PS[6], rtmp[:], rstd[:], [t_ya, t_m1], 1024, war_dst=[mix_readers])
                psum_free[6] = t_stat
                t_res = None
                for mt in range(DT):
                    slot = W.n % 4
                    W.n += 1
                    t_ld = P.dma("sp", wbuf[:, slot, :], wbf["wout"][mt], waits=[W.slot_free[slot], conv_tok["B"]], sig=f"w{slot}")
                    bank = 4 + mt % 2
                    tk = None
                    for kt in range(DT):
                        tk = P.op("pe", lambda e, kt=kt, slot=slot, bank=bank: e.matmul(PS[bank][:], lhsT=wbuf[:, slot, kt * 128:(kt + 1) * 128], rhs=mixT[:, kt, :], start=(kt == 0), stop=(kt == DT - 1)),
                                  waits=[t_ld, psum_free.get(bank), t_m1, t_m2] if kt == 0 else [], sig="pe" if kt == DT - 1 else None)
                    W.slot_free[slot] = tk
                    t_res = P.op("dve", lambda e, mt=mt, bank=bank: e.tensor_tensor(out=hT[:, mt, :], in0=PS[bank][:], in1=hT[:, mt, :], op=ALU.add), waits=[tk, t_h], sig="dve")
                    psum_free[bank] = t_res
                    mix_readers = tk
                in_free.append(mix_readers)
                t_xn, t_stat = rms_to_bf16(hT[:], DT, gains[:, 2 * DT:3 * DT], xnT, sq, PS[6], rtmp[:], rstd[:], [t_res, ffn.hid_free], D, war_dst=[])
                psum_free[6] = t_stat
                t_res2, t_pe_last = ffn(hT, xnT, hid, wbuf, wdbuf, sg, wbf["wg2"], wbf["wu2"], wbf["wd2"], conv_tok["B"], psum_free, list(rms_to_bf16.last_toks))
                t_fin, t_stat = rms_to_bf16(hT[:], DT, gains[:, 3 * DT:4 * DT], hT, sq, PS[6], rtmp[:], rstd[:], [t_res2, ffn.hid_free], D, war_dst=[])
                psum_free[6] = t_stat
                tk = None
                for s_ in range(4):
                    t_ev = None
                    for dq in range(4):
                        bank = 6 + (ntp % 2)
                        ntp += 1
                        for j in range(4):
                            dt = dq * 4 + j
                            tk = P.op("pe", lambda e, dt=dt, j=j, s_=s_, bank=bank: e.transpose(out=PS[bank][:, j * 128:(j + 1) * 128], in_=hT[:, dt, s_ * 128:(s_ + 1) * 128], identity=ident[:]),
                                      waits=[t_fin, psum_free.get(bank)] if j == 0 else [], sig="pe" if j == 3 else None)
                        t_ev = P.op("act", lambda e, dq=dq, bank=bank: e.activation(out=ost[:, dq * 512:(dq + 1) * 512], in_=PS[bank][:], func=AF.Copy), waits=[tk, ost_free if dq == 0 else None], sig="act")
                        psum_free[bank] = t_ev
                    ost_free = P.dma("pool", y_t[i * 4 + s_], ost[:], waits=[t_ev], sig="yout")
                in_free.append(tk)
            p4.close()

        _p4()

    fin = [t for t in phase_end if t is not None]
    for s, (h, cnt) in list(P.sems.items()):
        if cnt > 0:
            fin.append((s, cnt))
    P.op("pool", lambda e: e.memset(epsc[:, 0:1], EPS), waits=fin)
    P.emit()
    st.close()
    return nc


def _tile_w(w, kt, mt):
    return np.ascontiguousarray(w.reshape(kt, 128, mt, 128).transpose(2, 1, 0, 3).reshape(mt, 128, kt * 128))


def _prep_shared(inp):
    f = lambda a: np.asarray(a, dtype=np.float32)
    sh = {}
    sh["wg1"] = _tile_w(f(inp["ffn1_w_gate"])[0], DT, FT)
    sh["wu1"] = _tile_w(f(inp["ffn1_w_up"])[0], DT, FT)
    sh["wd1"] = _tile_w(f(inp["ffn1_w_down"])[0], FT, DT)
    sh["wg2"] = _tile_w(f(inp["ffn2_w_gate"])[0], DT, FT)
    sh["wu2"] = _tile_w(f(inp["ffn2_w_up"])[0], DT, FT)
    sh["wd2"] = _tile_w(f(inp["ffn2_w_down"])[0], FT, DT)
    sh["win"] = _tile_w(f(inp["w_in"])[0], DT, 32)
    sh["wout"] = _tile_w(f(inp["w_out"])[0], DT, DT)
    sh["wglu"] = _tile_w(f(inp["ssm_w_glu"])[0], 8, 8)
    g16 = lambda v: f(v).reshape(DT, 128).T
    g8 = lambda v: f(v).reshape(8, 128).T
    sh["gains"] = np.ascontiguousarray(np.concatenate([g16(inp["ffn1_norm"][0]), g16(inp["mix_norm"][0]), g16(inp["ffn2_norm"][0]), g16(inp["final_norm"])], axis=1))
    sh["gains8"] = np.ascontiguousarray(np.concatenate([g8(inp["ssm_out_norm"][0]), g8(inp["attn_out_norm"][0]), g8(inp["ssm_b_glu"][0])], axis=1))
    def st_l(a):
        a = f(a)
        r = a.reshape(32, 2, 64, *a.shape[2:])
        r = np.moveaxis(r, 0, 2)
        return np.ascontiguousarray(r.reshape(128, 32, *a.shape[2:]))
    ldt = np.broadcast_to(f(inp["ssm_log_dt"])[0][:, None], (64, 64))
    sh["ssm_sc"] = np.ascontiguousarray(np.concatenate([st_l(ldt), st_l(inp["ssm_a_re"][0]), st_l(inp["ssm_a_im"][0])], axis=1))
    sh["ssm_b"] = np.ascontiguousarray(np.concatenate([st_l(inp["ssm_b_re"][0]).reshape(128, -1), st_l(inp["ssm_b_im"][0]).reshape(128, -1)], axis=1))
    cre = np.transpose(f(inp["ssm_c_re"])[0], (0, 2, 1))
    cim = np.transpose(f(inp["ssm_c_im"])[0], (0, 2, 1))
    sh["ssm_c"] = np.ascontiguousarray(np.concatenate([st_l(cre).reshape(128, -1), st_l(cim).reshape(128, -1)], axis=1))
    sh["ssm_dd"] = np.ascontiguousarray(f(inp["ssm_d"])[0].reshape(8, 128).T)
    sh["ident"] = np.eye(128, dtype=np.float32)
    k = np.arange(128)[:, None]
    q = np.arange(128)[None, :]
    sh["maskc"] = np.where(k <= q, 0.0, NEG).astype(np.float32)
    sh["maskp"] = np.where(k >= q, 0.0, NEG).astype(np.float32)
    return sh


def _prep_core(x, c):
    b, q = divmod(c, 4)
    xs = np.zeros((SEQ, D), np.float32)
    n = (q + 1) * OWN
    xs[SEQ - n:] = x[b, :n]
    hb = np.full((128, 1), 0.0 if q >= 1 else NEG, np.float32)
    return {"xs": xs, "hbias": hb}


def kernel(**inputs):
    x = np.asarray(inputs["x"], dtype=np.float32)
    nc = build_nc()
    sh = _prep_shared(inputs)
    in_maps = []
    for c in range(NCORES):
        m = dict(sh)
        m.update(_prep_core(x, c))
        in_maps.append(m)
    res = run_bass_kernel_spmd(nc, in_maps, core_ids=list(range(NCORES)))
    out = np.empty((BATCH, SEQ, D), np.float32)
    for c in range(NCORES):
        b, q = divmod(c, 4)
        out[b, q * OWN:(q + 1) * OWN] = np.asarray(res.results[c]["y"])
    return out
```

```python
from contextlib import ExitStack
import math
import numpy as np
import concourse.bass as bass
import concourse.mybir as mybir
from concourse.bass_utils import run_bass_kernel_spmd

F32 = mybir.dt.float32
BF16 = mybir.dt.bfloat16
I32 = mybir.dt.int32
ALU = mybir.AluOpType
AF = mybir.ActivationFunctionType

NCORES = 8
D = 2048
DT = 16
SEQ = 8192
BATCH = 2
OWN = 2048
TB = 512
NTB = SEQ // TB
DFF = 5504
FT = DFF // 128
EPS = 1e-6
NEG = -30000.0
L = 256
SUBS = TB // L
NST = 32


class Prog:
    ENGS = ("sp", "act", "pool", "pe", "dve")

    def __init__(self, nc, stack):
        self.nc = nc
        self.stack = stack
        self.ops = {e: [] for e in self.ENGS}
        self.sems = {}

    def _sem(self, name):
        if name not in self.sems:
            self.sems[name] = [self.stack.enter_context(self.nc.semaphore(name)), 0]
        return self.sems[name]

    def op(self, eng, fn, waits=(), sig=None, inc=1):
        tok = None
        if sig is not None:
            s = self._sem(sig)
            s[1] += inc
            tok = (sig, s[1])
        ws = tuple(w for w in waits if w is not None)
        self.ops[eng].append((fn, ws, sig, inc))
        return tok

    def dma(self, eng, out, in_, waits=(), sig=None):
        return self.op(eng, lambda e, o=out, i=in_: e.dma_start(out=o, in_=i), waits, sig, 16)

    def barrier(self):
        toks = [(s, c) for s, (h, c) in self.sems.items() if c > 0]
        for e in self.ENGS:
            self.op(e, None, waits=toks)
        return toks

    def emit(self):
        sems = self.sems
        ops = self.ops

        def run(e, lst):
            seen = {}
            for fn, ws, sig, inc in lst:
                for (s, v) in ws:
                    if seen.get(s, 0) < v:
                        e.wait_ge(sems[s][0], v)
                        seen[s] = v
                if fn is None:
                    continue
                ins = fn(e)
                if sig is not None:
                    ins.then_inc(sems[sig][0], inc)

        with self.nc.Block() as block:
            @block.sync
            def _(e):
                run(e, ops["sp"])

            @block.scalar
            def _(e):
                run(e, ops["act"])

            @block.gpsimd
            def _(e):
                run(e, ops["pool"])

            @block.tensor
            def _(e):
                run(e, ops["pe"])

            @block.vector
            def _(e):
                run(e, ops["dve"])


def build_nc(dbg=None):
    dbg = dbg or {}
    tbs_p1 = dbg.get("tbs_p1", list(range(NTB)))
    do = dbg.get("phases", (0, 1, 2, 3, 4))
    nc = bass.Bass("TRN2", target_bir_lowering=False)
    st = ExitStack()
    P = Prog(nc, st)

    def din(name, shape, dt=F32):
        return nc.dram_tensor(name, list(shape), dt, kind="ExternalInput").ap()

    def dscr(name, shape, dt):
        return nc.dram_tensor(name, list(shape), dt).ap()

    def sb(name, shape, dt):
        return st.enter_context(nc.sbuf_tensor("s_" + name, list(shape), dt))

    xs_d = din("xs", [SEQ, D])
    ident_d = din("ident", [128, 128])
    maskc_d = din("maskc", [128, 128])
    maskp_d = din("maskp", [128, 128])
    hbias_d = din("hbias", [128, 1])
    gains_d = din("gains", [128, 4 * DT])
    gains8_d = din("gains8", [128, 3 * 8])
    wfp = {
        "wg1": din("wg1", [FT, 128, D]), "wu1": din("wu1", [FT, 128, D]), "wd1": din("wd1", [DT, 128, DFF]),
        "win": din("win", [32, 128, D]), "wout": din("wout", [DT, 128, D]), "wglu": din("wglu", [8, 128, 1024]),
        "wg2": din("wg2", [FT, 128, D]), "wu2": din("wu2", [FT, 128, D]), "wd2": din("wd2", [DT, 128, DFF]),
    }
    ssm_sc_d = din("ssm_sc", [128, 3 * NST])
    ssm_b_d = din("ssm_b", [128, 2 * NST * 16])
    ssm_c_d = din("ssm_c", [128, 2 * NST * 16])
    ssm_dd_d = din("ssm_dd", [128, 8])
    y_d = nc.dram_tensor("y", [OWN, D], F32, kind="ExternalOutput").ap()

    wbf = {k: dscr(k + "b", v.shape, BF16) for k, v in wfp.items()}
    h1scr = dscr("h1scr", [D, OWN], F32)
    qscr = dscr("qscr", [1024, OWN], BF16)
    kscr = dscr("kscr", [1024, 2 * OWN], BF16)
    vscr = dscr("vscr", [2 * OWN, 2048], BF16)
    uscr = dscr("uscr", [1024, SEQ], BF16)
    yscr = dscr("yscr", [1024, OWN], BF16)
    ascr = dscr("ascr", [1024, OWN], BF16)

    dbg_outs = {}
    if dbg.get("outs"):
        for name, shape, dt in dbg["outs"]:
            dbg_outs[name] = nc.dram_tensor(name, list(shape), dt, kind="ExternalOutput").ap()

    ident = sb("ident", [128, 128], F32)
    identb = sb("identb", [128, 128], BF16)
    onesb = sb("onesb", [128, 128], BF16)
    gains = sb("gains", [128, 4 * DT], F32)
    gains8 = sb("gains8", [128, 24], F32)
    epsc = sb("epsc", [128, 1], F32)
    hbias = sb("hbias", [128, 1], F32)

    PS = [st.enter_context(nc.psum_tensor(f"ps{i}", [128, 512], F32)) for i in range(8)]

    t_c = []
    t_c.append(P.dma("sp", ident[:], ident_d, sig="cst"))
    t_c.append(P.dma("sp", gains[:], gains_d, sig="cst"))
    t_c.append(P.dma("sp", gains8[:], gains8_d, sig="cst"))
    t_c.append(P.dma("sp", hbias[:], hbias_d, sig="cst"))
    tk_cst = t_c[-1]
    P.op("dve", lambda e: e.memset(onesb[:], 1.0))
    P.op("dve", lambda e: e.memset(epsc[:], EPS))
    tk_setup = P.op("dve", lambda e: e.tensor_copy(out=identb[:], in_=ident[:]), waits=[tk_cst], sig="dve")

    conv_tok = {}
    gu_tok = [None] * FT
    NG = 4
    per = (FT + NG - 1) // NG
    for gi in range(NG):
        tok = None
        for fc in range(gi * per, min(FT, (gi + 1) * per)):
            P.dma("pool", wbf["wg1"][fc], wfp["wg1"][fc], sig=f"cvG{gi}")
            tok = P.dma("pool", wbf["wu1"][fc], wfp["wu1"][fc], sig=f"cvG{gi}")
        for fc in range(gi * per, min(FT, (gi + 1) * per)):
            gu_tok[fc] = tok
    for nm in ("wd1", "win"):
        tok = None
        for i in range(wfp[nm].shape[0]):
            tok = P.dma("pool", wbf[nm][i], wfp[nm][i], sig="cv_" + nm)
        conv_tok[nm] = tok
    conv_tok["A"] = conv_tok["win"]
    conv_A = {"gu": gu_tok, "d": conv_tok["wd1"]}
    NB_ = sum(wfp[nm].shape[0] for nm in ("wglu", "wout", "wg2", "wu2", "wd2"))
    P._sem("cvB")
    conv_tok["B"] = ("cvB", 16 * NB_)

    def issue_conv_B():
        if issue_conv_B.done:
            return
        issue_conv_B.done = True
        for nm in ("wglu", "wout", "wg2", "wu2", "wd2"):
            for i in range(wfp[nm].shape[0]):
                P.dma("pool", wbf[nm][i], wfp[nm][i], sig="cvB")
    issue_conv_B.done = False
    if 1 not in do:
        issue_conv_B()

    class Ctx:
        pass

    def rms_to_bf16(srcT, ntile, gain_ap, dstT, sq, pst, rtmp, rstd, wait_src, dim, war_dst=()):
        t_sqs = []
        for g4 in range(ntile // 4):
            t_sqs.append(P.op("act", lambda e, g4=g4: e.activation(out=sq[:, 4 * g4:4 * g4 + 4, :], in_=srcT[:, 4 * g4:4 * g4 + 4, :], func=AF.Square), waits=list(wait_src) if g4 == 0 else [], sig="act"))
        tk = None
        for dt in range(ntile):
            tk = P.op("pe", lambda e, dt=dt: e.matmul(pst[:], lhsT=onesb[:], rhs=sq[:, dt, :], start=(dt == 0), stop=(dt == ntile - 1)),
                      waits=[t_sqs[dt // 4], tk_setup] if dt % 4 == 0 else [], sig="pe" if dt == ntile - 1 else None)
        t_sqrt = P.op("act", lambda e: e.activation(out=rtmp, in_=pst[:], func=AF.Sqrt, bias=epsc[:, 0:1], scale=1.0 / dim), waits=[tk], sig="act")
        t_r = P.op("dve", lambda e: e.reciprocal(out=rstd, in_=rtmp), waits=[t_sqrt] + list(war_dst), sig="dve")
        tk2 = None
        toks = []
        for dt in range(ntile):
            tk2 = P.op("dve", lambda e, dt=dt: e.scalar_tensor_tensor(out=dstT[:, dt, :], in0=srcT[:, dt, :], scalar=gain_ap[:, dt:dt + 1], in1=rstd, op0=ALU.mult, op1=ALU.mult),
                       waits=[t_r] if dt == 0 else [], sig="dve" if (dt % 4 == 3 or dt == ntile - 1) else None)
            toks.append(tk2)
        for dt in range(ntile - 2, -1, -1):
            if toks[dt] is None:
                toks[dt] = toks[dt + 1]
        rms_to_bf16.last_toks = toks
        return tk2, t_sqrt

    W = Ctx()
    W.slot_free = [None] * 4
    W.n = 0
    Wd = Ctx()
    Wd.slot_free = [None] * 2
    Wd.n = 0

    def ffn(hT, xnT, hid, wbuf, wdbuf, sg, wg, wu, wd, conv_wait, psum_free, t_xn):
        hid_tok = [None] * FT
        for fc in range(FT):
            toks = []
            for k, wsrc in enumerate((wg, wu)):
                slot = W.n % 4
                W.n += 1
                cw = conv_wait["gu"][fc] if isinstance(conv_wait, dict) else conv_wait
                t_ld = P.dma("sp", wbuf[:, slot, :], wsrc[fc], waits=[W.slot_free[slot], cw], sig=f"w{slot}")
                bank = (0 if k == 0 else 2) + (fc % 2)
                tk = None
                for dt in range(DT):
                    xw = t_xn[dt] if isinstance(t_xn, list) else (t_xn if dt == 0 else None)
                    tk = P.op("pe", lambda e, dt=dt, slot=slot, bank=bank: e.matmul(PS[bank][:], lhsT=wbuf[:, slot, dt * 128:(dt + 1) * 128], rhs=xnT[:, dt, :], start=(dt == 0), stop=(dt == DT - 1)),
                              waits=([t_ld, psum_free.get(bank)] if dt == 0 else []) + [xw], sig="pe" if dt == DT - 1 else None)
                W.slot_free[slot] = tk
                toks.append(tk)
            bg, bu = fc % 2, 2 + fc % 2
            t_s = P.op("act", lambda e, bg=bg, fc=fc: e.activation(out=sg[:, fc % 2, :], in_=PS[bg][:], func=AF.Silu), waits=[toks[0], ffn.sg_free[fc % 2]], sig="act")
            psum_free[bg] = t_s
            t_h = P.op("dve", lambda e, bu=bu, fc=fc: e.tensor_tensor(out=hid[:, fc, :], in0=sg[:, fc % 2, :], in1=PS[bu][:], op=ALU.mult), waits=[t_s, toks[1], ffn.hid_free], sig="dve")
            psum_free[bu] = t_h
            ffn.sg_free[fc % 2] = t_h
            hid_tok[fc] = t_h
        t_last = None
        for mt in range(DT):
            slot = Wd.n % 2
            Wd.n += 1
            cw = conv_wait["d"] if isinstance(conv_wait, dict) else conv_wait
            t_ld = P.dma("sp", wdbuf[:, slot, :], wd[mt], waits=[Wd.slot_free[slot], cw], sig=f"wd{slot}")
            bank = 4 + mt % 2
            tk = None
            for ft in range(FT):
                tk = P.op("pe", lambda e, ft=ft, slot=slot, bank=bank: e.matmul(PS[bank][:], lhsT=wdbuf[:, slot, ft * 128:(ft + 1) * 128], rhs=hid[:, ft, :], start=(ft == 0), stop=(ft == FT - 1)),
                          waits=[t_ld, psum_free.get(bank), hid_tok[FT - 1]] if ft == 0 else [], sig="pe" if ft == FT - 1 else None)
            Wd.slot_free[slot] = tk
            t_last = P.op("dve", lambda e, mt=mt, bank=bank: e.scalar_tensor_tensor(out=hT[:, mt, :], in0=PS[bank][:], scalar=0.5, in1=hT[:, mt, :], op0=ALU.mult, op1=ALU.add), waits=[tk], sig="dve")
            psum_free[bank] = t_last
            t_pe_last = tk
        ffn.hid_free = t_pe_last
        return t_last, t_pe_last

    ffn.sg_free = [None, None]
    ffn.hid_free = None

    psum_free = {}
    phase_end = []

    if 1 in do:
        p1 = ExitStack()

        def sb1(name, shape, dt):
            return p1.enter_context(nc.sbuf_tensor("s1_" + name, list(shape), dt))
        hT = sb1("hT", [128, DT, TB], F32)
        xnT = sb1("xnT", [128, DT, TB], BF16)
        hid = sb1("hid", [128, FT, TB], BF16)
        xst = sb1("xst", [128, 2, D], F32)
        wbuf = sb1("wbuf", [128, 4, D], BF16)
        wdbuf = sb1("wdbuf", [128, 2, DFF], BF16)
        sg = sb1("sg", [128, 2, TB], F32)
        rtmp = sb1("rtmp", [128, TB], F32)
        rstd = sb1("rstd", [128, TB], F32)
        ostg = sb1("ostg", [128, 2, TB], BF16)
        vstg = sb1("vstg", [128, 4, 16, 128], BF16)
        sq = hid[:, 0:DT, :]

        xs_t = xs_d.rearrange("(n p) d -> n p d", p=128)
        xst_free = [None, None]
        nx = 0
        t_h_prev_readers = []
        t_xn_readers = None
        ostg_free = [None, None]
        no = 0
        vstg_free = P.op("dve", lambda e: e.memset(vstg[:].rearrange("p a h c -> p (a h c)"), 1.0), sig="dve")
        tpbank = [6, 7]
        ntp = 0
        for tb in tbs_p1:
            t_evs = []
            for s in range(4):
                slot = nx % 2
                nx += 1
                t_ld = P.dma("sp", xst[:, slot, :], xs_t[tb * 4 + s], waits=[xst_free[slot]], sig=f"x{slot}")
                tk = None
                for dq in range(4):
                    bank = tpbank[ntp % 2]
                    ntp += 1
                    for j in range(4):
                        dt = dq * 4 + j
                        tk = P.op("pe", lambda e, dt=dt, j=j, slot=slot, bank=bank: e.transpose(out=PS[bank][:, j * 128:(j + 1) * 128], in_=xst[:, slot, dt * 128:(dt + 1) * 128], identity=ident[:]),
                                  waits=[t_ld, psum_free.get(bank), tk_setup] if j == 0 else [], sig="pe" if j == 3 else None)
                    t_ev = P.op("act", lambda e, dq=dq, s=s, bank=bank: e.activation(out=hT[:, dq * 4:dq * 4 + 4, s * 128:(s + 1) * 128], in_=PS[bank][:].rearrange("p (a b) -> p a b", a=4), func=AF.Copy),
                                waits=[tk] + t_h_prev_readers, sig="act")
                    psum_free[bank] = t_ev
                    t_evs.append(t_ev)
                xst_free[slot] = tk
            t_h_prev_readers = []
            t_xn, t_stat = rms_to_bf16(hT[:], DT, gains[:, 0:DT], xnT, sq, PS[6], rtmp[:], rstd[:], [t_evs[-1], ffn.hid_free], D, war_dst=[t_xn_readers])
            psum_free[6] = t_stat
            t_res, t_pe_last = ffn(hT, xnT, hid, wbuf, wdbuf, sg, wbf["wg1"], wbf["wu1"], wbf["wd1"], conv_A, psum_free, list(rms_to_bf16.last_toks))
            own_i = tb - (NTB - 4)
            halo_i = tb - (NTB - 8)
            if own_i >= 0:
                t_st = P.dma("pool", h1scr.rearrange("(dt p) t -> p dt t", p=128)[:, :, own_i * TB:(own_i + 1) * TB], hT[:], waits=[t_res], sig="h1st")
                t_h_prev_readers.append(t_st)
            t_hn, t_stat = rms_to_bf16(hT[:], DT, gains[:, DT:2 * DT], xnT, sq, PS[6], rtmp[:], rstd[:], [t_res, ffn.hid_free], D, war_dst=[])
            psum_free[6] = t_stat
            t_h_prev_readers.append(t_hn)
            tiles = list(range(24, 32))
            if halo_i >= 0:
                tiles = list(range(8, 24)) + tiles
            if own_i >= 0:
                tiles = list(range(0, 8)) + tiles
            for ct in tiles:
                slot = W.n % 4
                W.n += 1
                t_ld = P.dma("sp", wbuf[:, slot, :], wbf["win"][ct], waits=[W.slot_free[slot], conv_tok["A"]], sig=f"w{slot}")
                bank = 4 + (ct % 2)
                if 16 <= ct < 24:
                    tk = None
                    for s in range(4):
                        for dt in range(DT):
                            first = (s == 0 and dt == 0)
                            last = (s == 3 and dt == DT - 1)
                            tk = P.op("pe", lambda e, s=s, dt=dt, slot=slot, bank=bank: e.matmul(PS[bank][:, s * 128:(s + 1) * 128], lhsT=xnT[:, dt, s * 128:(s + 1) * 128], rhs=wbuf[:, slot, dt * 128:(dt + 1) * 128], start=(dt == 0), stop=(dt == DT - 1)),
                                      waits=[t_ld, psum_free.get(bank), t_hn] if first else [], sig="pe" if last else None)
                    W.slot_free[slot] = tk
                    t_ev = P.op("act", lambda e, ct=ct, bank=bank: e.activation(out=vstg[:, :, 2 * (ct - 16):2 * (ct - 16) + 2, 0:64], in_=PS[bank][:].rearrange("p (a h c) -> p a h c", a=4, h=2), func=AF.Copy),
                                waits=[tk, vstg_free if ct == 16 else None], sig="act")
                    psum_free[bank] = t_ev
                    if ct == 23:
                        vstg_free = P.dma("pool", vscr[halo_i * TB:(halo_i + 1) * TB, :].rearrange("(s p) c -> p s c", p=128), vstg[:].rearrange("p a h c -> p a (h c)"), waits=[t_ev], sig="vst")
                else:
                    tk = None
                    for dt in range(DT):
                        tk = P.op("pe", lambda e, dt=dt, slot=slot, bank=bank: e.matmul(PS[bank][:], lhsT=wbuf[:, slot, dt * 128:(dt + 1) * 128], rhs=xnT[:, dt, :], start=(dt == 0), stop=(dt == DT - 1)),
                                  waits=[t_ld, psum_free.get(bank), t_hn] if dt == 0 else [], sig="pe" if dt == DT - 1 else None)
                    W.slot_free[slot] = tk
                    os_ = no % 2
                    no += 1
                    t_ev = P.op("act", lambda e, os_=os_, bank=bank: e.activation(out=ostg[:, os_, :], in_=PS[bank][:], func=AF.Copy), waits=[tk, ostg_free[os_]], sig="act")
                    psum_free[bank] = t_ev
                    if ct < 8:
                        dst = qscr[ct * 128:(ct + 1) * 128, own_i * TB:(own_i + 1) * TB]
                    elif ct < 16:
                        dst = kscr[(ct - 8) * 128:(ct - 7) * 128, halo_i * TB:(halo_i + 1) * TB]
                    else:
                        dst = uscr[(ct - 24) * 128:(ct - 23) * 128, tb * TB:(tb + 1) * TB]
                    ostg_free[os_] = P.dma("pool", dst, ostg[:, os_, :], waits=[t_ev], sig=f"os{os_}")
                t_xn_readers = tk
            if tb == tbs_p1[min(1, len(tbs_p1) - 1)]:
                issue_conv_B()
        issue_conv_B()
        phase_end = [t for t in (ostg_free + [vstg_free, t_xn_readers] + t_h_prev_readers) if t is not None]
        p1.close()

    if "h1" in dbg_outs:
        tk = P.dma("sp", dbg_outs["h1"], h1scr, waits=phase_end, sig="dbg")
        phase_end.append(tk)
    if "u" in dbg_outs:
        phase_end.append(P.dma("sp", dbg_outs["u"], uscr, waits=phase_end, sig="dbg"))
        phase_end.append(P.dma("sp", dbg_outs["q"], qscr, waits=phase_end, sig="dbg"))
        phase_end.append(P.dma("sp", dbg_outs["k"], kscr, waits=phase_end, sig="dbg"))
        phase_end.append(P.dma("sp", dbg_outs["v"], vscr, waits=phase_end, sig="dbg"))


    if dbg.get("u_in"):
        u_in = din("u_in", [1024, SEQ], BF16)
        P.dma("sp", uscr, u_in, sig="dbg")

    if 2 in do:
        def _p2():
            P.barrier()
            p2 = ExitStack()

            def sb2(name, shape, dt):
                return p2.enter_context(nc.sbuf_tensor("s2_" + name, list(shape), dt))
            sc = sb2("sc", [128, 3 * NST], F32)
            rbig = sb2("rbig", [128, 4 * 2 * TB], F32)
            bl = rbig[:, 0:1024].rearrange("p (a s h) -> p a s h", a=2, h=16)
            cl = rbig[:, 1024:2048].rearrange("p (a s h) -> p a s h", a=2, h=16)
            dd = sb2("dd", [128, 8], F32)
            bb = rbig[:, 2048:3072].rearrange("p (a s h) -> p a s h", a=2, h=16)
            Zf = sb2("Zf", [128, NST, 2, 128], F32)
            Bt = sb2("Bt", [128, NST, 2, 128], BF16)
            Ct = sb2("Ct", [128, NST, 3, 128], BF16)
            Tre = sb2("Tre", [128, NST, L], F32)
            Tim = sb2("Tim", [128, NST, L], F32)
            sv = sb2("sv", [128, 24, NST], F32)
            ki = sb2("ki", [128, NST], I32)
            uT = sb2("uT", [128, 2, 8, TB], BF16)
            wre = sb2("wre", [128, 2, TB], F32)
            wim = sb2("wim", [128, 2, TB], F32)
            yv = sb2("yv", [128, TB], F32)
            y2 = sb2("y2", [128, TB], F32)
            ysg = sb2("ysg", [128, TB], F32)
            yst = sb2("yst", [128, 2, TB], BF16)
            ini = sb2("ini", [128, 2, NST], F32)
            rt = sb2("rt", [128, 4, 2], F32)

            V = lambda i: sv[:, i, :]
            LDT, LR, LI, DTv, MAG, TH, TQ, KF, THR, SIN, COS, MSK, THC, LBR, LBI, DEN, NR, FRE, FIM, ERE, EIM, TA, TB_, E7R = range(24)
            E7I = TQ
            t0 = P.dma("sp", sc[:], ssm_sc_d, sig="cst")
            P.dma("sp", rbig[:, 0:1024], ssm_b_d, sig="cst")
            P.dma("sp", rbig[:, 1024:2048], ssm_c_d, sig="cst")
            t_in = P.dma("sp", dd[:], ssm_dd_d, sig="cst")
            lr, li, ldt = sc[:, NST:2 * NST], sc[:, 2 * NST:3 * NST], sc[:, 0:NST]

            chain = {"on": True, "last": None}

            def dv(fn, waits=(), sig="dve"):
                ws = list(waits)
                if chain["on"]:
                    ws.append(chain["last"])
                    sig = "dve"
                tok = P.op("dve", fn, waits=ws, sig=sig)
                if chain["on"]:
                    chain["last"] = tok
                return tok

            def ac(fn, waits=(), sig="act"):
                return P.op("act", fn, waits=waits, sig=sig)
            t = ac(lambda e: e.activation(out=V(DTv), in_=ldt, func=AF.Exp), waits=[t_in])
            t = dv(lambda e: e.tensor_tensor(out=V(TA), in0=lr, in1=V(DTv), op=ALU.mult), waits=[t])
            t_mag = ac(lambda e: e.activation(out=V(MAG), in_=V(TA), func=AF.Exp), waits=[t])
            dv(lambda e: e.tensor_tensor(out=V(TH), in0=li, in1=V(DTv), op=ALU.mult))
            dv(lambda e: e.tensor_scalar(out=V(TQ), in0=V(TH), scalar1=1.0 / (2 * math.pi), scalar2=None, op0=ALU.mult))
            dv(lambda e: e.tensor_copy(out=ki[:], in_=V(TQ)))
            dv(lambda e: e.tensor_copy(out=V(KF), in_=ki[:]))
            dv(lambda e: e.scalar_tensor_tensor(out=V(THR), in0=V(KF), scalar=-6.28125, in1=V(TH), op0=ALU.mult, op1=ALU.add))
            dv(lambda e: e.scalar_tensor_tensor(out=V(THR), in0=V(KF), scalar=-(2 * math.pi - 6.28125), in1=V(THR), op0=ALU.mult, op1=ALU.add))
            dv(lambda e: e.tensor_scalar(out=V(THR), in0=V(THR), scalar1=3.1415925, scalar2=-3.1415925, op0=ALU.min, op1=ALU.max))
            dv(lambda e: e.tensor_scalar(out=V(MSK), in0=V(THR), scalar1=math.pi / 2, scalar2=None, op0=ALU.is_gt))
            dv(lambda e: e.scalar_tensor_tensor(out=V(THC), in0=V(MSK), scalar=-2 * math.pi, in1=V(THR), op0=ALU.mult, op1=ALU.add))
            t = dv(lambda e: e.tensor_scalar(out=V(THC), in0=V(THC), scalar1=math.pi / 2, scalar2=3.1415925, op0=ALU.add, op1=ALU.min))
            ac(lambda e: e.activation(out=V(SIN), in_=V(THR), func=AF.Sin), waits=[t])
            t = ac(lambda e: e.activation(out=V(COS), in_=V(THC), func=AF.Sin))
            dv(lambda e: e.tensor_tensor(out=V(LBR), in0=V(MAG), in1=V(COS), op=ALU.mult), waits=[t, t_mag])
            dv(lambda e: e.tensor_tensor(out=V(LBI), in0=V(MAG), in1=V(SIN), op=ALU.mult))
            dv(lambda e: e.tensor_tensor(out=V(DEN), in0=lr, in1=lr, op=ALU.mult))
            dv(lambda e: e.tensor_tensor(out=V(TA), in0=li, in1=li, op=ALU.mult))
            dv(lambda e: e.tensor_tensor(out=V(DEN), in0=V(DEN), in1=V(TA), op=ALU.add))
            dv(lambda e: e.reciprocal(out=V(DEN), in_=V(DEN)))
            dv(lambda e: e.tensor_scalar(out=V(NR), in0=V(LBR), scalar1=-1.0, scalar2=None, op0=ALU.add))
            dv(lambda e: e.tensor_tensor(out=V(TA), in0=V(NR), in1=lr, op=ALU.mult))
            dv(lambda e: e.tensor_tensor(out=V(TB_), in0=V(LBI), in1=li, op=ALU.mult))
            dv(lambda e: e.tensor_tensor(out=V(TA), in0=V(TA), in1=V(TB_), op=ALU.add))
            dv(lambda e: e.tensor_tensor(out=V(FRE), in0=V(TA), in1=V(DEN), op=ALU.mult))
            dv(lambda e: e.tensor_tensor(out=V(TA), in0=V(LBI), in1=lr, op=ALU.mult))
            dv(lambda e: e.tensor_tensor(out=V(TB_), in0=V(NR), in1=li, op=ALU.mult))
            dv(lambda e: e.tensor_tensor(out=V(TA), in0=V(TA), in1=V(TB_), op=ALU.subtract))
            dv(lambda e: e.tensor_tensor(out=V(FIM), in0=V(TA), in1=V(DEN), op=ALU.mult))
            fre_b = V(FRE).unsqueeze(2).to_broadcast([128, NST, 16])
            fim_b = V(FIM).unsqueeze(2).to_broadcast([128, NST, 16])
            tmp16a = Zf[:, 0:4, :, :].rearrange("p a b c -> p (a b c)")[:, 0:NST * 16].rearrange("p (s h) -> p s h", h=16)
            tmp16b = Zf[:, 4:8, :, :].rearrange("p a b c -> p (a b c)")[:, 0:NST * 16].rearrange("p (s h) -> p s h", h=16)
            dv(lambda e: e.tensor_tensor(out=tmp16a, in0=bl[:, 0], in1=fre_b, op=ALU.mult))
            dv(lambda e: e.tensor_tensor(out=tmp16b, in0=bl[:, 1], in1=fim_b, op=ALU.mult))
            dv(lambda e: e.tensor_tensor(out=bb[:, 0], in0=tmp16a, in1=tmp16b, op=ALU.subtract))
            dv(lambda e: e.tensor_tensor(out=tmp16a, in0=bl[:, 1], in1=fre_b, op=ALU.mult))
            dv(lambda e: e.tensor_tensor(out=tmp16b, in0=bl[:, 0], in1=fim_b, op=ALU.mult))
            dv(lambda e: e.tensor_tensor(out=bb[:, 1], in0=tmp16a, in1=tmp16b, op=ALU.add))
            Zflat = Zf[:].rearrange("p a b c -> p (a b c)")
            tmp1 = Zflat[:, 0:NST * (L // 2)].rearrange("p (s n) -> p s n", n=L // 2)
            dv(lambda e: e.memset(Tre[:, :, 0:1], 1.0))
            dv(lambda e: e.memset(Tim[:, :, 0:1], 0.0))
            dv(lambda e: e.tensor_copy(out=V(ERE), in_=V(COS)))
            dv(lambda e: e.tensor_copy(out=V(EIM), in_=V(SIN)))
            k = 0
            while (1 << k) < L:
                n = 1 << k
                k += 1
                er = V(ERE).unsqueeze(2).to_broadcast([128, NST, n])
                ei = V(EIM).unsqueeze(2).to_broadcast([128, NST, n])
                tt_ = tmp1[:, :, 0:n]
                dv(lambda e, n=n, er=er: e.tensor_tensor(out=Tre[:, :, n:2 * n], in0=Tre[:, :, 0:n], in1=er, op=ALU.mult))
                dv(lambda e, n=n, ei=ei, tt_=tt_: e.tensor_tensor(out=tt_, in0=Tim[:, :, 0:n], in1=ei, op=ALU.mult))
                dv(lambda e, n=n, tt_=tt_: e.tensor_tensor(out=Tre[:, :, n:2 * n], in0=Tre[:, :, n:2 * n], in1=tt_, op=ALU.subtract))
                dv(lambda e, n=n, ei=ei: e.tensor_tensor(out=Tim[:, :, n:2 * n], in0=Tre[:, :, 0:n], in1=ei, op=ALU.mult))
                dv(lambda e, n=n, er=er, tt_=tt_: e.tensor_tensor(out=tt_, in0=Tim[:, :, 0:n], in1=er, op=ALU.mult))
                dv(lambda e, n=n, tt_=tt_: e.tensor_tensor(out=Tim[:, :, n:2 * n], in0=Tim[:, :, n:2 * n], in1=tt_, op=ALU.add))
                dv(lambda e: e.tensor_tensor(out=V(TA), in0=V(ERE), in1=V(ERE), op=ALU.mult))
                dv(lambda e: e.tensor_tensor(out=V(TB_), in0=V(EIM), in1=V(EIM), op=ALU.mult))
                dv(lambda e: e.tensor_tensor(out=V(EIM), in0=V(ERE), in1=V(EIM), op=ALU.mult))
                dv(lambda e: e.tensor_scalar(out=V(EIM), in0=V(EIM), scalar1=2.0, scalar2=None, op0=ALU.mult))
                dv(lambda e: e.tensor_tensor(out=V(ERE), in0=V(TA), in1=V(TB_), op=ALU.subtract))
            dv(lambda e: e.tensor_copy(out=V(E7R), in_=V(ERE)))
            t_tab = dv(lambda e: e.tensor_copy(out=V(E7I), in_=V(EIM)))
            dv(lambda e: e.memset(Zflat, 0.0))
            dv(lambda e: e.memset(Ct[:].rearrange("p a b c -> p (a b c)"), 0.0))
            dv(lambda e: e.memset(ini[:].rearrange("p a s -> p (a s)"), 0.0))
            t_z = None
            for j in range(4):
                for gg in range(2):
                    ps_ = slice(gg * 64, (gg + 1) * 64)
                    cs_ = slice(32 * j + 16 * gg, 32 * j + 16 * gg + 16)
                    for ri in range(2):
                        dv(lambda e, j=j, ps_=ps_, cs_=cs_, ri=ri: e.tensor_copy(out=Zf[ps_, j::4, ri, cs_], in_=bb[ps_, ri, j::4, :]))
                        if ri == 0:
                            dv(lambda e, j=j, ps_=ps_, cs_=cs_: e.tensor_scalar(out=Ct[ps_, j::4, 2, cs_], in0=cl[ps_, 0, j::4, :], scalar1=-1.0, scalar2=None, op0=ALU.mult))
                            t_z = dv(lambda e, j=j, ps_=ps_, cs_=cs_: e.tensor_copy(out=Ct[ps_, j::4, 0, cs_], in_=cl[ps_, 0, j::4, :]))
                        else:
                            t_z = dv(lambda e, j=j, ps_=ps_, cs_=cs_: e.tensor_scalar(out=Ct[ps_, j::4, 1, cs_], in0=cl[ps_, 1, j::4, :], scalar1=-1.0, scalar2=None, op0=ALU.mult))
            t_bt = None
            pfree = {}
            for g4 in range(16):
                bank = 6 + g4 % 2
                tk = None
                for q4 in range(4):
                    st_, ri = divmod(g4 * 4 + q4, 2)
                    tk = P.op("pe", lambda e, st_=st_, ri=ri, q4=q4, bank=bank: e.transpose(out=PS[bank][:, q4 * 128:(q4 + 1) * 128], in_=Zf[:, st_, ri, :], identity=ident[:]),
                              waits=[t_z, pfree.get(bank)] if q4 == 0 else [], sig="pe" if q4 == 3 else None)
                st0_ = (g4 * 4) // 2
                t_bt = ac(lambda e, st0_=st0_, bank=bank: e.activation(out=Bt[:, st0_:st0_ + 2, :, :].rearrange("p a b c -> p (a b) c"), in_=PS[bank][:].rearrange("p (a c) -> p a c", a=4), func=AF.Copy), waits=[tk])
                pfree[bank] = t_bt

            chain["on"] = False
            P.barrier()
            Zfl = Zf[:].rearrange("p a b c -> p (a b c)")
            mm_ = [Zfl[:, i * 2 * TB:(i + 1) * 2 * TB].rearrange("p (j t) -> p j t", j=2) for i in range(4)]
            Zb = Zfl[:, 4 * 2 * TB:8 * 2 * TB].bitcast(BF16)
            dmb = [[Zb[:, (sl_ * 4 + i) * 2 * TB:(sl_ * 4 + i + 1) * 2 * TB].rearrange("p (j t) -> p j t", j=2) for i in range(4)] for sl_ in range(2)]
            rre2 = [rbig[:, (2 * k) * 2 * TB:(2 * k + 1) * 2 * TB].rearrange("p (j t) -> p j t", j=2) for k in range(2)]
            rim2 = [rbig[:, (2 * k + 1) * 2 * TB:(2 * k + 2) * 2 * TB].rearrange("p (j t) -> p j t", j=2) for k in range(2)]
            uscr_t = uscr.rearrange("(ct p) t -> p ct t", p=128)
            wre2 = [wre, sb2("wre_b", [128, 2, TB], F32)]
            wim2 = [wim, sb2("wim_b", [128, 2, TB], F32)]
            NP = NTB * 16
            S = dict(tk_y={}, dm_done={}, bu_free=None, t_u={}, u_free=[None, None], ybank_free={}, s_free=[None, None], dm_free=None,
                     yst_free=[None, None], yv_free=None, ny=0, t_rot=t_tab, mm_free=None, tk_bu={}, t_wj={}, t_m={},
                     chain_end={}, last_bu_tb={}, t_y_tb={})

            def load_u(tb):
                us = tb % 2
                S["t_u"][tb] = P.dma("sp", uT[:, us], uscr_t[:, :, tb * TB:(tb + 1) * TB], waits=[S["u_free"][us]], sig=f"u{us}")

            def do_bu(g):
                tb, pp = divmod(g, 16)
                us, ct = tb % 2, pp // 2
                tk = None
                for jj in range(2):
                    st_ = 2 * pp + jj
                    for ri in range(2):
                        tk = P.op("pe", lambda e, st_=st_, ri=ri, jj=jj, us=us, ct=ct: e.matmul(PS[2 * jj + ri][:], lhsT=Bt[:, st_, ri, :], rhs=uT[:, us, ct, :], start=True, stop=True),
                                  waits=[S["t_u"][tb], t_bt, S["bu_free"]] if (jj == 0 and ri == 0) else [], sig="pe" if (jj == 1 and ri == 1) else None)
                S["tk_bu"][g] = tk
                S["last_bu_tb"][tb] = tk

            def mod_piece(g, jj, half):
                tb, pp = divmod(g, 16)
                st_ = 2 * pp + jj
                wb = g % 2
                trb = Tre[:, st_, :].unsqueeze(1).to_broadcast([128, SUBS, L])
                tib = Tim[:, st_, :].unsqueeze(1).to_broadcast([128, SUBS, L])
                pre = PS[2 * jj][:].rearrange("p (a b) -> p a b", a=SUBS)
                pim = PS[2 * jj + 1][:].rearrange("p (a b) -> p a b", a=SUBS)
                combos = ((pre, trb), (pim, tib), (pim, trb), (pre, tib))
                t_m = None
                for i in (2 * half, 2 * half + 1):
                    src, tab = combos[i]
                    first = (jj == 0 and i == 0)
                    t_m = dv(lambda e, i=i, jj=jj, src=src, tab=tab: e.tensor_tensor(out=mm_[i][:, jj, :].rearrange("p (a b) -> p a b", a=SUBS), in0=src, in1=tab, op=ALU.mult),
                             waits=[S["tk_bu"][g], t_tab, S["mm_free"]] if first else [], sig="dve" if i == 3 else None)
                if half == 1:
                    dv(lambda e, jj=jj, wb=wb: e.tensor_tensor(out=wre2[wb][:, jj, :], in0=mm_[0][:, jj, :], in1=mm_[1][:, jj, :], op=ALU.add), waits=[t_m], sig=None)
                    S["t_wj"][(g, jj)] = dv(lambda e, jj=jj, wb=wb: e.tensor_tensor(out=wim2[wb][:, jj, :], in0=mm_[2][:, jj, :], in1=mm_[3][:, jj, :], op=ALU.subtract))
                    if jj == 1:
                        S["bu_free"] = t_m
                        S["mm_free"] = S["t_wj"][(g, 1)]

            def run_chain(g, pieces):
                tb, pp = divmod(g, 16)
                wb = g % 2
                rre, rim = rre2[wb], rim2[wb]
                pieces = list(pieces)
                for sub in range(SUBS):
                    sl = slice(sub * L, (sub + 1) * L)
                    t_sc = None
                    for jj in range(2):
                        st_ = 2 * pp + jj
                        magb = V(MAG)[:, st_:st_ + 1].to_broadcast([128, L])
                        dv(lambda e, jj=jj, st_=st_, sl=sl, magb=magb, wb=wb: e.tensor_tensor_scan(out=rre[:, jj, sl], data0=magb, data1=wre2[wb][:, jj, sl], initial=ini[:, 0, st_:st_ + 1], op0=ALU.mult, op1=ALU.add),
                           waits=[S["t_wj"][(g, jj)], S["t_rot"], S["dm_done"].get(g - 2)], sig=None)
                        t_sc = dv(lambda e, jj=jj, st_=st_, sl=sl, magb=magb, wb=wb: e.tensor_tensor_scan(out=rim[:, jj, sl], data0=magb, data1=wim2[wb][:, jj, sl], initial=ini[:, 1, st_:st_ + 1], op0=ALU.mult, op1=ALU.add))
                    if pieces:
                        pieces.pop(0)()
                    last = sub * L + L - 1
                    fr = rre[:, :, last]
                    fi = rim[:, :, last]
                    e7r = V(E7R)[:, 2 * pp:2 * pp + 2]
                    e7i = V(E7I)[:, 2 * pp:2 * pp + 2]
                    dv(lambda e, fr=fr, e7r=e7r: e.tensor_tensor(out=rt[:, 0, :], in0=fr, in1=e7r, op=ALU.mult), waits=[t_sc], sig=None)
                    dv(lambda e, fi=fi, e7i=e7i: e.tensor_tensor(out=rt[:, 1, :], in0=fi, in1=e7i, op=ALU.mult), sig=None)
                    dv(lambda e, fr=fr, e7i=e7i: e.tensor_tensor(out=rt[:, 2, :], in0=fr, in1=e7i, op=ALU.mult), sig=None)
                    t_rt = dv(lambda e, fi=fi, e7r=e7r: e.tensor_tensor(out=rt[:, 3, :], in0=fi, in1=e7r, op=ALU.mult))
                    if pieces:
                        pieces.pop(0)()
                    dv(lambda e, pp=pp: e.tensor_tensor(out=ini[:, 0, 2 * pp:2 * pp + 2], in0=rt[:, 0, :], in1=rt[:, 1, :], op=ALU.subtract), waits=[t_rt], sig=None)
                    S["t_rot"] = dv(lambda e, pp=pp: e.tensor_tensor(out=ini[:, 1, 2 * pp:2 * pp + 2], in0=rt[:, 2, :], in1=rt[:, 3, :], op=ALU.add))
                for pc in pieces:
                    pc()
                S["chain_end"][g] = S["t_rot"]

            def own_part(g):
                tb, pp = divmod(g, 16)
                us, ct = tb % 2, pp // 2
                own_i = tb - (NTB - 4)
                ss = pp % 2
                wb = g % 2
                rre, rim = rre2[wb], rim2[wb]
                t_d = None
                for jj in range(2):
                    st_ = 2 * pp + jj
                    trb = Tre[:, st_, :].unsqueeze(1).to_broadcast([128, SUBS, L])
                    tib = Tim[:, st_, :].unsqueeze(1).to_broadcast([128, SUBS, L])
                    rr = rre[:, jj, :].rearrange("p (a b) -> p a b", a=SUBS)
                    ri_ = rim[:, jj, :].rearrange("p (a b) -> p a b", a=SUBS)
                    for i, (src, tab) in enumerate(((rr, trb), (ri_, tib), (rr, tib), (ri_, trb))):
                        t_d = P.op("pool", lambda e, i=i, jj=jj, src=src, tab=tab, ss=ss: e.tensor_tensor(out=dmb[ss][i][:, jj, :].rearrange("p (a b) -> p a b", a=SUBS), in0=src, in1=tab, op=ALU.mult),
                                   waits=[S["t_rot"], S["s_free"][ss]] if (jj == 0 and i == 0) else [], sig="pool")
                S["dm_done"][g] = t_d
                yb = 4 + ct % 2
                tk_y = None
                csel = (0, 2, 1, 1)
                for jj in range(2):
                    st_ = 2 * pp + jj
                    for i in range(4):
                        first = (ss == 0 and jj == 0 and i == 0)
                        lastm = (ss == 1 and jj == 1 and i == 3)
                        tk_y = P.op("pe", lambda e, st_=st_, i=i, jj=jj, ss=ss, yb=yb, first=first, lastm=lastm: e.matmul(PS[yb][:], lhsT=Ct[:, st_, csel[i], :], rhs=dmb[ss][i][:, jj, :], start=first, stop=lastm),
                                    waits=[t_d, t_z, S["ybank_free"].get(yb)] if (jj == 0 and i == 0) else [], sig="pe" if (jj == 1 and i == 3) else None)
                S["s_free"][ss] = tk_y
                S["tk_y"][g] = tk_y

            def own_b(g):
                tb, pp = divmod(g, 16)
                us, ct = tb % 2, pp // 2
                own_i = tb - (NTB - 4)
                ss = pp % 2
                yb = 4 + ct % 2
                tk_y = S["tk_y"][g]
                if ss == 1:
                    t_y = dv(lambda e, us=us, ct=ct, yb=yb: e.scalar_tensor_tensor(out=yv[:], in0=uT[:, us, ct, :], scalar=dd[:, ct:ct + 1], in1=PS[yb][:], op0=ALU.mult, op1=ALU.add), waits=[tk_y, S["yv_free"]])
                    S["ybank_free"][yb] = t_y
                    S["t_y_tb"][tb] = t_y
                    t_i = P.op("pool", lambda e: e.tensor_tensor(out=y2[:], in0=yv[:], in1=yv[:], op=ALU.mult), waits=[t_y], sig="pool")
                    t_i = P.op("pool", lambda e: e.tensor_scalar(out=y2[:], in0=y2[:], scalar1=0.044715, scalar2=1.0, op0=ALU.mult, op1=ALU.add), waits=[t_i], sig="pool")
                    t_i = P.op("pool", lambda e: e.tensor_tensor(out=y2[:], in0=y2[:], in1=yv[:], op=ALU.mult), waits=[t_i], sig="pool")
                    t_g = ac(lambda e: e.activation(out=ysg[:], in_=y2[:], func=AF.Sigmoid, scale=2.0 * 0.7978845608028654), waits=[t_i])
                    ys = S["ny"] % 2
                    S["ny"] += 1
                    t_o = P.op("pool", lambda e, ys=ys: e.tensor_tensor(out=yst[:, ys, :], in0=yv[:], in1=ysg[:], op=ALU.mult), waits=[t_g, S["yst_free"][ys]], sig="pool")
                    S["yv_free"] = t_o
                    S["yst_free"][ys] = P.dma("act", yscr[ct * 128:(ct + 1) * 128, own_i * TB:(own_i + 1) * TB], yst[:, ys, :], waits=[t_o], sig=f"ys{ys}")

            load_u(0)
            do_bu(0)
            for jj in range(2):
                for half in range(2):
                    mod_piece(0, jj, half)
            do_bu(1)
            for g in range(NP):
                tb, pp = divmod(g, 16)
                if pp == 1 and tb + 1 < NTB:
                    rd = [S["last_bu_tb"].get(tb - 1), S["t_y_tb"].get(tb - 1)]
                    rd = [r for r in rd if r is not None]
                    S["u_free"][(tb + 1) % 2] = rd[-1] if rd else None
                    if len(rd) == 2:
                        P.op("sp", None, waits=[rd[0]])
                    load_u(tb + 1)
                pieces = []
                if g + 1 < NP:
                    pieces = [(lambda jj=jj, half=half, g=g: mod_piece(g + 1, jj, half)) for jj in range(2) for half in range(2)]
                run_chain(g, pieces)
                if g + 2 < NP:
                    do_bu(g + 2)
                if g >= 1 and (g - 1) // 16 >= NTB - 4:
                    own_b(g - 1)
                if tb >= NTB - 4:
                    own_part(g)
            own_b(NP - 1)
            if "sv" in dbg_outs:
                P.barrier()
                P.dma("sp", dbg_outs["sv"], sv[:].rearrange("p a s -> p (a s)"), sig="dbg")
                P.dma("sp", dbg_outs["Tre"], Tre[:].rearrange("p a s -> p (a s)"), sig="dbg")
                P.dma("sp", dbg_outs["Tim"], Tim[:].rearrange("p a s -> p (a s)"), sig="dbg")
                P.dma("sp", dbg_outs["bbo"], rbig[:, 2048:3072], sig="dbg")
                P.dma("sp", dbg_outs["inio"], ini[:].rearrange("p a s -> p (a s)"), sig="dbg")
                P.barrier()
            p2.close()
            if "yssm" in dbg_outs:
                P.barrier()
                P.dma("sp", dbg_outs["yssm"], yscr, sig="dbg")

        _p2()

    if dbg.get("qkv_in"):
        P.dma("sp", qscr, din("q_in", [1024, OWN], BF16), sig="dbg")
        P.dma("sp", kscr, din("k_in", [1024, 2 * OWN], BF16), sig="dbg")
        P.dma("sp", vscr, din("v_in", [2 * OWN, 2048], BF16), sig="dbg")

    if 3 in do:
        def _p3():
            P.barrier()
            p3 = ExitStack()

            def sb3(name, shape, dt):
                return p3.enter_context(nc.sbuf_tensor("s3_" + name, list(shape), dt))
            mtmp = sb3("mtmp", [128, 2, 128], F32)
            qbd = sb3("qbd", [128, 2, 2, OWN], BF16)
            kT = sb3("kT", [128, 2, 2 * OWN], BF16)
            acc = sb3("acc", [128, 4, OWN], F32)
            dtmp = sb3("dtmp", [64, 4, OWN], F32)
            NVS = 12
            vch = sb3("vch", [128, NVS, 512], BF16)
            pT = sb3("pT", [128, 3, 512], BF16)
            pE = sb3("pE", [128, 3, 512], BF16)
            M01 = sb3("M01", [128, 2, 512], BF16)
            hval = sb3("hval", [128, 1], F32)
            yat = sb3("yat", [64, 4, OWN], BF16)

            t0 = P.dma("sp", mtmp[:, 0, :], maskc_d, sig="cst")
            t0 = P.dma("sp", mtmp[:, 1, :], maskp_d, sig="cst")
            t_hv = P.op("dve", lambda e: e.tensor_scalar(out=hval[:], in0=hbias[:, 0:1], scalar1=0.0, scalar2=None, op0=ALU.is_equal), waits=[t0, tk_cst], sig="dve")
            for v_ in range(2):
                for c_ in range(2):
                    P.op("dve", lambda e, v_=v_, c_=c_: e.tensor_scalar(out=M01[:, v_, c_ * 128:(c_ + 1) * 128], in0=mtmp[:, 1, :], scalar1=0.0, scalar2=None, op0=ALU.is_equal), sig="dve")
                    P.op("dve", lambda e, v_=v_, c_=c_: e.tensor_scalar(out=M01[:, v_, 256 + c_ * 128:256 + (c_ + 1) * 128], in0=mtmp[:, 0, :], scalar1=0.0, scalar2=None, op0=ALU.is_equal), sig="dve")
            t_mask = P.op("dve", lambda e: e.tensor_scalar(out=M01[:, 1, 0:256], in0=M01[:, 1, 0:256], scalar1=hval[:, 0:1], scalar2=None, op0=ALU.mult), waits=[t_hv], sig="dve")
            t_qz = P.op("dve", lambda e: e.memset(qbd[:].rearrange("p a b c -> p (a b c)"), 0.0), sig="dve")

            v_free = [None] * NVS
            nv = 0
            ps_s_free = [None, None, None]
            pT_free = [None, None, None]
            pE_free = [None, None, None]
            ps_od_free = [None, None]
            qk_free = t_qz
            acc_free = None
            yat_free = None
            dtmp_free = None
            for hq in range(4):
                t_q = None
                for t in range(2):
                    r0 = (2 * hq + t) * 128
                    P.dma("sp", qbd[0:64, t, 0, :], qscr[r0:r0 + 64, :], waits=[qk_free], sig="qk")
                    P.dma("sp", qbd[64:128, t, 1, :], qscr[r0 + 64:r0 + 128, :], waits=[qk_free], sig="qk")
                    t_q = P.dma("sp", kT[:, t, :], kscr[r0:r0 + 128, :], waits=[qk_free], sig="qk")
                t_z = P.op("dve", lambda e: e.memset(acc[:].rearrange("p a c -> p (a c)"), 0.0), waits=[acc_free], sig="dve")
                chunks = []
                blocks = []
                for d in (1, 4, 16):
                    nb = OWN // (128 * d)
                    for r in range(d):
                        for n in range(-1, nb):
                            chunks.append((OWN + 128 * n * d + r, d))
                            if n >= 0:
                                blocks.append((d, r, n, len(chunks) - 2, len(chunks) - 1))
                chunk_tok = {}
                nextc = {"c": 0}

                def ensure(upto):
                    while nextc["c"] <= min(upto, len(chunks) - 1):
                        c = nextc["c"]
                        a0, d_ = chunks[c]
                        slot = (nv0 + c) % NVS
                        chunk_tok[c] = P.dma("sp", vch[:, slot, :], vscr[a0:a0 + 127 * d_ + 1:d_, hq * 512:(hq + 1) * 512], waits=[v_free[slot]], sig=f"v{slot}")
                        nextc["c"] += 1
                nv0 = nv
                items = []
                for bj, (d, r, n, pc, cc) in enumerate(blocks):
                    for t in range(2):
                        items.append(dict(d=d, r=r, n=n, t=t, bj=bj, pc=pc, cc=cc))
                nv += len(chunks)
                state = {}

                def emit_S(i, it):
                    d, r, n, t = it["d"], it["r"], it["n"], it["t"]
                    b = i % 3
                    o0 = 128 * n * d + r
                    qap = qbd[:, t, :, o0:o0 + 127 * d + 1:d]
                    kcur = kT[:, t, OWN + o0:OWN + o0 + 127 * d + 1:d]
                    kprev = kT[:, t, OWN + o0 - 128 * d:OWN + o0 - d + 1:d]
                    mv = 1 if n == 0 else 0
                    P.op("pe", lambda e, b=b, kprev=kprev, qap=qap: e.matmul(PS[b][:, 0:256].rearrange("p (a c) -> p a c", a=2), lhsT=kprev, rhs=qap, start=True, stop=True), waits=[ps_s_free[b], t_q, tk_setup])
                    tk = P.op("pe", lambda e, b=b, kcur=kcur, qap=qap: e.matmul(PS[b][:, 256:512].rearrange("p (a c) -> p a c", a=2), lhsT=kcur, rhs=qap, start=True, stop=True), sig="pe")
                    t_e = P.op("act", lambda e, b=b: e.activation(out=pE[:, b, :], in_=PS[b][:], func=AF.Exp, scale=0.125), waits=[tk, pE_free[b]], sig="act")
                    ps_s_free[b] = t_e
                    t_p = P.op("pool", lambda e, b=b, mv=mv: e.tensor_tensor(out=pT[:, b, :], in0=pE[:, b, :], in1=M01[:, mv, :], op=ALU.mult), waits=[t_e, pT_free[b], t_mask], sig="pool")
                    pE_free[b] = t_p
                    it["t_e"] = t_p

                def emit_PV(i, it):
                    d, r, n, t = it["d"], it["r"], it["n"], it["t"]
                    b = i % 3
                    ob = 4 + i % 2
                    o0 = 128 * n * d + r
                    sp_, sc_ = (nv0 + it["pc"]) % NVS, (nv0 + it["cc"]) % NVS
                    tk = None
                    for ab in range(2):
                        hh = 2 * t + ab
                        P.op("pe", lambda e, b=b, ob=ob, sp_=sp_, hh=hh, ab=ab: e.matmul(PS[ob][:, ab * 128:(ab + 1) * 128], lhsT=vch[:, sp_, hh * 128:(hh + 1) * 128], rhs=pT[:, b, ab * 128:(ab + 1) * 128], start=True, stop=False),
                             waits=[it["t_e"], chunk_tok[it["pc"]], chunk_tok[it["cc"]], ps_od_free[i % 2]] if ab == 0 else [])
                        tk = P.op("pe", lambda e, b=b, ob=ob, sc_=sc_, hh=hh, ab=ab: e.matmul(PS[ob][:, ab * 128:(ab + 1) * 128], lhsT=vch[:, sc_, hh * 128:(hh + 1) * 128], rhs=pT[:, b, 256 + ab * 128:256 + (ab + 1) * 128], start=False, stop=True),
                                  sig="pe" if ab == 1 else None)
                    pT_free[b] = tk
                    if t == 1:
                        v_free[sp_] = tk
                        v_free[sc_] = tk
                    dst = acc[:, 2 * t:2 * t + 2, o0:o0 + 127 * d + 1:d]
                    t_a = P.op("dve", lambda e, ob=ob, dst=dst: e.tensor_tensor(out=dst, in0=dst, in1=PS[ob][:, 0:256].rearrange("p (a c) -> p a c", a=2), op=ALU.add), waits=[tk, t_z], sig="dve")
                    ps_od_free[i % 2] = t_a
                    state["t_a"] = t_a
                    state["last_pe"] = tk

                for i, it in enumerate(items):
                    if it["t"] == 0:
                        ensure(blocks[min(it["bj"] + 2, len(blocks) - 1)][4])
                    emit_S(i, it)
                    if i >= 2:
                        emit_PV(i - 2, items[i - 2])
                emit_PV(len(items) - 2, items[-2])
                emit_PV(len(items) - 1, items[-1])
                qk_free = state["last_pe"]
                t_dm = P.dma("sp", dtmp[:], acc[64:128, :, :], waits=[state["t_a"], dtmp_free], sig="dtmp")
                P.op("act", lambda e: e.activation(out=dtmp[:], in_=dtmp[:], func=AF.Ln), waits=[t_dm], sig="act")
                t_r = P.op("act", lambda e: e.activation(out=dtmp[:], in_=dtmp[:], func=AF.Exp, scale=-1.0), sig="act")
                t_y = P.op("dve", lambda e: e.tensor_tensor(out=yat[:], in0=acc[0:64, :, :], in1=dtmp[:], op=ALU.mult), waits=[t_r, yat_free], sig="dve")
                acc_free = t_y
                dtmp_free = t_y
                yat_free = P.dma("sp", ascr[hq * 256:(hq + 1) * 256, :].rearrange("(hh e) t -> e hh t", e=64), yat[:], waits=[t_y], sig="yat")
            p3.close()
            if "yatt" in dbg_outs:
                P.barrier()
                P.dma("sp", dbg_outs["yatt"], ascr, sig="dbg")

        _p3()

    if dbg.get("p4_in"):
        P.dma("sp", h1scr, din("h1_in", [D, OWN], F32), sig="dbg")
        P.dma("sp", yscr, din("ys_in", [1024, OWN], BF16), sig="dbg")
        P.dma("sp", ascr, din("ya_in", [1024, OWN], BF16), sig="dbg")

    if 4 in do:
        def _p4():
            P.barrier()
            p4 = ExitStack()

            def sb4(name, shape, dt):
                return p4.enter_context(nc.sbuf_tensor("s4_" + name, list(shape), dt))
            hT = sb4("hT", [128, DT, TB], F32)
            xnT = sb4("xnT", [128, DT, TB], BF16)
            hid = sb4("hid", [128, FT, TB], BF16)
            wbuf = sb4("wbuf", [128, 4, D], BF16)
            wdbuf = sb4("wdbuf", [128, 2, DFF], BF16)
            sg = sb4("sg", [128, 2, TB], F32)
            rtmp = sb4("rtmp", [128, TB], F32)
            rstd = sb4("rstd", [128, TB], F32)
            yT = sb4("yT", [128, 8, TB], BF16)
            aT = sb4("aT", [128, 8, TB], BF16)
            yg = sb4("yg", [128, 8, TB], F32)
            mixT = sb4("mixT", [128, DT, TB], BF16)
            ost = sb4("ost", [128, D], F32)
            sq = hid[:, 0:DT, :]
            W.slot_free = [None] * 4
            Wd.slot_free = [None] * 2
            ffn.sg_free = [None, None]
            ffn.hid_free = None
            psum_free = {}
            y_t = y_d.rearrange("(n p) d -> n p d", p=128)
            in_free = []
            ost_free = None
            mix_readers = None
            ntp = 0
            for i in dbg.get("tbs_p4", list(range(4))):
                sl = slice(i * TB, (i + 1) * TB)
                t_h = P.dma("sp", hT[:], h1scr.rearrange("(dt p) t -> p dt t", p=128)[:, :, sl], waits=in_free, sig="p4h")
                t_ys = P.dma("sp", yT[:], yscr.rearrange("(c p) t -> p c t", p=128)[:, :, sl], waits=in_free, sig="p4y")
                t_ya = P.dma("sp", aT[:], ascr.rearrange("(c p) t -> p c t", p=128)[:, :, sl], waits=in_free, sig="p4a")
                in_free = []
                t_g = None
                for mt in range(8):
                    slot = W.n % 4
                    W.n += 1
                    t_ld = P.dma("sp", wbuf[:, slot, 0:1024], wbf["wglu"][mt], waits=[W.slot_free[slot], conv_tok["B"]], sig=f"w{slot}")
                    bank = 4 + mt % 2
                    tk = None
                    for kt in range(8):
                        tk = P.op("pe", lambda e, kt=kt, slot=slot, bank=bank: e.matmul(PS[bank][:], lhsT=wbuf[:, slot, kt * 128:(kt + 1) * 128], rhs=yT[:, kt, :], start=(kt == 0), stop=(kt == 7)),
                                  waits=[t_ld, psum_free.get(bank), t_ys, tk_setup] if kt == 0 else [], sig="pe" if kt == 7 else None)
                    W.slot_free[slot] = tk
                    t_s = P.op("act", lambda e, mt=mt, bank=bank: e.activation(out=sg[:, mt % 2, :], in_=PS[bank][:], func=AF.Sigmoid, bias=gains8[:, 16 + mt:17 + mt]), waits=[tk, ffn.sg_free[mt % 2]], sig="act")
                    psum_free[bank] = t_s
                    t_g = P.op("dve", lambda e, mt=mt: e.tensor_tensor(out=yg[:, mt, :], in0=yT[:, mt, :], in1=sg[:, mt % 2, :], op=ALU.mult), waits=[t_s, mix_readers], sig="dve")
                    ffn.sg_free[mt % 2] = t_g
                t_m1, t_stat = rms_to_bf16(yg[:], 8, gains8[:, 0:8], mixT[:, 0:8, :], hid[:, 0:8, :], PS[6], rtmp[:], rstd[:], [t_g, ffn.hid_free], 1024, war_dst=[mix_readers])
                psum_free[6] = t_stat
                t_m2, t_stat = rms_to_bf16(aT[:], 8, gains8[:, 8:16], mixT[:, 8:16, :], hid[:, 0:8, :], PS[6], rtmp[:], rstd[:], [t_ya, t_m1], 1024, war_dst=[mix_readers])
                psum_free[6] = t_stat
                t_res = None
                for mt in range(DT):
                    slot = W.n % 4
                    W.n += 1
                    t_ld = P.dma("sp", wbuf[:, slot, :], wbf["wout"][mt], waits=[W.slot_free[slot], conv_tok["B"]], sig=f"w{slot}")
                    bank = 4 + mt % 2
                    tk = None
                    for kt in range(DT):
                        tk = P.op("pe", lambda e, kt=kt, slot=slot, bank=bank: e.matmul(PS[bank][:], lhsT=wbuf[:, slot, kt * 128:(kt + 1) * 128], rhs=mixT[:, kt, :], start=(kt == 0), stop=(kt == DT - 1)),
                                  waits=[t_ld, psum_free.get(bank), t_m1, t_m2] if kt == 0 else [], sig="pe" if kt == DT - 1 else None)
                    W.slot_free[slot] = tk
                    t_res = P.op("dve", lambda e, mt=mt, bank=bank: e.tensor_tensor(out=hT[:, mt, :], in0=PS[bank][:], in1=hT[:, mt, :], op=ALU.add), waits=[tk, t_h], sig="dve")
                    psum_free[bank] = t_res
                    mix_readers = tk
                in_free.append(mix_readers)
                t_xn, t_stat = rms_to_bf16(hT[:], DT, gains[:, 2 * DT:3 * DT], xnT, sq, PS[6], rtmp[:], rstd[:], [t_res, ffn.hid_free], D, war_dst=[])
                psum_free[6] = t_stat
                t_res2, t_pe_last = ffn(hT, xnT, hid, wbuf, wdbuf, sg, wbf["wg2"], wbf["wu2"], wbf["wd2"], conv_tok["B"], psum_free, list(rms_to_bf16.last_toks))
                t_fin, t_stat = rms_to_bf16(hT[:], DT, gains[:, 3 * DT:4 * DT], hT, sq, PS[6], rtmp[:], rstd[:], [t_res2, ffn.hid_free], D, war_dst=[])
                psum_free[6] = t_stat
                tk = None
                for s_ in range(4):
                    t_ev = None
                    for dq in range(4):
                        bank = 6 + (ntp % 2)
                        ntp += 1
                        for j in range(4):
                            dt = dq * 4 + j
                            tk = P.op("pe", lambda e, dt=dt, j=j, s_=s_, bank=bank: e.transpose(out=PS[bank][:, j * 128:(j + 1) * 128], in_=hT[:, dt, s_ * 128:(s_ + 1) * 128], identity=ident[:]),
                                      waits=[t_fin, psum_free.get(bank)] if j == 0 else [], sig="pe" if j == 3 else None)
                        t_ev = P.op("act", lambda e, dq=dq, bank=bank: e.activation(out=ost[:, dq * 512:(dq + 1) * 512], in_=PS[bank][:], func=AF.Copy), waits=[tk, ost_free if dq == 0 else None], sig="act")
                        psum_free[bank] = t_ev
                    ost_free = P.dma("pool", y_t[i * 4 + s_], ost[:], waits=[t_ev], sig="yout")
                in_free.append(tk)
            p4.close()

        _p4()

    fin = [t for t in phase_end if t is not None]
    for s, (h, cnt) in list(P.sems.items()):
        if cnt > 0:
            fin.append((s, cnt))
    P.op("pool", lambda e: e.memset(epsc[:, 0:1], EPS), waits=fin)
    P.emit()
    st.close()
    return nc


def _tile_w(w, kt, mt):
    return np.ascontiguousarray(w.reshape(kt, 128, mt, 128).transpose(2, 1, 0, 3).reshape(mt, 128, kt * 128))


def _prep_shared(inp):
    f = lambda a: np.asarray(a, dtype=np.float32)
    sh = {}
    sh["wg1"] = _tile_w(f(inp["ffn1_w_gate"])[0], DT, FT)
    sh["wu1"] = _tile_w(f(inp["ffn1_w_up"])[0], DT, FT)
    sh["wd1"] = _tile_w(f(inp["ffn1_w_down"])[0], FT, DT)
    sh["wg2"] = _tile_w(f(inp["ffn2_w_gate"])[0], DT, FT)
    sh["wu2"] = _tile_w(f(inp["ffn2_w_up"])[0], DT, FT)
    sh["wd2"] = _tile_w(f(inp["ffn2_w_down"])[0], FT, DT)
    sh["win"] = _tile_w(f(inp["w_in"])[0], DT, 32)
    sh["wout"] = _tile_w(f(inp["w_out"])[0], DT, DT)
    sh["wglu"] = _tile_w(f(inp["ssm_w_glu"])[0], 8, 8)
    g16 = lambda v: f(v).reshape(DT, 128).T
    g8 = lambda v: f(v).reshape(8, 128).T
    sh["gains"] = np.ascontiguousarray(np.concatenate([g16(inp["ffn1_norm"][0]), g16(inp["mix_norm"][0]), g16(inp["ffn2_norm"][0]), g16(inp["final_norm"])], axis=1))
    sh["gains8"] = np.ascontiguousarray(np.concatenate([g8(inp["ssm_out_norm"][0]), g8(inp["attn_out_norm"][0]), g8(inp["ssm_b_glu"][0])], axis=1))
    def st_l(a):
        a = f(a)
        r = a.reshape(32, 2, 64, *a.shape[2:])
        r = np.moveaxis(r, 0, 2)
        return np.ascontiguousarray(r.reshape(128, 32, *a.shape[2:]))
    ldt = np.broadcast_to(f(inp["ssm_log_dt"])[0][:, None], (64, 64))
    sh["ssm_sc"] = np.ascontiguousarray(np.concatenate([st_l(ldt), st_l(inp["ssm_a_re"][0]), st_l(inp["ssm_a_im"][0])], axis=1))
    sh["ssm_b"] = np.ascontiguousarray(np.concatenate([st_l(inp["ssm_b_re"][0]).reshape(128, -1), st_l(inp["ssm_b_im"][0]).reshape(128, -1)], axis=1))
    cre = np.transpose(f(inp["ssm_c_re"])[0], (0, 2, 1))
    cim = np.transpose(f(inp["ssm_c_im"])[0], (0, 2, 1))
    sh["ssm_c"] = np.ascontiguousarray(np.concatenate([st_l(cre).reshape(128, -1), st_l(cim).reshape(128, -1)], axis=1))
    sh["ssm_dd"] = np.ascontiguousarray(f(inp["ssm_d"])[0].reshape(8, 128).T)
    sh["ident"] = np.eye(128, dtype=np.float32)
    k = np.arange(128)[:, None]
    q = np.arange(128)[None, :]
    sh["maskc"] = np.where(k <= q, 0.0, NEG).astype(np.float32)
    sh["maskp"] = np.where(k >= q, 0.0, NEG).astype(np.float32)
    return sh


def _prep_core(x, c):
    b, q = divmod(c, 4)
    xs = np.zeros((SEQ, D), np.float32)
    n = (q + 1) * OWN
    xs[SEQ - n:] = x[b, :n]
    hb = np.full((128, 1), 0.0 if q >= 1 else NEG, np.float32)
    return {"xs": xs, "hbias": hb}


def kernel(**inputs):
    x = np.asarray(inputs["x"], dtype=np.float32)
    nc = build_nc()
    sh = _prep_shared(inputs)
    in_maps = []
    for c in range(NCORES):
        m = dict(sh)
        m.update(_prep_core(x, c))
        in_maps.append(m)
    res = run_bass_kernel_spmd(nc, in_maps, core_ids=list(range(NCORES)))
    out = np.empty((BATCH, SEQ, D), np.float32)
    for c in range(NCORES):
        b, q = divmod(c, 4)
        out[b, q * OWN:(q + 1) * OWN] = np.asarray(res.results[c]["y"])
    return out
```
